# Optimizing a Trainium2 kernel written in Bass

```python
import math
import jax
import jax.numpy as jnp
from jax import lax
import numpy as np

D_MODEL = 1024
BATCH = 8
SEQ = 2048
DEPTH = 2

CTX_LEN = 256
GRID_W = 64
MIX_WIDTH = D_MODEL
BRANCH_WIDTH = MIX_WIDTH // 4
DA_HEADS = 4
DA_VDIM = BRANCH_WIDTH // DA_HEADS
DA_QK = DA_VDIM // 2
HG_HEADS = 4
HG_DK = BRANCH_WIDTH // HG_HEADS
HG_DV = BRANCH_WIDTH // HG_HEADS
HG_CHUNK = 16
SW_HEADS = 4
SW_KV_HEADS = 2
SW_HD = BRANCH_WIDTH // SW_HEADS
SW_WINDOW = 128
SW_BLOCK = 128
ML_HEADS = 4
ML_HD = BRANCH_WIDTH // ML_HEADS
ML_CHUNK = 64

ATTN_QBLOCK = 128
ROPE_BASE = 10000.0
NORM_EPS = 1e-6
F32 = jnp.float32

PROJ_SIZES = (
    DA_HEADS * DA_VDIM, DA_HEADS * DA_VDIM, DA_HEADS * DA_VDIM,
    HG_HEADS * HG_DK, HG_HEADS * HG_DK, HG_HEADS * HG_DK, HG_HEADS * HG_DV,
    SW_HEADS * SW_HD, SW_KV_HEADS * SW_HD, SW_KV_HEADS * SW_HD,
    ML_HEADS * ML_HD, ML_HEADS * ML_HD, ML_HEADS * ML_HD,
    ML_HEADS, ML_HEADS, ML_HEADS, ML_HEADS,
    ML_HEADS * ML_HD,
    MIX_WIDTH,
)
PROJ_WIDTH = sum(PROJ_SIZES)

kernel_name = "hybrid_parallel_diffattn_hgrn2_swa_mlstm"


def rmsnorm(x, g):
    xf = x.astype(F32)
    y = xf * lax.rsqrt(jnp.mean(xf * xf, axis=-1, keepdims=True) + NORM_EPS)
    return (y * g.astype(F32)).astype(x.dtype)


def to_heads(t, n_heads):
    b, n, w = t.shape
    return t.reshape(b, n, n_heads, w // n_heads).transpose(0, 2, 1, 3)


def from_heads(t):
    b, h, n, d = t.shape
    return t.transpose(0, 2, 1, 3).reshape(b, n, h * d)


def flip_time(t):
    return jnp.flip(t, axis=2)


def split_proj(p):
    return jnp.split(p, np.cumsum(PROJ_SIZES)[:-1].tolist(), axis=-1)


def axial_rope_tables(n_tok, dim):
    rows = n_tok // GRID_W
    row = jnp.repeat(jnp.arange(rows, dtype=F32), GRID_W)
    col = jnp.tile(jnp.arange(GRID_W, dtype=F32), rows)
    half = dim // 2
    inv = ROPE_BASE ** (-jnp.arange(0, half, 2, dtype=F32) / half)
    ar = row[:, None] * inv
    ac = col[:, None] * inv
    return (jnp.cos(ar), jnp.sin(ar), jnp.cos(ac), jnp.sin(ac))


def rope2d(x, tabs):
    cr, sr, cc, sc = [t.astype(x.dtype) for t in tabs]
    x1, x2, x3, x4 = jnp.split(x, 4, axis=-1)
    return jnp.concatenate([x1 * cr - x2 * sr, x2 * cr + x1 * sr,
                            x3 * cc - x4 * sc, x4 * cc + x3 * sc], axis=-1)


def diff_attention(qa, ka, va, qc, kc, vc, lam_p, g, layer_idx, tabs, need_ctx_out):
    b, n_tok, _ = qa.shape
    lam_init = 0.8 - 0.6 * math.exp(-0.3 * layer_idx)
    lp = lam_p.astype(F32)
    lam = jnp.exp(jnp.sum(lp[0] * lp[1])) - jnp.exp(jnp.sum(lp[2] * lp[3])) + lam_init
    scale = DA_QK ** -0.5

    def maps(t):
        return t.reshape(t.shape[0], t.shape[1], DA_HEADS, 2, DA_QK).transpose(3, 0, 2, 1, 4)

    q = rope2d(maps(qa), tabs)
    k = rope2d(maps(ka), tabs)
    q_c, k_c = maps(qc), maps(kc)
    v, v_c = to_heads(va, DA_HEADS), to_heads(vc, DA_HEADS)
    keys = jnp.concatenate([k, k_c], axis=3)
    vals = jnp.concatenate([v, v_c], axis=2)

    def attend(qs, ks, vs):
        s = jnp.einsum('mbhqd,mbhkd->mbhqk', qs, ks).astype(F32) * scale
        p = jax.nn.softmax(s, axis=-1)
        w = (p[0] - lam * p[1]).astype(vs.dtype)
        return jnp.einsum('bhqk,bhkv->bhqv', w, vs)

    nb = n_tok // ATTN_QBLOCK
    qb = jnp.moveaxis(q.reshape(2, b, DA_HEADS, nb, ATTN_QBLOCK, DA_QK), 3, 0)
    o = lax.map(lambda qq: attend(qq, keys, vals), qb)
    o = jnp.moveaxis(o, 0, 2).reshape(b, DA_HEADS, n_tok, DA_VDIM)
    y = from_heads(rmsnorm(o, g) * (1.0 - lam_init))
    yc = None
    if need_ctx_out:
        yc = from_heads(rmsnorm(attend(q_c, k_c, v_c), g) * (1.0 - lam_init))
    return y, yc


def hgrn_gates(z, lb):
    z = z.astype(F32)
    log_f = jnp.logaddexp(jnp.log(lb), jnp.log1p(-lb) + jax.nn.log_sigmoid(z))
    k = (1.0 - lb) * jax.nn.sigmoid(-z)
    return log_f, k


def gla_chunked(q, k, v, log_f, s0, chunk, with_output):
    b, h, n_tok, dk = q.shape
    dv = v.shape[-1]
    nc = n_tok // chunk
    q, k, log_f = [t.reshape(b, h, nc, chunk, dk) for t in (q, k, log_f)]
    v = v.reshape(b, h, nc, chunk, dv)
    cum = jnp.cumsum(log_f, axis=3)
    cum_last = cum[:, :, :, -1]
    k_end = k * jnp.exp(cum_last[:, :, :, None] - cum)
    ds = jnp.einsum('bhnlk,bhnlv->bhnkv', k_end, v)

    def step(s, inp):
        ds_j, g_j = inp
        return jnp.exp(g_j)[..., None] * s + ds_j, s

    s_fin, s_start = lax.scan(step, s0, (jnp.moveaxis(ds, 2, 0), jnp.moveaxis(cum_last, 2, 0)))
    if not with_output:
        return None, s_fin
    s_start = jnp.moveaxis(s_start, 0, 2)
    tri = jnp.tril(jnp.ones((chunk, chunk), dtype=bool))
    diff = cum[:, :, :, :, None, :] - cum[:, :, :, None, :, :]
    decay = jnp.exp(jnp.where(tri[:, :, None], diff, -jnp.inf))
    a = jnp.einsum('bhnlk,bhnsk,bhnlsk->bhnls', q, k, decay)
    o = (jnp.einsum('bhnls,bhnsv->bhnlv', a, v)
         + jnp.einsum('bhnlk,bhnkv->bhnlv', q * jnp.exp(cum), s_start))
    return o.reshape(b, h, n_tok, dv), s_fin


def hgrn2_mixer(p_lat, p_ctx, lb, g, need_ctx_out):
    lb_h = lb.reshape(1, HG_HEADS, 1, HG_DK)

    def prep(parts):
        q, ff, fb, i = parts
        q = to_heads(q, HG_HEADS).astype(F32) * HG_DK ** -0.5
        i = to_heads(i, HG_HEADS).astype(F32)
        return q, i, hgrn_gates(to_heads(ff, HG_HEADS), lb_h), hgrn_gates(to_heads(fb, HG_HEADS), lb_h)

    q, i, (lf_f, k_f), (lf_b, k_b) = prep(p_lat)
    qc, ic, (lfc_f, kc_f), (lfc_b, kc_b) = prep(p_ctx)
    s0 = jnp.zeros((q.shape[0], HG_HEADS, HG_DK, HG_DV), F32)
    oc_f, s_f = gla_chunked(qc, kc_f, ic, lfc_f, s0, HG_CHUNK, need_ctx_out)
    oc_b, s_b = gla_chunked(flip_time(qc), flip_time(kc_b), flip_time(ic), flip_time(lfc_b),
                            s0, HG_CHUNK, need_ctx_out)
    o_f, _ = gla_chunked(q, k_f, i, lf_f, s_f, HG_CHUNK, True)
    o_b, _ = gla_chunked(flip_time(q), flip_time(k_b), flip_time(i), flip_time(lf_b),
                         s_b, HG_CHUNK, True)
    dt = p_lat[0].dtype
    y = from_heads(rmsnorm(o_f + flip_time(o_b), g)).astype(dt)
    yc = None
    if need_ctx_out:
        yc = from_heads(rmsnorm(oc_f + flip_time(oc_b), g)).astype(dt)
    return y, yc


def window_gqa(qa, ka, va, qc, kc, vc, sink, tabs, need_ctx_out):
    b, n_tok, _ = qa.shape
    n_ctx = qc.shape[1]
    grp = SW_HEADS // SW_KV_HEADS
    scale = SW_HD ** -0.5

    def qheads(t):
        return t.reshape(b, t.shape[1], SW_KV_HEADS, grp, SW_HD).transpose(0, 2, 3, 1, 4)

    q = rope2d(qheads(qa), tabs)
    k = rope2d(to_heads(ka, SW_KV_HEADS), tabs)
    v = to_heads(va, SW_KV_HEADS)
    q_c, k_c, v_c = qheads(qc), to_heads(kc, SW_KV_HEADS), to_heads(vc, SW_KV_HEADS)
    blk = SW_BLOCK
    nb = n_tok // blk

    def band(t):
        tp = jnp.pad(t, ((0, 0), (0, 0), (blk, blk), (0, 0))).reshape(b, SW_KV_HEADS, nb + 2, blk, SW_HD)
        return jnp.concatenate([tp[:, :, :-2], tp[:, :, 1:-1], tp[:, :, 2:]], axis=3)

    kb, vb = band(k), band(v)
    qb = q.reshape(b, SW_KV_HEADS, grp, nb, blk, SW_HD)
    s_band = jnp.einsum('bkgnqd,bknsd->bkgnqs', qb, kb).astype(F32) * scale
    s_ctx = jnp.einsum('bkgnqd,bkcd->bkgnqc', qb, k_c).astype(F32) * scale
    qpos = jnp.arange(nb)[:, None, None] * blk + jnp.arange(blk)[None, :, None]
    kpos = jnp.arange(nb)[:, None, None] * blk + jnp.arange(3 * blk)[None, None, :] - blk
    valid = (jnp.abs(qpos - kpos) <= SW_WINDOW) & (kpos >= 0) & (kpos < n_tok)
    s_band = jnp.where(valid, s_band, -jnp.inf)
    sink_f = sink.astype(F32)
    sink_l = jnp.broadcast_to(sink_f.reshape(1, SW_KV_HEADS, grp, 1, 1, 1), s_band.shape[:-1] + (1,))
    p = jax.nn.softmax(jnp.concatenate([sink_l, s_ctx, s_band], axis=-1), axis=-1)
    p_ctx = p[..., 1:1 + n_ctx].astype(v.dtype)
    p_band = p[..., 1 + n_ctx:].astype(v.dtype)
    o = (jnp.einsum('bkgnqc,bkcd->bkgnqd', p_ctx, v_c)
         + jnp.einsum('bkgnqs,bknsd->bkgnqd', p_band, vb))
    y = o.reshape(b, SW_KV_HEADS, grp, n_tok, SW_HD).transpose(0, 3, 1, 2, 4).reshape(b, n_tok, SW_HEADS * SW_HD)
    yc = None
    if need_ctx_out:
        s_c = jnp.einsum('bkgqd,bkcd->bkgqc', q_c, k_c).astype(F32) * scale
        sink_c = jnp.broadcast_to(sink_f.reshape(1, SW_KV_HEADS, grp, 1, 1), s_c.shape[:-1] + (1,))
        pc = jax.nn.softmax(jnp.concatenate([sink_c, s_c], axis=-1), axis=-1)[..., 1:].astype(v.dtype)
        oc = jnp.einsum('bkgqc,bkcd->bkgqd', pc, v_c)
        yc = oc.transpose(0, 3, 1, 2, 4).reshape(b, n_ctx, SW_HEADS * SW_HD)
    return y, yc


def mlstm_chunked(q, k, v, ig, lf, state, chunk, with_output):
    b, h, n_tok, d = q.shape
    nc = n_tok // chunk
    q, k, v = [t.reshape(b, h, nc, chunk, d) for t in (q, k, v)]
    ig, lf = [t.reshape(b, h, nc, chunk) for t in (ig, lf)]
    cum = jnp.cumsum(lf, axis=-1)
    cum_last = cum[..., -1]
    a = cum_last[..., None] - cum + ig
    m_loc = jnp.max(a, axis=-1)
    w = jnp.exp(a - m_loc[..., None])
    d_c = jnp.einsum('bhnl,bhnlk,bhnlv->bhnkv', w, k, v)
    d_n = jnp.einsum('bhnl,bhnlk->bhnk', w, k)

    def step(carry, inp):
        c_s, n_s, m_s = carry
        dc_j, dn_j, ml_j, cl_j = inp
        m_new = jnp.maximum(cl_j + m_s, ml_j)
        sp = jnp.exp(cl_j + m_s - m_new)
        sl = jnp.exp(ml_j - m_new)
        c_new = sp[..., None, None] * c_s + sl[..., None, None] * dc_j
        n_new = sp[..., None] * n_s + sl[..., None] * dn_j
        return (c_new, n_new, m_new), (c_s, n_s, m_s)

    final, starts = lax.scan(step, state, (jnp.moveaxis(d_c, 2, 0), jnp.moveaxis(d_n, 2, 0),
                                           jnp.moveaxis(m_loc, 2, 0), jnp.moveaxis(cum_last, 2, 0)))
    if not with_output:
        return None, final
    c0, n0, m0 = [jnp.moveaxis(t, 0, 2) for t in starts]
    tri = jnp.tril(jnp.ones((chunk, chunk), dtype=bool))
    logd = jnp.where(tri, cum[..., :, None] - cum[..., None, :] + ig[..., None, :], -jnp.inf)
    inter = cum + m0[..., None]
    m_t = jnp.maximum(jnp.max(logd, axis=-1), inter)
    dmat = jnp.exp(logd - m_t[..., None])
    g0 = jnp.exp(inter - m_t)
    s = jnp.einsum('bhnld,bhnsd->bhnls', q, k) * dmat
    num = (jnp.einsum('bhnls,bhnsv->bhnlv', s, v)
           + g0[..., None] * jnp.einsum('bhnlk,bhnkv->bhnlv', q, c0))
    den = jnp.sum(s, axis=-1) + g0 * jnp.einsum('bhnlk,bhnk->bhnl', q, n0)
    hid = num / jnp.maximum(jnp.abs(den), jnp.exp(-m_t))[..., None]
    return hid.reshape(b, h, n_tok, d), final


def mlstm_mixer(p_lat, p_ctx, g, need_ctx_out):
    def prep(parts):
        q, k, v, ig_f, ig_b, fg_f, fg_b, og = parts
        q = to_heads(q, ML_HEADS).astype(F32)
        k = to_heads(k, ML_HEADS).astype(F32) * ML_HD ** -0.5
        v = to_heads(v, ML_HEADS).astype(F32)
        tg = lambda t: jnp.swapaxes(t.astype(F32), 1, 2)
        return (q, k, v, tg(ig_f), tg(ig_b),
                jax.nn.log_sigmoid(tg(fg_f)), jax.nn.log_sigmoid(tg(fg_b)), og)

    q, k, v, ig_f, ig_b, lf_f, lf_b, og = prep(p_lat)
    qc, kc, vc, igc_f, igc_b, lfc_f, lfc_b, ogc = prep(p_ctx)
    bsz = q.shape[0]
    st0 = (jnp.zeros((bsz, ML_HEADS, ML_HD, ML_HD), F32),
           jnp.zeros((bsz, ML_HEADS, ML_HD), F32),
           jnp.zeros((bsz, ML_HEADS), F32))
    hc_f, st_f = mlstm_chunked(qc, kc, vc, igc_f, lfc_f, st0, ML_CHUNK, need_ctx_out)
    hc_b, st_b = mlstm_chunked(flip_time(qc), flip_time(kc), flip_time(vc), flip_time(igc_b),
                               flip_time(lfc_b), st0, ML_CHUNK, need_ctx_out)
    h_f, _ = mlstm_chunked(q, k, v, ig_f, lf_f, st_f, ML_CHUNK, True)
    h_b, _ = mlstm_chunked(flip_time(q), flip_time(k), flip_time(v), flip_time(ig_b),
                           flip_time(lf_b), st_b, ML_CHUNK, True)
    y = (from_heads(rmsnorm(h_f + flip_time(h_b), g)) * jax.nn.sigmoid(og.astype(F32))).astype(og.dtype)
    yc = None
    if need_ctx_out:
        yc = (from_heads(rmsnorm(hc_f + flip_time(hc_b), g)) * jax.nn.sigmoid(ogc.astype(F32))).astype(ogc.dtype)
    return y, yc


def hybrid_layer(x, ctx, c, c_ctx, w_mod, b_mod, norm_g, w_in, b_in, diff_lam, diff_g,
                 lb, hg_g, sw_sink, ml_g, w_out, layer_idx, tabs_a, tabs_c, need_ctx_out):
    shift, scale, gate = jnp.split(jax.nn.silu(c) @ w_mod + b_mod, 3, axis=-1)
    shift_c, scale_c, gate_c = jnp.split(jax.nn.silu(c_ctx) @ w_mod + b_mod, 3, axis=-1)
    h = rmsnorm(x, norm_g) * (1.0 + scale[:, None]) + shift[:, None]
    hc = rmsnorm(ctx, norm_g) * (1.0 + scale_c) + shift_c
    p = split_proj(h @ w_in + b_in)
    pc = split_proj(hc @ w_in + b_in)
    ya, ya_c = diff_attention(*p[0:3], *pc[0:3], diff_lam, diff_g, layer_idx, tabs_a, need_ctx_out)
    yb, yb_c = hgrn2_mixer(p[3:7], pc[3:7], lb, hg_g, need_ctx_out)
    yc, yc_c = window_gqa(*p[7:10], *pc[7:10], sw_sink, tabs_c, need_ctx_out)
    yd, yd_c = mlstm_mixer(p[10:18], pc[10:18], ml_g, need_ctx_out)
    mixed = jnp.concatenate([ya, yb, yc, yd], axis=-1) * jax.nn.silu(p[18])
    x = x + gate[:, None] * (mixed @ w_out)
    if need_ctx_out:
        mixed_c = jnp.concatenate([ya_c, yb_c, yc_c, yd_c], axis=-1) * jax.nn.silu(pc[18])
        ctx = ctx + gate_c * (mixed_c @ w_out)
    return x, ctx


def setup_inputs(seed: int = 0) -> dict:
    key = jax.random.key(seed)
    ks = jax.random.split(key, 18)
    nrm = jax.random.normal
    d = D_MODEL
    return {
        "x": nrm(ks[0], (BATCH, SEQ, d), F32),
        "c": nrm(ks[1], (BATCH, d), F32),
        "ctx": nrm(ks[2], (BATCH, CTX_LEN, d), F32),
        "c_ctx": nrm(ks[3], (d,), F32),
        "w_mod": nrm(ks[4], (DEPTH, d, 3 * d), F32) * (0.5 * d ** -0.5),
        "b_mod": nrm(ks[5], (DEPTH, 3 * d), F32) * 0.02,
        "norm_g": 1.0 + 0.1 * nrm(ks[6], (DEPTH, d), F32),
        "w_in": nrm(ks[7], (DEPTH, d, PROJ_WIDTH), F32) * d ** -0.5,
        "b_in": nrm(ks[8], (DEPTH, PROJ_WIDTH), F32) * 0.02,
        "diff_lam": nrm(ks[9], (DEPTH, 4, DA_QK), F32) * 0.1,
        "diff_g": 1.0 + 0.1 * nrm(ks[10], (DEPTH, DA_VDIM), F32),
        "hg_lb": nrm(ks[11], (DEPTH, HG_HEADS * HG_DK), F32),
        "hg_g": 1.0 + 0.1 * nrm(ks[12], (DEPTH, HG_DV), F32),
        "sw_sink": nrm(ks[13], (DEPTH, SW_HEADS), F32),
        "ml_g": 1.0 + 0.1 * nrm(ks[14], (DEPTH, ML_HD), F32),
        "w_out": nrm(ks[15], (DEPTH, MIX_WIDTH, d), F32) * MIX_WIDTH ** -0.5,
        "final_g": 1.0 + 0.1 * nrm(ks[16], (d,), F32),
    }


def reference(x, c, ctx, c_ctx, w_mod, b_mod, norm_g, w_in, b_in, diff_lam, diff_g,
              hg_lb, hg_g, sw_sink, ml_g, w_out, final_g):
    n_tok = x.shape[1]
    tabs_a = axial_rope_tables(n_tok, DA_QK)
    tabs_c = axial_rope_tables(n_tok, SW_HD)
    lb_all = jnp.cumsum(jax.nn.softmax(hg_lb.astype(F32), axis=0), axis=0)
    lb_all = lb_all - lb_all[0]
    for l in range(DEPTH):
        x, ctx = hybrid_layer(x, ctx, c, c_ctx, w_mod[l], b_mod[l], norm_g[l], w_in[l], b_in[l],
                              diff_lam[l], diff_g[l], lb_all[l], hg_g[l], sw_sink[l], ml_g[l],
                              w_out[l], l, tabs_a, tabs_c, l < DEPTH - 1)
    return rmsnorm(x, final_g)
```

```python
import bisect
import contextlib
import math
import numpy as np
import ml_dtypes
import concourse.bass as bass
import concourse.mybir as mybir
from concourse.bass_utils import run_bass_kernel_spmd

F32 = mybir.dt.float32
BF16 = mybir.dt.bfloat16
AF = mybir.ActivationFunctionType
OP = mybir.AluOpType

D = 1024
NL = 2
T = 2304
NT = 18
EPS = 1e-6
DBG = False
import os
KSTOP = int(os.environ.get('KSTOP', '99'))
KSUB = float(os.environ.get('KSUB', '99'))
KTILES = int(os.environ.get('KTILES', '99'))
KF = os.environ.get('KF', '234')
KDIRS = int(os.environ.get('KDIRS', '2'))


class StopB(Exception):
    pass


def chk(n):
    if KSUB <= n:
        raise StopB()


class Prog:
    ENGS = ("pe", "act", "dve", "pool", "sp")

    def __init__(self):
        self.ops = []
        self.last_w = {}
        self.readers = {}
        self.dma_keys = {}
        self.last_on = {}
        self.sp_barrier = None

    def op(self, eng, fn, r=(), w=(), dma_key=None, extra=()):
        i = len(self.ops)
        deps = set()
        for k in list(r) + list(w):
            lw = self.last_w.get(k)
            if lw is not None:
                deps.add(lw)
        for k in w:
            for rd in self.readers.get(k, ()):
                deps.add(rd)
        keep = set(extra)
        rs = set(r)
        for d in deps:
            od = self.ops[d]
            if od["dma_key"] is None and dma_key is None and od["eng"] == eng:
                if eng == "pe":
                    continue
                if not (set(od["w"]) & rs):
                    continue
            keep.add(d)
        keep.discard(i)
        self.ops.append(dict(eng=eng, fn=fn, deps=keep, dma_key=dma_key, w=tuple(w)))
        for k in w:
            self.last_w[k] = i
            self.readers[k] = []
        for k in r:
            self.readers.setdefault(k, []).append(i)
        if dma_key is not None:
            self.dma_keys.setdefault(dma_key, []).append(i)
        self.last_on[eng] = i
        return i

    def barrier(self):
        lasts = [v for v in self.last_on.values()]
        for e in ("pe", "act", "dve", "pool"):
            self.op(e, lambda eng: eng.nop(), extra=lasts)
        if self.sp_barrier is not None:
            self.sp_barrier(lasts)

    def emit(self, nc):
        ops = self.ops
        n = len(ops)
        signaled = [False] * n
        for o in ops:
            for d in o["deps"]:
                if ops[d]["dma_key"] is None:
                    signaled[d] = True
        cnt = {}
        sigval = [0] * n
        for i, o in enumerate(ops):
            if o["dma_key"] is None and signaled[i]:
                cnt[o["eng"]] = cnt.get(o["eng"], 0) + 1
                sigval[i] = cnt[o["eng"]]
        self.max_counts = cnt
        ctxs = []
        sems = {}
        for e in self.ENGS:
            c = nc.semaphore("s_" + e)
            ctxs.append(c)
            sems[e] = c.__enter__()
        dsems = {}
        for k in self.dma_keys:
            c = nc.semaphore("d_" + str(k))
            ctxs.append(c)
            dsems[k] = c.__enter__()
        per_eng = {e: [] for e in self.ENGS}
        for i, o in enumerate(ops):
            per_eng[o["eng"]].append(i)
        blk = nc.Block()
        block = blk.__enter__()

        def make(e):
            def body(eng):
                waited = {}
                for i in per_eng[e]:
                    o = ops[i]
                    need = {}
                    for d in o["deps"]:
                        od = ops[d]
                        if od["dma_key"] is None:
                            key = ("e", od["eng"])
                            val = sigval[d]
                        else:
                            k = od["dma_key"]
                            key = ("d", k)
                            val = 16 * bisect.bisect_left(self.dma_keys[k], i)
                        if val > need.get(key, 0):
                            need[key] = val
                    for key, val in need.items():
                        if waited.get(key, 0) >= val:
                            continue
                        waited[key] = val
                        s = sems[key[1]] if key[0] == "e" else dsems[key[1]]
                        eng.wait_ge(s, val)
                    ins = o["fn"](eng)
                    if o["dma_key"] is not None:
                        ins.then_inc(dsems[o["dma_key"]], 16)
                    elif signaled[i]:
                        ins.then_inc(sems[e], 1)
                if e == "sp":
                    for k, lst in self.dma_keys.items():
                        eng.wait_ge(dsems[k], 16 * len(lst))
            return body

        block.tensor(make("pe"))
        block.scalar(make("act"))
        block.vector(make("dve"))
        block.gpsimd(make("pool"))
        block.sync(make("sp"))
        blk.__exit__(None, None, None)
        for c in reversed(ctxs):
            c.__exit__(None, None, None)


def _swap_idx(n, grp):
    j = np.arange(n)
    return ((j // grp) ^ 1) * grp + j % grp


def build_cols():
    perm = []
    segs = {}

    def add(name, cols):
        segs[name] = (len(perm), len(cols))
        perm.extend(list(cols))

    sw32 = _swap_idx(32, 8)
    sw64 = _swap_idx(64, 16)
    for pr in range(2):
        q = 0 + pr * 128 + np.arange(128)
        k = 256 + pr * 128 + np.arange(128)
        qs = 0 + pr * 128 + (np.arange(128) // 32) * 32 + sw32[np.arange(128) % 32]
        ks = 256 + pr * 128 + (np.arange(128) // 32) * 32 + sw32[np.arange(128) % 32]
        add("Aq%d" % pr, q); add("Aqs%d" % pr, qs); add("Ak%d" % pr, k); add("Aks%d" % pr, ks)
        add("Av%d" % pr, 512 + pr * 128 + np.arange(128))
    add("Bff", 1024 + np.arange(256)); add("Bq", 768 + np.arange(256))
    add("Bfb", 1280 + np.arange(256)); add("Bi", 1536 + np.arange(256))
    for pr in range(2):
        q = 1792 + pr * 128 + np.arange(128)
        qs = 1792 + pr * 128 + (np.arange(128) // 64) * 64 + sw64[np.arange(128) % 64]
        add("Cq%d" % pr, q); add("Cqs%d" % pr, qs)
    for kv in range(2):
        k = 2048 + kv * 64 + np.arange(128) % 64
        ks = 2048 + kv * 64 + sw64[np.arange(128) % 64]
        add("Ck%d" % kv, k); add("Cks%d" % kv, ks)
    add("Cv", 2176 + np.arange(128))
    for pr in range(2):
        add("Dq%d" % pr, 2304 + pr * 128 + np.arange(128))
    for pr in range(2):
        add("Dk%d" % pr, 2560 + pr * 128 + np.arange(128))
    add("Dkt", 2560 + np.arange(256)); add("Dvt", 2816 + np.arange(256)); add("Dg", 3072 + np.arange(16)); add("pad", 3072 + np.zeros(112, np.int64))
    for pr in range(2):
        add("Dog%d" % pr, 3088 + pr * 128 + np.arange(128))
    for c in range(8):
        add("G%d" % c, 3344 + c * 128 + np.arange(128))
    return np.array(perm), segs


PERM, SEGS = build_cols()
NW = len(PERM)
TMSEGS = ["Av0", "Av1", "Bff", "Bq", "Bfb", "Bi", "Cv", "Dkt", "Dvt", "Dg"]
TMOFF = {}
_o = 0
for _n in TMSEGS:
    TMOFF[_n] = _o
    _o += SEGS[_n][1]
NTM = 2048


def rope_tables():
    n = 2048
    row = np.repeat(np.arange(32, dtype=np.float32), 64)
    col = np.tile(np.arange(64, dtype=np.float32), 32)
    out = {}
    for nm, dim in (("A", 32), ("C", 64)):
        q = dim // 4
        half = dim // 2
        inv = (10000.0 ** (-np.arange(0, half, 2, dtype=np.float32) / np.float32(half))).astype(np.float32)
        tc = np.zeros((128, n), np.float32)
        ts = np.zeros((128, n), np.float32)
        for r in range(128):
            j = r % dim
            g = j // q
            i = j % q
            pos = row if g < 2 else col
            ang = (pos * inv[i]).astype(np.float32)
            tc[r] = np.cos(ang)
            ts[r] = np.sin(ang) * (-1.0 if g % 2 == 0 else 1.0)
        out[nm + "c"] = tc
        out[nm + "s"] = ts
    return out


def const_arrays():
    c = {}
    s = np.arange(128)[:, None]
    t = np.arange(128)[None, :]
    same = (s // 32) == (t // 32)
    c["ident"] = np.eye(128, dtype=np.float32)
    c["ones"] = np.ones((128, 128), np.float32)
    c["bd64"] = (((s // 64) == (t // 64)) / 64.0).astype(np.float32)
    c["tri32f"] = (same & (s <= t)).astype(np.float32)
    c["tri32b"] = (same & (s >= t)).astype(np.float32)
    c["rem32f"] = (same & (s > t)).astype(np.float32)
    c["rem32b"] = (same & (s < t)).astype(np.float32)
    c["tri128f"] = (s <= t).astype(np.float32)
    c["tri128b"] = (s >= t).astype(np.float32)
    c["rem128f"] = (s > t).astype(np.float32)
    c["rem128b"] = (s < t).astype(np.float32)
    c["negf"] = np.where(s <= t, 0.0, -30000.0).astype(np.float32)
    c["negb"] = np.where(s >= t, 0.0, -30000.0).astype(np.float32)
    seg = np.zeros((128, 4), np.float32)
    seg[np.arange(128), np.arange(128) // 32] = 1.0
    c["segind"] = seg
    rm = np.zeros((128, 2), np.float32)
    rm[:, 0] = ((np.arange(128) % 64) < 32)
    rm[:, 1] = ((np.arange(128) % 64) >= 32)
    c["rowmask"] = rm
    k = np.arange(128)[:, None]
    q = np.arange(512)[None, :]
    wm = np.stack([(np.abs(q - r * 128 - k) <= 128) for r in range(-1, 5)], 0).astype(np.float32)
    c["wmask"] = np.ascontiguousarray(wm.transpose(1, 0, 2)).reshape(128, 6 * 512)
    return c


CONST_F32 = ["ident", "ones", "bd64", "tri32f", "tri32b", "rem32f", "rem32b", "tri128f", "tri128b",
             "rem128f", "rem128b", "negf", "negb"]


def build_program(dbg=False):
    nc = bass.Bass("TRN2", target_bir_lowering=False)
    P = Prog()
    st = contextlib.ExitStack()

    def din(name, shape, dt=F32):
        return nc.dram_tensor(name, list(shape), dt, kind="ExternalInput").ap()

    xin = din("xin", [T, D])
    cT_d = din("cT", [128, 16])
    wmod_d = din("w_mod", [NL, D, 3 * D])
    bmodT_d = din("bmodT", [128, NL * 24])
    bgate_d = din("bgate", [NL, 128, D])
    normg_d = din("normgT", [128, NL * 8])
    w2_d = din("w2", [NL, NW // 128, 128, 8 * 128])
    b2T_d = din("b2T", [128, NL * 64])
    b2row_d = din("b2row", [NL, 1, NTM])
    wout_d = din("w_out", [NL, D, D])
    fing_d = din("fing", [128, D])
    lam_d = din("lam", [128, NL * 128])
    gcol_d = din("gcol", [128, NL * 3])
    hglb_d = din("hglb", [128, 512])
    sink_d = din("sink", [128, NL * 4])
    tab_d = {k: din("tab" + k, [128, 2048]) for k in ("Ac", "As", "Cc", "Cs")}
    cst_d = din("cst", [128, len(CONST_F32) * 128])
    small_d = din("small", [128, 8])
    wmask_d = din("wmask", [128, 6 * 512])
    out_d = nc.dram_tensor("out", [2048, D], F32, kind="ExternalOutput").ap()
    if dbg:
        dbg_mix = nc.dram_tensor("dbg_mix", [NL * 4, 128, 2 * T], BF16, kind="ExternalOutput").ap()
        dbg_x = nc.dram_tensor("dbg_x", [NL, T, D], F32, kind="ExternalOutput").ap()

    uniq = [0]

    def sb(name, shape, dt, stack=None):
        uniq[0] += 1
        return (stack or st).enter_context(nc.sbuf_tensor("%s_%d" % (name, uniq[0]), list(shape), dt))

    X = sb("X", [128, NT, D], F32)
    HT = sb("HT", [128, 8, T], BF16)
    MIX = sb("MIX", [128, 2, T], BF16)
    CST = sb("CST", [128, len(CONST_F32), 128], F32)
    SMALL = sb("SMALL", [128, 8], F32)
    IDB = sb("IDB", [128, 128], BF16)
    ONESROW = sb("ONESROW", [1, 512], BF16)
    CTs = sb("CTs", [128, 16], F32)
    SC = sb("SC", [128, 16], F32)
    BMODT = sb("BMODT", [128, NL * 24], F32)
    NORMG = sb("NORMG", [128, NL * 8], F32)
    B2T = sb("B2T", [128, NL * 64], F32)
    GCOL = sb("GCOL", [128, NL * 3], F32)
    GCOLA = sb("GCOLA", [128, NL], F32)
    LB = sb("LB", [128, 256], F32)
    OML = sb("OML", [128, 256], F32)
    SINK = sb("SINK", [128, NL * 4], F32)
    ESINK = sb("ESINK", [128, NL * 4], F32)
    NLAM = sb("NLAM", [128, NL], F32)
    MODP = sb("MODP", [128, 32], F32)
    HS = sb("HS", [128, 16], F32)
    SH = sb("SH", [128, 16], F32)
    GATEB = sb("GATEB", [128, 2, D], F32)
    B2ROW = sb("B2ROW", [1, NTM], BF16)
    STG = [sb("STG%d" % i, [128, 8, 128], F32) for i in range(2)]
    SS = sb("SS", [128, 4], F32)
    CC = sb("CC", [128, 2], F32)
    P.op("pool", lambda e: e.memset(CC[:, 0:1], EPS), w=["CC"])
    P.op("pool", lambda e: e.memset(CC[:, 1:2], 1.0), w=["CC"])
    EPS_AP = CC[:, 0:1]
    ONE_AP = CC[:, 1:2]
    BSCR = sb("BSCR", [1, 8], F32)
    P.sp_barrier = lambda lasts: P.op("sp", lambda e: e.dma_start(out=BSCR[0:1, 0:8], in_=small_d[0:1, 0:8]), w=["BSCR"], dma_key="bar", extra=lasts)
    ist = contextlib.ExitStack()
    LAMIN = sb("LAMIN", [128, NL * 128], F32, ist)
    HGLB = sb("HGLB", [128, 512], F32, ist)
    EH = sb("EH", [128, 512], F32, ist)
    LT = sb("LT", [128, 64], F32, ist)
    LS = sb("LS", [128, 4], F32, ist)

    PB = [st.enter_context(nc.psum_tensor("PB%d" % i, [128, 512], F32)) for i in range(7)]
    PBT = st.enter_context(nc.psum_tensor("PBT", [128, 1024], BF16))
    bank_ctr = [0]

    def nb():
        i = bank_ctr[0] % 5
        bank_ctr[0] += 1
        return PB[i], "PB%d" % i

    def bk(i):
        return PB[i], "PB%d" % i

    acc_ctr = [0]

    def nacc():
        i = 5 + acc_ctr[0] % 2
        acc_ctr[0] += 1
        return PB[i], "PB%d" % i

    cidx = {n: i for i, n in enumerate(CONST_F32)}

    def C(name):
        return CST[:, cidx[name], :]

    dma_ctr = [0]

    def dma(out, in_, r=(), w=(), key=None):
        if key is None:
            key = "m%d" % (dma_ctr[0] % 4)
            dma_ctr[0] += 1
        return P.op("sp", lambda e: e.dma_start(out=out, in_=in_), r=r, w=w, dma_key=key)

    def mm(out, lhsT, rhs, start, stop, r, w, tp=None):
        if tp is None:
            return P.op("pe", lambda e: e.matmul(out, lhsT=lhsT, rhs=rhs, start=start, stop=stop), r=r, w=w)
        return P.op("pe", lambda e: e.matmul(out, lhsT=lhsT, rhs=rhs, start=start, stop=stop, tile_position=tp), r=r, w=w)

    def act(out, in_, func, r, w, bias=None, scale=None, accum=None):
        kw = {}
        if bias is not None:
            kw["bias"] = bias
        if scale is not None:
            kw["scale"] = scale
        if accum is not None:
            kw["accum_out"] = accum
        return P.op("act", lambda e: e.activation(out=out, in_=in_, func=func, **kw), r=r, w=w)

    def tt(eng, out, in0, in1, op, r, w):
        return P.op(eng, lambda e: e.tensor_tensor(out=out, in0=in0, in1=in1, op=op), r=r, w=w)

    def ts(eng, out, in0, s1, s2, op0, op1, r, w):
        if op1 is None and eng == "pool" and op0 == OP.mult:
            s2, op1 = 0.0, OP.add
        if op1 is None:
            return P.op(eng, lambda e: e.tensor_scalar(out=out, in0=in0, scalar1=s1, scalar2=None, op0=op0), r=r, w=w)
        return P.op(eng, lambda e: e.tensor_scalar(out=out, in0=in0, scalar1=s1, scalar2=s2, op0=op0, op1=op1), r=r, w=w)

    def stt(out, in0, scalar, in1, op0, op1, r, w):
        return P.op("dve", lambda e: e.scalar_tensor_tensor(out=out, in0=in0, scalar=scalar, in1=in1, op0=op0, op1=op1), r=r, w=w)

    def cp(eng, out, in_, r, w):
        return P.op(eng, lambda e: e.tensor_copy(out=out, in_=in_), r=r, w=w)

    def memset(eng, ap, val, w):
        return P.op(eng, lambda e: e.memset(ap, val), w=w)

    def tss(out, in_, scalar, op, r, w):
        return P.op("dve", lambda e: e.tensor_single_scalar(out=out, in_=in_, scalar=scalar, op=op), r=r, w=w)

    def recip(out, in_, r, w):
        return P.op("dve", lambda e: e.reciprocal(out=out, in_=in_), r=r, w=w)

    for t in range(NT):
        dma(X[:, t, :], xin[t * 128:(t + 1) * 128, :], w=["X%d" % t], key="x%d" % (t % 4))
    dma(CST[:].rearrange("p a b -> p (a b)"), cst_d, w=["CST"])
    dma(SMALL[:], small_d, w=["SMALL"])
    dma(CTs[:], cT_d, w=["CTs"])
    dma(BMODT[:], bmodT_d, w=["BMODT"])
    dma(NORMG[:], normg_d, w=["NORMG"])
    dma(B2T[:], b2T_d, w=["B2T"])
    dma(LAMIN[:], lam_d, w=["LAMIN"])
    dma(GCOL[:], gcol_d, w=["GCOL"])
    dma(HGLB[:], hglb_d, w=["HGLB"])
    dma(SINK[:], sink_d, w=["SINK"])
    SEGIND = SMALL[:, 0:4]
    ROWM = SMALL[:, 4:6]
    HEADM = SMALL[:, 6:8]

    cp("dve", IDB[:], C("ident"), ["CST"], ["IDB"])
    memset("pool", ONESROW[:], 1.0, ["ONESROW"])
    act(SC[:], CTs[:], AF.Silu, ["CTs"], ["SC"])
    act(ESINK[:], SINK[:], AF.Exp, ["SINK"], ["ESINK"])
    act(EH[:], HGLB[:], AF.Exp, ["HGLB"], ["EH"])
    tt("dve", OML[:], EH[:, 0:256], EH[:, 256:512], OP.add, ["EH"], ["OML"])
    recip(OML[:], OML[:], ["OML"], ["OML"])
    tt("dve", LB[:], EH[:, 256:512], OML[:], OP.mult, ["EH", "OML"], ["LB"])
    ts("dve", OML[:], LB[:], -1.0, 1.0, OP.mult, OP.add, ["LB"], ["OML"])
    for l in range(NL):
        lam_init = 0.8 - 0.6 * math.exp(-0.3 * l)
        for j in range(2):
            tt("dve", LT[:, j * 32:(j + 1) * 32], LAMIN[:, l * 128 + j * 64:l * 128 + j * 64 + 32],
               LAMIN[:, l * 128 + j * 64 + 32:l * 128 + j * 64 + 64], OP.mult, ["LAMIN"], ["LT"])
            P.op("dve", lambda e, j=j: e.reduce_sum(out=LS[:, j:j + 1], in_=LT[:, j * 32:(j + 1) * 32], axis=mybir.AxisListType.X),
                 r=["LT"], w=["LS"])
        act(LS[:, 2:4], LS[:, 0:2], AF.Exp, ["LS"], ["LS"])
        tt("dve", LS[:, 0:1], LS[:, 3:4], LS[:, 2:3], OP.subtract, ["LS"], ["LS"])
        ts("dve", NLAM[:, l:l + 1], LS[:, 0:1], -lam_init, None, OP.add, None, ["LS"], ["NLAM"])
        ts("dve", GCOLA[:, l:l + 1], GCOL[:, l * 3:l * 3 + 1], 1.0 - lam_init, None, OP.mult, None, ["GCOL"], ["GCOLA"])

    P.barrier()
    ist.close()
    stg_ctr = [0]

    def load_w(l, seg, dst, dkey):
        s0, n = SEGS[seg]
        for c0 in range(0, n, 128):
            cn = min(128, n - c0)
            i = stg_ctr[0] % 2
            stg_ctr[0] += 1
            src = w2_d[l, (s0 + c0) // 128, :, :].rearrange("p (kc n) -> p kc n", kc=8)[:, :, 0:cn]
            dma(STG[i][:, :, 0:cn], src, w=["STG%d" % i], key="stg%d" % i)
            if i == 0:
                act(dst[:, :, c0:c0 + cn], STG[i][:, :, 0:cn], AF.Copy, ["STG%d" % i], [dkey])
            else:
                cp("dve", dst[:, :, c0:c0 + cn], STG[i][:, :, 0:cn], ["STG%d" % i], [dkey])

    def proj_fm(ps, pkey, wt, wkey, tok0, ntok):
        for kc in range(8):
            mm(ps[:, 0:ntok], wt[:, kc, :], HT[:, kc, tok0:tok0 + ntok], kc == 0, kc == 7,
               [wkey, "HT"], [pkey])

    def proj_tm(ps_ap, pkey, wt, wkey, c0, ncol, tile, s0):
        for kc in range(8):
            mm(ps_ap, HT[:, kc, tile * 128:(tile + 1) * 128], wt[:, kc, c0:c0 + ncol], kc == 0, False,
               [wkey, "HT"], [pkey])
        mm(ps_ap, ONESROW[0:1, 0:128], B2ROW[0:1, s0 + c0:s0 + c0 + ncol], False, True, ["ONESROW", "B2ROW"], [pkey])

    BLOCKS = [(0, 256, True)] + [(256 + 512 * j, 512, False) for j in range(4)]

    for l in range(NL if KSTOP >= 10 else 1):
        need_ctx = l < NL - 1
        blocks = BLOCKS if need_ctx else BLOCKS[1:]
        tiles_out = list(range(NT)) if need_ctx else list(range(2, NT))
        for hh_ in range(2):
            B2ROWF = STG[hh_][0:1, :, :].rearrange("p a b -> p (a b)")
            dma(B2ROWF[:, 0:1024], b2row_d[l, :, hh_ * 1024:(hh_ + 1) * 1024], w=["STG%d" % hh_], key="stg%d" % hh_)
            cp("dve", B2ROW[:, hh_ * 1024:(hh_ + 1) * 1024], B2ROWF, ["STG%d" % hh_], ["B2ROW"])
        dma(GATEB[:, 0, :], bgate_d[l], w=["GATEB"])
        cp("pool", GATEB[:, 1, :], GATEB[:, 0, :], ["GATEB"], ["GATEB"])

        if KSTOP < 1:
            break
        with contextlib.ExitStack() as ms:
            SCB = sb("SCB", [128, 16, 128], F32, ms)
            WM = [sb("WMs%d" % i, [128, 8, 512], F32, ms) for i in range(3)]
            for kc in range(16):
                ts("dve", SCB[:, kc, :], C("ones"), SC[:, kc:kc + 1], None, OP.mult, None, ["CST", "SC"], ["SCB"])
            sc3 = SC[:].rearrange("p (a k) -> p a k", a=2)
            pm, pmk = nacc()
            for j in range(6):
                wm = WM[j % 3]
                wmk = "WM%d" % (j % 3)
                for kh in range(2):
                    dma(wm[:, kh * 4:(kh + 1) * 4, :],
                        wmod_d[l, kh * 512:(kh + 1) * 512, j * 512:(j + 1) * 512].rearrange("(kc p) n -> p kc n", p=128),
                        w=[wmk], key="wm%d_%d" % (j % 3, kh))
                if j < 4:
                    for c4 in range(4):
                        ch = j * 4 + c4
                        for kc in range(8):
                            mm(pm[:, ch * 2:ch * 2 + 2], wm[:, kc, c4 * 128:(c4 + 1) * 128], sc3[:, :, kc], kc == 0, kc == 7,
                               [wmk, "SC"], [pmk])
                else:
                    for which in range(2):
                        pg, pgk = nb()
                        for kc in range(8):
                            mm(pg[:, :], SCB[:, which * 8 + kc, :], wm[:, kc, :], kc == 0, kc == 7, [wmk, "SCB"], [pgk])
                        tt("dve", GATEB[:, which, (j - 4) * 512:(j - 3) * 512], pg[:, :], GATEB[:, which, (j - 4) * 512:(j - 3) * 512],
                           OP.add, [pgk, "GATEB"], ["GATEB"])
            cp("dve", MODP[:], pm[:, 0:32], [pmk], ["MODP"])
            mp3 = MODP[:].rearrange("p (c a) -> p c a", a=2)
            for a in range(2):
                tt("dve", SH[:, a * 8:(a + 1) * 8], mp3[:, 0:8, a], BMODT[:, l * 24:l * 24 + 8], OP.add, ["MODP", "BMODT"], ["SH"])
                tt("dve", HS[:, a * 8:(a + 1) * 8], mp3[:, 8:16, a], BMODT[:, l * 24 + 8:l * 24 + 16], OP.add, ["MODP", "BMODT"], ["HS"])
                stt(HS[:, a * 8:(a + 1) * 8], HS[:, a * 8:(a + 1) * 8], 1.0, NORMG[:, l * 8:(l + 1) * 8], OP.add, OP.mult,
                    ["HS", "NORMG"], ["HS"])
        P.barrier()

        if KSTOP < 2:
            break
        with contextlib.ExitStack() as ns:
            XN = [sb("XN%d" % i, [128, D], BF16, ns) for i in range(2)]
            JUNK = sb("JUNK", [128, D], BF16, ns)
            SSA = sb("SSA", [128, 2 * NT], F32, ns)
            for t in range(NT):
                act(JUNK[:], X[:, t, :], AF.Square, ["X%d" % t], ["JUNK", "SSA"], accum=SSA[:, t:t + 1])
            act(SSA[:, NT:2 * NT], SSA[:, 0:NT], AF.Ln, ["SSA", "CC"], ["SSA1"], bias=EPS_AP, scale=1.0 / D)
            act(SSA[:, NT:2 * NT], SSA[:, NT:2 * NT], AF.Exp, ["SSA1"], ["SSA1"], scale=-0.5)
            for t in range(NT):
                a = 1 if t < 2 else 0
                xn = XN[t % 2]
                xk = "XN%d" % (t % 2)
                ts("dve", xn[:], X[:, t, :], SSA[:, NT + t:NT + t + 1], None, OP.mult, None, ["X%d" % t, "SSA1"], [xk])
                for kc in range(8):
                    P.op("pe", lambda e, kc=kc, xn=xn: e.transpose(out=PBT[:, kc * 128:(kc + 1) * 128], in_=xn[:, kc * 128:(kc + 1) * 128], identity=IDB[:]),
                         r=[xk, "IDB"], w=["PBT"])
                for kc in range(8):
                    eng = "dve" if kc % 2 == 0 else "pool"
                    if eng == "dve":
                        ts("dve", HT[:, kc, t * 128:(t + 1) * 128], PBT[:, kc * 128:(kc + 1) * 128], HS[:, a * 8 + kc:a * 8 + kc + 1],
                           SH[:, a * 8 + kc:a * 8 + kc + 1], OP.mult, OP.add, ["PBT", "HS", "SH"], ["HT"])
                    else:
                        act(HT[:, kc, t * 128:(t + 1) * 128], PBT[:, kc * 128:(kc + 1) * 128], AF.Identity, ["PBT", "HS", "SH"], ["HT"],
                            bias=SH[:, a * 8 + kc:a * 8 + kc + 1], scale=HS[:, a * 8 + kc:a * 8 + kc + 1])
        P.barrier()

        def epilogue(mx, extra_sig=None):
            if dbg and l == 0 and KSTOP < 10:
                dma(dbg_mix[4 + mx], MIX[:].rearrange("p a t -> p (a t)"), r=["MIX"], key="dbg")
            with contextlib.ExitStack() as es:
                WG = sb("WG", [128, 8, 256], BF16, es)
                SG = [sb("SGt%d" % i, [128, 512], BF16, es) for i in range(2)]
                WOS = sb("WOS", [128, 2, D], F32, es)
                WO = sb("WO", [128, 2, 2, D], BF16, es)
                load_w(l, "G%d" % (2 * mx), WG[:, :, 0:128], "WG")
                load_w(l, "G%d" % (2 * mx + 1), WG[:, :, 128:256], "WG")
                if extra_sig is not None:
                    WS = sb("WSg", [128, 8, 256], BF16, es)
                    load_w(l, "Dog0", WS[:, :, 0:128], "WSg")
                    load_w(l, "Dog1", WS[:, :, 128:256], "WSg")
                dma(WOS[:], wout_d[l, mx * 256:(mx + 1) * 256, :].rearrange("(pc p) n -> p pc n", p=128), w=["WOS"])
                for pc in range(2):
                    for a in range(2 if need_ctx else 1):
                        tt("dve", WO[:, pc, a, :], WOS[:, pc, :], GATEB[:, a, :], OP.mult, ["WOS", "GATEB"], ["WO"])
                for (tok0, ntok, isctx) in blocks:
                    for pc in range(2):
                        ps, pk = nb()
                        proj_fm(ps, pk, WG[:, :, pc * 128:(pc + 1) * 128], "WG", tok0, ntok)
                        sg = SG[pc]
                        gi = SEGS["G%d" % (2 * mx + pc)][0] // 128
                        act(sg[:, 0:ntok], ps[:, 0:ntok], AF.Silu, [pk, "B2T"], ["SG%d" % pc], bias=B2T[:, l * 64 + gi:l * 64 + gi + 1])
                        tt("dve", MIX[:, pc, tok0:tok0 + ntok], MIX[:, pc, tok0:tok0 + ntok], sg[:, 0:ntok], OP.mult, ["MIX", "SG%d" % pc], ["MIX"])
                        if extra_sig is not None:
                            ps2, pk2 = nb()
                            proj_fm(ps2, pk2, WS[:, :, pc * 128:(pc + 1) * 128], "WSg", tok0, ntok)
                            oi = SEGS["Dog%d" % pc][0] // 128
                            act(sg[:, 0:ntok], ps2[:, 0:ntok], AF.Sigmoid, [pk2, "B2T"], ["SG%d" % pc], bias=B2T[:, l * 64 + oi:l * 64 + oi + 1])
                            tt("dve", MIX[:, pc, tok0:tok0 + ntok], MIX[:, pc, tok0:tok0 + ntok], sg[:, 0:ntok], OP.mult, ["MIX", "SG%d" % pc], ["MIX"])
                    for tl in range(tok0 // 128, (tok0 + ntok) // 128):
                        a = 1 if isctx else 0
                        for half in range(2):
                            po, pok = nb()
                            for pc in range(2):
                                mm(po[:, :], MIX[:, pc, tl * 128:(tl + 1) * 128], WO[:, pc, a, half * 512:(half + 1) * 512], pc == 0, pc == 1,
                                   ["MIX", "WO"], [pok])
                            tt("dve", X[:, tl, half * 512:(half + 1) * 512], po[:, :], X[:, tl, half * 512:(half + 1) * 512], OP.add,
                               [pok, "X%d" % tl], ["X%d" % tl])
                if dbg:
                    dma(dbg_mix[l * 4 + mx], MIX[:].rearrange("p a t -> p (a t)"), r=["MIX"], key="dbg")
            P.barrier()

        def bias_col(seg):
            gi = SEGS[seg][0] // 128
            return B2T[:, l * 64 + gi:l * 64 + gi + 1]

        def group_norm_store(oa_ap, oakey, gcol_ap, ntok, dst_ap, scr, tagr, bank=None, split=None):
            SQ, RS = scr
            act(SQ[:, 0:ntok], oa_ap, AF.Square, [oakey], ["SQ" + tagr])
            pn, pnk = nb() if bank is None else bk(bank)
            mm(pn[:, 0:ntok], C("bd64"), SQ[:, 0:ntok], True, True, ["CST", "SQ" + tagr], [pnk])
            act(RS[:, 0:ntok], pn[:, 0:ntok], AF.Ln, [pnk], ["RS" + tagr], bias=EPS_AP, scale=1.0)
            act(RS[:, 0:ntok], RS[:, 0:ntok], AF.Exp, ["RS" + tagr], ["RS" + tagr], scale=-0.5)
            if split is None:
                stt(dst_ap, oa_ap, gcol_ap, RS[:, 0:ntok], OP.mult, OP.mult, [oakey, "RS" + tagr, "GCOL", "GCOLA"], ["MIX"])
            else:
                stt(dst_ap, oa_ap.rearrange("p (a b) -> p a b", a=split), gcol_ap, RS[:, 0:ntok].rearrange("p (a b) -> p a b", a=split),
                    OP.mult, OP.mult, [oakey, "RS" + tagr, "GCOL", "GCOLA"], ["MIX"])

        if KSTOP < 3:
            break
        for pr in range(2):
            with contextlib.ExitStack() as ws:
                WA = sb("WA", [128, 8, 512], BF16, ws)
                WV = sb("WV", [128, 8, 128], BF16, ws)
                QZ = [sb("QZ%d" % m, [128, T], BF16, ws) for m in range(2)]
                KT = sb("KTa", [128, 2, T], BF16, ws)
                VA = sb("VAa", [128, NT, 2, 192], BF16, ws)
                PT = [sb("PTa%d" % i, [128, 512], BF16, ws) for i in range(4)]
                OM = [sb("OM%d" % i, [128, 512], F32, ws) for i in range(2)]
                TB = OM
                OA = sb("OA", [128, 512], F32, ws)
                REC = sb("REC", [128, 512], F32, ws)
                SQ = sb("SQa", [128, 512], F32, ws)
                RS = sb("RSa", [128, 512], F32, ws)
                RT = [SQ, RS, OA]
                for i, sg in enumerate(["Aq%d" % pr, "Aqs%d" % pr, "Ak%d" % pr, "Aks%d" % pr]):
                    load_w(l, sg, WA[:, :, i * 128:(i + 1) * 128], "WA")
                load_w(l, "Av%d" % pr, WV[:, :, :], "WV")
                memset("pool", VA[:].rearrange("p a b c -> p (a b c)"), 1.0, ["VA"])
                for t in range(NT):
                    pv, pvk = nb()
                    proj_tm(pv[:, 0:128], pvk, WV, "WV", 0, 128, t, TMOFF["Av%d" % pr])
                    cp("dve", VA[:, t, :, 64:128], pv[:, 0:128].rearrange("p (h d) -> p h d", h=2), [pvk], ["VA"])
                for (tok0, ntok, isctx) in BLOCKS:
                    if not isctx:
                        dma(TB[0][:], tab_d["Ac"][:, tok0 - 256:tok0 - 256 + 512], w=["TB0"], key="tb0")
                        dma(TB[1][:], tab_d["As"][:, tok0 - 256:tok0 - 256 + 512], w=["TB1"], key="tb1")
                    for qk in range(2):
                        if qk == 0 and isctx and not need_ctx:
                            continue
                        ps, pk = nb()
                        proj_fm(ps, pk, WA[:, :, qk * 256:qk * 256 + 128], "WA", tok0, ntok)
                        bc = bias_col(("Aq%d" if qk == 0 else "Ak%d") % pr)
                        if isctx:
                            act(RT[2][:, 0:ntok], ps[:, 0:ntok], AF.Identity, [pk, "B2T"], ["RT2"], bias=bc)
                        else:
                            ps2, pk2 = nb()
                            proj_fm(ps2, pk2, WA[:, :, qk * 256 + 128:qk * 256 + 256], "WA", tok0, ntok)
                            bcs = bias_col(("Aqs%d" if qk == 0 else "Aks%d") % pr)
                            stt(RT[0][:, :], ps[:, :], bc, TB[0][:, :], OP.add, OP.mult, [pk, "TB0", "B2T"], ["RT0"])
                            stt(RT[1][:, :], ps2[:, :], bcs, TB[1][:, :], OP.add, OP.mult, [pk2, "TB1", "B2T"], ["RT1"])
                            tt("dve", RT[2][:, :], RT[0][:, :], RT[1][:, :], OP.add, ["RT0", "RT1"], ["RT2"])
                        if qk == 0:
                            for m in range(2):
                                if m == 0:
                                    act(QZ[m][:, tok0:tok0 + ntok], RT[2][:, 0:ntok], AF.Identity, ["RT2", "SMALL"], ["QZ%d" % m], scale=ROWM[:, m:m + 1])
                                else:
                                    ts("pool", QZ[m][:, tok0:tok0 + ntok], RT[2][:, 0:ntok], ROWM[:, m:m + 1], None, OP.mult, None,
                                       ["RT2", "SMALL"], ["QZ%d" % m])
                        else:
                            for hh_ in range(2):
                                if hh_ == 0:
                                    act(KT[:, hh_, tok0:tok0 + ntok], RT[2][:, 0:ntok], AF.Identity, ["RT2", "SMALL"], ["KTa"], scale=HEADM[:, hh_:hh_ + 1])
                                else:
                                    ts("pool", KT[:, hh_, tok0:tok0 + ntok], RT[2][:, 0:ntok], HEADM[:, hh_:hh_ + 1], None, OP.mult, None,
                                       ["RT2", "SMALL"], ["KTa"])
                P.barrier()
                items = []
                for (tok0, ntok, isctx) in blocks:
                    kts = [0, 1] if isctx else list(range(NT))
                    for hh in range(2):
                        for m in range(2):
                            pacc, pak = nacc()
                            for ki, kt in enumerate(kts):
                                items.append(dict(tok0=tok0, ntok=ntok, hh=hh, m=m, kt=kt, first=(ki == 0), last=(ki == len(kts) - 1),
                                                  pacc=pacc, pak=pak))

                rotA = [0]

                def S_a(it):
                    hb = it["hh"] * 64
                    ps, pk = bk(rotA[0] % 4)
                    rotA[0] += 1
                    it["ps"], it["pk"] = ps, pk
                    mm(ps[:, 0:it["ntok"]], KT[:, it["hh"], it["kt"] * 128:(it["kt"] + 1) * 128],
                       QZ[it["m"]][:, it["tok0"]:it["tok0"] + it["ntok"]], True, True, ["KTa", "QZ%d" % it["m"]], [pk])

                pti_a = [0]

                def EV_a(it):
                    hh, m, ntok, tok0 = it["hh"], it["m"], it["ntok"], it["tok0"]
                    hb = hh * 64
                    num = slice(hb, hb + 64)
                    den = slice(64 - hb, 128 - hb)
                    pacc, pak = it["pacc"], it["pak"]
                    pt = PT[pti_a[0] % 4]
                    ptk = "PT%d" % (pti_a[0] % 4)
                    pti_a[0] += 1
                    act(pt[:, 0:ntok], it["ps"][:, 0:ntok], AF.Exp, [it["pk"]], [ptk], scale=32 ** -0.5)
                    va = VA[:, it["kt"], hh, 64:192] if hh == 0 else VA[:, it["kt"], hh, 0:128]
                    mm(pacc[:, 0:ntok], va, pt[:, 0:ntok], it["first"], it["last"], ["VA", ptk], [pak])
                    if it["last"]:
                        recip(REC[den, 0:ntok], pacc[den, 0:ntok], [pak], ["REC"])
                        tt("dve", OM[m][num, 0:ntok], pacc[num, 0:ntok], REC[den, 0:ntok], OP.mult, [pak, "REC"], ["OM%d" % m])
                        if m == 1:
                            stt(OA[num, 0:ntok], OM[1][num, 0:ntok], NLAM[num, l:l + 1], OM[0][num, 0:ntok], OP.mult, OP.add,
                                ["OM0", "OM1", "NLAM"], ["OA"])
                            if hh == 1:
                                group_norm_store(OA[:, 0:ntok], "OA", GCOLA[:, l:l + 1], ntok, MIX[:, pr, tok0:tok0 + ntok], (SQ, RS), "a", bank=4)

                LA = 3
                for i_ in range(len(items) + LA):
                    if i_ < len(items):
                        S_a(items[i_])
                    if i_ >= LA:
                        EV_a(items[i_ - LA])
            P.barrier()
        epilogue(0)

        if KSTOP < 4:
            break
        with contextlib.ExitStack() as ws:
            WB = sb("WB", [128, 8, 1024], BF16, ws)
            M4F = sb("M4F", [128, 4, 128], BF16, ws)
            M4B = sb("M4B", [128, 4, 128], BF16, ws)
            for h in range(4):
                cp("dve", M4F[:, h, :], C("tri32f"), ["CST"], ["M4F"])
                cp("pool", M4B[:, h, :], C("tri32b"), ["CST"], ["M4B"])
            U = sb("Ub", [128, 256], F32, ws)
            FF = sb("Fb", [128, 256], F32, ws)
            LOGF = sb("LOGFb", [128, 256], F32, ws)
            KK = sb("KKb", [128, 256], F32, ws)
            VB = sb("VBb", [128, 256], BF16, ws)
            E = sb("Eb", [128, 256], F32, ws)
            EI = sb("EIb", [128, 256], F32, ws)
            ER = sb("ERb", [128, 256], F32, ws)
            G = sb("Gb", [128, 8], F32, ws)
            QE = sb("QEb", [128, 256], BF16, ws)
            KE = sb("KEb", [128, 256], BF16, ws)
            KEND = sb("KENDb", [128, 256], BF16, ws)
            KM = sb("KMb", [128, 4, 256], BF16, ws)
            QKT = sb("QKTb", [128, 4, 128], BF16, ws)
            AM = sb("AMb", [128, 4, 128], BF16, ws)
            S32S = sb("S32Sb", [128, 5, 2, 64], F32, ws)
            SBFS = sb("SBFSb", [128, 4, 2, 64], BF16, ws)
            OF = sb("OFb", [128, 2, T], BF16, ws)
            OT = sb("OTb", [128, 256], F32, ws)
            SQ = sb("SQb", [128, 512], F32, ws)
            RS = sb("RSb", [128, 512], F32, ws)
            for i, sg in enumerate(["Bff", "Bq", "Bfb", "Bi"]):
                load_w(l, sg, WB[:, :, i * 256:(i + 1) * 256], "WB")
            s0q = TMOFF["Bff"]
            try:
              chk(-1)
              for dr in range(2):
                if KSUB < 99 and dr == 1:
                    break
                tri = C("tri32f") if dr == 0 else C("tri32b")
                rem = C("rem32f") if dr == 0 else C("rem32b")
                m4 = M4F if dr == 0 else M4B
                order = list(range(NT)) if dr == 0 else [1, 0] + list(range(NT - 1, 1, -1))
                segs = [0, 1, 2, 3] if dr == 0 else [3, 2, 1, 0]
                order = order[:KTILES]
                if dr >= KDIRS:
                    break
                chk(0.1)
                memset("pool", S32S[:, 0, :, :].rearrange("p a b -> p (a b)"), 0.0, ["S32_0"])
                for t in order:
                    chk(0.2)
                    pq, pqk = bk(0)
                    proj_tm(pq[:, 0:512], pqk, WB, "WB", dr * 256, 512, t, s0q)
                    qsl = slice(256, 512) if dr == 0 else slice(0, 256)
                    zsl = slice(0, 256) if dr == 0 else slice(256, 512)
                    chk(0.3)
                    pv_, pvk_ = bk(2)
                    proj_tm(pv_[:, 0:256], pvk_, WB, "WB", 768, 256, t, s0q)
                    chk(0.4)
                    act(U[:], pq[:, zsl], AF.Exp, [pqk], ["U"], scale=-1.0)
                    chk(0.5)
                    act(U[:], U[:], AF.Ln, ["U", "CC"], ["U"], bias=ONE_AP, scale=1.0)
                    act(FF[:], U[:], AF.Exp, ["U"], ["FF"], scale=-1.0)
                    chk(0.6)
                    if l > 0:
                        tt("dve", FF[:], FF[:], OML[:], OP.mult, ["FF", "OML"], ["FF"])
                        tt("dve", FF[:], FF[:], LB[:], OP.add, ["FF", "LB"], ["FF"])
                        act(LOGF[:], FF[:], AF.Ln, ["FF"], ["LOGF"])
                    else:
                        ts("dve", LOGF[:], U[:], -1.0, None, OP.mult, None, ["U"], ["LOGF"])
                    ts("dve", KK[:], FF[:], -1.0, 1.0, OP.mult, OP.add, ["FF"], ["KK"])
                    cp("dve", VB[:], pv_[:, 0:256], [pvk_], ["VB"])
                    chk(1)
                    pc_, pck = bk(3)
                    mm(pc_[:, 0:256], tri, LOGF[:], True, True, ["CST", "LOGF"], [pck])
                    pr_, prk_ = bk(4)
                    mm(pr_[:, 0:256], rem, LOGF[:], True, True, ["CST", "LOGF"], [prk_])
                    pg, pgk = bk(1)
                    for hf in range(2):
                        mm(pg[:, hf * 4:(hf + 1) * 4], LOGF[:, hf * 128:(hf + 1) * 128], SEGIND, True, True, ["LOGF", "SMALL"], [pgk])
                    act(E[:], pc_[:, 0:256], AF.Exp, [pck], ["E"])
                    act(EI[:], pc_[:, 0:256], AF.Exp, [pck], ["EI"], scale=-1.0)
                    act(ER[:], pr_[:, 0:256], AF.Exp, [prk_], ["ER"])
                    act(G[:], pg[:, 0:8], AF.Exp, [pgk], ["G"])
                    chk(2)
                    stt(QE[:], pq[:, qsl], 0.125, E[:], OP.mult, OP.mult, [pqk, "E"], ["QE"])
                    tt("dve", KE[:], KK[:], EI[:], OP.mult, ["KK", "EI"], ["KE"])
                    tt("dve", KEND[:], KK[:], ER[:], OP.mult, ["KK", "ER"], ["KEND"])
                    for c4 in range(4):
                        ts("dve", KM[:, c4, :], KEND[:], SEGIND[:, c4:c4 + 1], None, OP.mult, None, ["KEND", "SMALL"], ["KM"])
                    for i4, (src, sk) in enumerate([(QE, "QE"), (QE, "QE"), (KE, "KE"), (KE, "KE")]):
                        hf = i4 % 2
                        P.op("pe", lambda e, i4=i4, src=src, hf=hf: e.transpose(out=PBT[:, i4 * 128:(i4 + 1) * 128], in_=src[:, hf * 128:(hf + 1) * 128], identity=IDB[:]),
                             r=[sk, "IDB"], w=["PBT"])
                    pd, pdk = bk(4)
                    for si, c4 in enumerate(segs):
                        for h in range(4):
                            hb = (h % 2) * 64
                            pp = h // 2
                            mm(pd[hb:hb + 64, si * 128 + pp * 64:si * 128 + (pp + 1) * 64], KM[:, c4, h * 64:(h + 1) * 64], VB[:, h * 64:(h + 1) * 64],
                               True, True, ["KM", "VB"], [pdk], tp=(0, hb))
                    for si, c4 in enumerate(segs):
                        for pp in range(2):
                            stt(S32S[:, si + 1, pp, :], S32S[:, si, pp, :], G[:, pp * 4 + c4:pp * 4 + c4 + 1], pd[:, si * 128 + pp * 64:si * 128 + (pp + 1) * 64],
                                OP.mult, OP.add, ["S32_%d" % si, "G", pdk], ["S32_%d" % (si + 1)])
                    act(SBFS[:].rearrange("p a b c -> p (a b c)"), S32S[:, 0:4, :, :].rearrange("p a b c -> p (a b c)"), AF.Copy,
                        ["S32_0", "S32_1", "S32_2", "S32_3"], ["SBFS"])
                    cp("pool", S32S[:, 0, :, :].rearrange("p a b -> p (a b)"), S32S[:, 4, :, :].rearrange("p a b -> p (a b)"), ["S32_4"], ["S32_0"])
                    chk(3)
                    cp("dve", QKT[:].rearrange("p a b -> p (a b)"), PBT[:, 0:512], ["PBT"], ["QKT"])
                    chk(4)
                    pa0, pak0 = bk(2)
                    pa1, pak1 = bk(3)
                    pas = [pa0, pa1]
                    paks = [pak0, pak1]
                    for h in range(4):
                        hb = (h % 2) * 64
                        pp = h // 2
                        mm(pas[h % 2][:, pp * 128:(pp + 1) * 128], QKT[hb:hb + 64, 2 + pp, :], QKT[hb:hb + 64, pp, :], True, True, ["QKT"], [paks[h % 2]])
                    for par in range(2):
                        tt("dve", AM[:, par::2, :] if False else AM[:].rearrange("p (a two) b -> p two a b", two=2)[:, par, :, :],
                           pas[par][:, 0:256].rearrange("p (a b) -> p a b", a=2), m4[:, 0:2, :], OP.mult, [paks[par], "M4F", "M4B"], ["AM"])
                    chk(5)
                    po0, pok0 = bk(5)
                    po1, pok1 = bk(6)
                    pos = [po0, po1]
                    poks = [pok0, pok1]
                    for h in range(4):
                        hb = (h % 2) * 64
                        pp = h // 2
                        po = pos[pp]
                        pok = poks[pp]
                        mm(po[hb:hb + 64, 0:128], VB[:, h * 64:(h + 1) * 64], AM[:, h, :], True, False, ["VB", "AM"],
                           [pok], tp=(0, hb))
                    chk(6)
                    for si, c4 in enumerate(segs):
                        for h in range(4):
                            hb = (h % 2) * 64
                            pp = h // 2
                            mm(pos[pp][hb:hb + 64, c4 * 32:c4 * 32 + 32], SBFS[hb:hb + 64, si, pp, :], QKT[hb:hb + 64, pp, c4 * 32:(c4 + 1) * 32],
                               False, si == 3, ["SBFS", "QKT"], [poks[pp]], tp=(hb, hb))
                    chk(7)
                    if dr == 0:
                        for pp in range(2):
                            cp("dve", OF[:, pp, t * 128:(t + 1) * 128], pos[pp][:, 0:128], [poks[pp]], ["OF"])
                        chk(8)
                    else:
                        for pp in range(2):
                            tt("dve", OT[:, pp * 128:(pp + 1) * 128], pos[pp][:, 0:128], OF[:, pp, t * 128:(t + 1) * 128], OP.add, [poks[pp], "OF"], ["OT"])
                        if need_ctx or t >= 2:
                            if "2" in KF:
                                group_norm_store(OT[:, 0:256], "OT", GCOL[:, l * 3 + 1:l * 3 + 2], 256,
                                                 MIX[:, :, t * 128:(t + 1) * 128], (SQ, RS), "b", bank=3, split=2)
                            else:
                                for pp in range(2):
                                    group_norm_store(OT[:, pp * 128:(pp + 1) * 128], "OT", GCOL[:, l * 3 + 1:l * 3 + 2], 128,
                                                     MIX[:, pp, t * 128:(t + 1) * 128], (SQ, RS), "b", bank=3)
            except StopB:
                pass
        P.barrier()
        epilogue(1)

        if KSTOP < 5:
            break
        with contextlib.ExitStack() as ws:
            QT = sb("QTc", [128, 4, T], BF16, ws)
            KT = sb("KTc", [128, 2, T], BF16, ws)
            VA = sb("VAc", [128, NT, 2, 192], BF16, ws)
            ws2 = contextlib.ExitStack()
            WC = sb("WC", [128, 8, 512], BF16, ws2)
            WV = sb("WVc", [128, 8, 128], BF16, ws2)
            TB = [sb("TBc%d" % i, [128, 512], F32, ws2) for i in range(2)]
            RT = [sb("RTc%d" % i, [128, 512], F32, ws2) for i in range(2)]
            names = ["Cq0", "Cqs0", "Cq1", "Cqs1", "Ck0", "Cks0", "Ck1", "Cks1"]
            load_w(l, "Cv", WV[:, :, :], "WVc")
            memset("pool", VA[:].rearrange("p a b c -> p (a b c)"), 1.0, ["VA"])
            for t in range(NT):
                pv, pvk = nb()
                proj_tm(pv[:, 0:128], pvk, WV, "WVc", 0, 128, t, TMOFF["Cv"])
                cp("dve", VA[:, t, :, 64:128], pv[:, 0:128].rearrange("p (h d) -> p h d", h=2), [pvk], ["VA"])
            for grp, (tok0, ntok, isctx) in [(g_, b_) for g_ in range(2) for b_ in BLOCKS]:
                if (tok0, ntok, isctx) == BLOCKS[0]:
                    for i in range(4):
                        load_w(l, names[grp * 4 + i], WC[:, :, i * 128:(i + 1) * 128], "WC")
                if not isctx:
                    dma(TB[0][:], tab_d["Cc"][:, tok0 - 256:tok0 - 256 + 512], w=["TB0"], key="tb0")
                    dma(TB[1][:], tab_d["Cs"][:, tok0 - 256:tok0 - 256 + 512], w=["TB1"], key="tb1")
                for ci in range(2 * grp, 2 * grp + 2):
                    if ci < 2 and isctx and not need_ctx:
                        continue
                    dst = RT[0][:, 0:ntok] if ci < 2 else KT[:, ci - 2, tok0:tok0 + ntok]
                    dk = "RT0" if ci < 2 else "KTc"
                    ps, pk = nb()
                    proj_fm(ps, pk, WC[:, :, (ci % 2) * 256:(ci % 2) * 256 + 128], "WC", tok0, ntok)
                    bc = bias_col(names[ci * 2])
                    if isctx:
                        act(dst, ps[:, 0:ntok], AF.Identity, [pk, "B2T"], [dk], bias=bc)
                    else:
                        ps2, pk2 = nb()
                        proj_fm(ps2, pk2, WC[:, :, (ci % 2) * 256 + 128:(ci % 2) * 256 + 256], "WC", tok0, ntok)
                        bcs = bias_col(names[ci * 2 + 1])
                        stt(RT[0][:, :], ps[:, :], bc, TB[0][:, :], OP.add, OP.mult, [pk, "TB0", "B2T"], ["RT0"])
                        stt(RT[1][:, :], ps2[:, :], bcs, TB[1][:, :], OP.add, OP.mult, [pk2, "TB1", "B2T"], ["RT1"])
                        tt("dve", dst, RT[0][:, :], RT[1][:, :], OP.add, ["RT0", "RT1"], [dk])
                    if ci < 2:
                        for half in range(2):
                            if half == 0:
                                act(QT[:, 2 * ci + half, tok0:tok0 + ntok], RT[0][:, 0:ntok], AF.Identity, ["RT0", "SMALL"], ["QTc"], scale=HEADM[:, half:half + 1])
                            else:
                                ts("pool", QT[:, 2 * ci + half, tok0:tok0 + ntok], RT[0][:, 0:ntok], HEADM[:, half:half + 1], None, OP.mult, None,
                                   ["RT0", "SMALL"], ["QTc"])
            P.barrier()
            ws2.close()
            PT = [sb("PTc%d" % i, [128, 512], BF16, ws) for i in range(4)]
            REC = sb("RECc", [128, 512], F32, ws)
            WMASK = sb("WMASK", [128, 6, 512], BF16, ws)
            for r6 in range(6):
                stg = STG[r6 % 2]
                sv = stg[:, 0:4, :].rearrange("p a b -> p (a b)")
                dma(sv, wmask_d[:, r6 * 512:(r6 + 1) * 512], w=["STG%d" % (r6 % 2)], key="stg%d" % (r6 % 2))
                cp("pool", WMASK[:, r6, :], sv, ["STG%d" % (r6 % 2)], ["WMASK"])
            items = []
            for (tok0, ntok, isctx) in blocks:
                if isctx:
                    kts = [(0, None), (1, None)]
                else:
                    J = (tok0 - 256) // 512
                    kts = [(0, None), (1, None)] + [(2 + lt, lt - 4 * J + 1) for lt in range(4 * J - 1, 4 * J + 5) if 0 <= lt < 16]
                for h in range(4):
                    pacc, pak = nacc()
                    for ki, (kt, mr) in enumerate(kts):
                        items.append(dict(tok0=tok0, ntok=ntok, h=h, kt=kt, mr=mr, first=(ki == 0), last=(ki == len(kts) - 1), pacc=pacc, pak=pak))

            def S_c(it):
                h = it["h"]
                hb = (h % 2) * 64
                ps, pk = nb()
                it["ps"], it["pk"] = ps, pk
                mm(ps[:, 0:it["ntok"]], KT[:, h // 2, it["kt"] * 128:(it["kt"] + 1) * 128],
                   QT[:, h, it["tok0"]:it["tok0"] + it["ntok"]], True, True, ["KTc", "QTc"], [pk])

            pti_c = [0]

            def EV_c(it):
                h, ntok, tok0 = it["h"], it["ntok"], it["tok0"]
                hb = (h % 2) * 64
                pp = h // 2
                kv = h // 2
                num = slice(hb, hb + 64)
                den = slice(64 - hb, 128 - hb)
                pacc, pak = it["pacc"], it["pak"]
                pt = PT[pti_c[0] % 4]
                ptk = "PT%d" % (pti_c[0] % 4)
                pti_c[0] += 1
                act(pt[:, 0:ntok], it["ps"][:, 0:ntok], AF.Exp, [it["pk"]], [ptk], scale=0.125)
                if it["mr"] is not None:
                    tt("dve", pt[:, 0:ntok], pt[:, 0:ntok], WMASK[:, it["mr"], 0:ntok], OP.mult, [ptk, "WMASK"], [ptk])
                va = VA[:, it["kt"], kv, 64:192] if hb == 0 else VA[:, it["kt"], kv, 0:128]
                mm(pacc[:, 0:ntok], va, pt[:, 0:ntok], it["first"], it["last"], ["VA", ptk], [pak])
                if it["last"]:
                    ts("dve", REC[den, 0:ntok], pacc[den, 0:ntok], ESINK[den, l * 4 + h:l * 4 + h + 1], None, OP.add, None, [pak, "ESINK"], ["REC"])
                    recip(REC[den, 0:ntok], REC[den, 0:ntok], ["REC"], ["REC"])
                    tt("dve", MIX[num, pp, tok0:tok0 + ntok], pacc[num, 0:ntok], REC[den, 0:ntok], OP.mult, [pak, "REC"], ["MIX"])

            LA = 4
            for i_ in range(len(items) + LA):
                if i_ < len(items):
                    S_c(items[i_])
                if i_ >= LA:
                    EV_c(items[i_ - LA])
        P.barrier()
        epilogue(2)

        if KSTOP < 6:
            break
        with contextlib.ExitStack() as ws:
            QKT = sb("QKTd", [128, 4, T], BF16, ws)
            ws2 = contextlib.ExitStack()
            WD = sb("WD", [128, 8, 512], BF16, ws2)
            for i, sg in enumerate(["Dq0", "Dq1", "Dk0", "Dk1"]):
                load_w(l, sg, WD[:, :, i * 128:(i + 1) * 128], "WD")
            for (tok0, ntok, isctx) in BLOCKS:
                for ci, sg in enumerate(["Dq0", "Dq1", "Dk0", "Dk1"]):
                    ps, pk = nb()
                    proj_fm(ps, pk, WD[:, :, ci * 128:(ci + 1) * 128], "WD", tok0, ntok)
                    if ci % 2 == 0:
                        act(QKT[:, ci, tok0:tok0 + ntok], ps[:, 0:ntok], AF.Identity, [pk, "B2T"], ["QKTd"], bias=bias_col(sg))
                    else:
                        ts("dve", QKT[:, ci, tok0:tok0 + ntok], ps[:, 0:ntok], bias_col(sg), None, OP.add, None, [pk, "B2T"], ["QKTd"])
            P.barrier()
            ws2.close()
            WT = sb("WTd", [128, 8, 528], BF16, ws)
            VA2 = [sb("VAd%d" % i, [128, 4, 192], BF16, ws) for i in range(2)]
            NEGB = sb("NEGBd", [128, 2, 128], BF16, ws)
            cp("dve", NEGB[:, 0, :], C("negf"), ["CST"], ["NEGB"])
            cp("dve", NEGB[:, 1, :], C("negb"), ["CST"], ["NEGB"])
            LBH = sb("LBHd", [128, 4, 128], F32, ws)
            DT = sb("DTd", [128, 4, 128], F32, ws)
            EROW = sb("EROWd", [128, 2, 128], F32, ws)
            QS2 = [sb("QSd%d" % i, [128, 2, 128], BF16, ws) for i in range(2)]
            SM2 = [sb("SMd%d" % i, [128, 4, 128], BF16, ws) for i in range(2)]
            KW2 = [sb("KWd%d" % i, [128, 4, 64], BF16, ws) for i in range(2)]
            CN32 = sb("CN32d", [128, 4, 128], F32, ws)
            CNB = sb("CNBd", [128, 4, 128], BF16, ws)
            DN = sb("DNd", [128, 4, 128], F32, ws)
            HF = sb("HFd", [128, 2, T], BF16, ws)
            HT2 = [sb("HTd%d" % i, [128, 256], F32, ws) for i in range(2)]
            SQ = sb("SQd", [128, 256], F32, ws)
            RS = sb("RSd", [128, 256], F32, ws)
            load_w(l, "Dkt", WT[:, :, 0:256], "WT")
            load_w(l, "Dvt", WT[:, :, 256:512], "WT")
            load_w(l, "Dg", WT[:, :, 512:528], "WT")
            s0k = TMOFF["Dkt"]
            for i_ in range(2):
                memset("pool", VA2[i_][:].rearrange("p a b -> p (a b)"), 1.0, ["VAd%d" % i_])
            LOGFA = sb("LOGFAd", [128, NT, 8], F32, ws)
            IGA = sb("IGAd", [128, NT, 8], F32, ws)
            BIASA = sb("BIASAd", [128, 2, NT * 4], F32, ws)
            WWA = sb("WWAd", [128, 2, NT * 4], F32, ws)
            GLA = sb("GLAd", [128, 2, NT * 4], F32, ws)
            pga, pgak = bk(1)
            for t in range(NT):
                proj_tm(pga[:, t * 16:(t + 1) * 16], pgak, WT, "WT", 512, 16, t, s0k)
            pga3 = pga[:, 0:NT * 16].rearrange("p (t g) -> p t g", g=16)
            act(LOGFA[:], pga3[:, :, 8:16], AF.Exp, [pgak], ["LOGFA"], scale=-1.0)
            act(LOGFA[:].rearrange("p t g -> p (t g)"), LOGFA[:].rearrange("p t g -> p (t g)"), AF.Ln, ["LOGFA", "CC"], ["LOGFA"], bias=ONE_AP, scale=1.0)
            ts("dve", LOGFA[:].rearrange("p t g -> p (t g)"), LOGFA[:].rearrange("p t g -> p (t g)"), -1.0, None, OP.mult, None, ["LOGFA"], ["LOGFA"])
            cp("dve", IGA[:], pga3[:, :, 0:8], [pgak], ["IGA"])
            for dr_ in range(2):
                tri_ = C("tri128f") if dr_ == 0 else C("tri128b")
                rem_ = C("rem128f") if dr_ == 0 else C("rem128b")
                pca, pcak = bk(2 + dr_)
                rhs_ = LOGFA[:, :, dr_ * 4:(dr_ + 1) * 4]
                mm(pca[:, 0:NT * 4].rearrange("p (t h) -> p t h", h=4), tri_, rhs_, True, True, ["CST", "LOGFA"], [pcak])
                mm(pca[:, NT * 4:2 * NT * 4].rearrange("p (t h) -> p t h", h=4), rem_, rhs_, True, True, ["CST", "LOGFA"], [pcak])
                mm(pca[:, 2 * NT * 4:3 * NT * 4].rearrange("p (t h) -> p t h", h=4), C("ones"), rhs_, True, True, ["CST", "LOGFA"], [pcak])
                tt("dve", BIASA[:, dr_, :].rearrange("p (t h) -> p t h", h=4), IGA[:, :, dr_ * 4:(dr_ + 1) * 4],
                   pca[:, 0:NT * 4].rearrange("p (t h) -> p t h", h=4), OP.subtract, ["IGA", pcak], ["BIASA"])
                tt("dve", WWA[:, dr_, :].rearrange("p (t h) -> p t h", h=4), IGA[:, :, dr_ * 4:(dr_ + 1) * 4],
                   pca[:, NT * 4:2 * NT * 4].rearrange("p (t h) -> p t h", h=4), OP.add, ["IGA", pcak], ["WWA"])
                act(WWA[:, dr_, :], WWA[:, dr_, :], AF.Exp, ["WWA"], ["WWA"])
                act(GLA[:, dr_, :], pca[:, 2 * NT * 4:3 * NT * 4], AF.Exp, [pcak], ["GLA"])
            for dr in range(2):
                tri = C("tri128f") if dr == 0 else C("tri128b")
                rem = C("rem128f") if dr == 0 else C("rem128b")
                neg = C("negf") if dr == 0 else C("negb")
                order = list(range(NT)) if dr == 0 else [1, 0] + list(range(NT - 1, 1, -1))
                memset("pool", CN32[:].rearrange("p a b -> p (a b)"), 0.0, ["CN32"])
                memset("pool", CNB[:].rearrange("p a b -> p (a b)"), 0.0, ["CNB"])
                def producer(t, bp):
                    tsl = slice(t * 128, (t + 1) * 128)
                    VAp, KWp, QSp, SMp = VA2[bp], KW2[bp], QS2[bp], SM2[bp]
                    pk_, pkk = bk(0)
                    proj_tm(pk_[:, 0:512], pkk, WT, "WT", 0, 512, t, s0k)
                    cp("dve", VAp[:, :, 64:128], pk_[:, 256:512].rearrange("p (h d) -> p h d", h=4), [pkk], ["VAd%d" % bp])
                    for h in range(4):
                        ts("dve", KWp[:, h, :], pk_[:, h * 64:(h + 1) * 64], WWA[:, dr, t * 4 + h:t * 4 + h + 1], 0.125, OP.mult, OP.mult,
                           [pkk, "WWA"], ["KW%d_%d" % (h, bp)])
                    pf, pfk = bk(3)
                    pe2, pe2k = bk(4)
                    for h in range(4):
                        hb = (h % 2) * 64
                        pp = h // 2
                        ts("dve", LBH[:, h, :], C("ones"), LOGFA[:, t, dr * 4 + h:dr * 4 + h + 1], None, OP.mult, None, ["CST", "LOGFA"], ["LBH%d" % h])
                        mm(pf[:, h * 128:(h + 1) * 128], LBH[:, h, :], tri, True, False, ["LBH%d" % h, "CST"], [pfk])
                        mm(pf[:, h * 128:(h + 1) * 128], IDB[:], NEGB[:, dr, :], False, True, ["IDB", "NEGB"], [pfk])
                        mm(pe2[hb:hb + 64, pp * 128:(pp + 1) * 128], LBH[:, h, 0:64], tri, True, True, ["LBH%d" % h, "CST"], [pe2k], tp=(0, hb))
                    for h in range(4):
                        act(DT[:, h, :], pf[:, h * 128:(h + 1) * 128], AF.Exp, [pfk, "BIASA"], ["DT"], bias=BIASA[:, dr, t * 4 + h:t * 4 + h + 1], scale=1.0)
                    act(EROW[:].rearrange("p a b -> p (a b)"), pe2[:, 0:256], AF.Exp, [pe2k], ["EROW"])
                    tt("dve", QSp[:], QKT[:, 0:2, tsl], EROW[:], OP.mult, ["QKTd", "EROW"], ["QS%d" % bp])
                    pkq0, pkqk0 = bk(3)
                    pkq1, pkqk1 = bk(4)
                    pkqs = [pkq0, pkq1]
                    pkqks = [pkqk0, pkqk1]
                    for h in range(4):
                        hb = (h % 2) * 64
                        pp = h // 2
                        mm(pkqs[h % 2][:, pp * 128:(pp + 1) * 128], QKT[hb:hb + 64, 2 + pp, tsl], QKT[hb:hb + 64, pp, tsl], True, True, ["QKTd"], [pkqks[h % 2]])
                    for par in range(2):
                        stt(SMp[:].rearrange("p (a two) b -> p two a b", two=2)[:, par, :, :], pkqs[par][:, 0:256].rearrange("p (a b) -> p a b", a=2), 0.125,
                            DT[:].rearrange("p (a two) b -> p two a b", two=2)[:, par, :, :], OP.mult, OP.mult, [pkqks[par], "DT"], ["SM%d" % bp])

                def gn_d(t, bp):
                    tsl = slice(t * 128, (t + 1) * 128)
                    if dr == 1 and (need_ctx or t >= 2):
                        group_norm_store(HT2[bp][:, 0:256], "HTd_%d" % bp, GCOL[:, l * 3 + 2:l * 3 + 3], 256,
                                         MIX[:, :, tsl], (SQ, RS), "d", bank=2, split=2)

                def consumer(t, bp):
                    HT_ = HT2[bp]
                    tsl = slice(t * 128, (t + 1) * 128)
                    VAp, KWp, QSp, SMp = VA2[bp], KW2[bp], QS2[bp], SM2[bp]
                    pn, pnk = bk(5 if bp == 0 else 1)
                    for h in range(4):
                        hb = (h % 2) * 64
                        pp = h // 2
                        va = VAp[:, h, 64:192] if hb == 0 else VAp[:, h, 0:128]
                        mm(pn[:, h * 128:(h + 1) * 128], va, SMp[:, h, :], True, False, ["VAd%d" % bp, "SM%d" % bp], [pnk])
                        mm(pn[:, h * 128:(h + 1) * 128], CNB[hb:hb + 64, h, :], QSp[hb:hb + 64, pp, :], False, True, ["CNB", "QS%d" % bp], [pnk])
                    pst, pstk = bk(6)
                    for h in range(4):
                        hb = (h % 2) * 64
                        va = VAp[:, h, 64:192] if hb == 0 else VAp[:, h, 0:128]
                        mm(pst[hb:hb + 64, h * 128:(h + 1) * 128], KWp[:, h, :], va, True, True, ["KW%d_%d" % (h, bp), "VAd%d" % bp], [pstk], tp=(0, hb))
                    for h in range(4):
                        hb = (h % 2) * 64
                        sl = slice(hb, hb + 64)
                        stt(CN32[sl, h, :], CN32[sl, h, :], GLA[sl, dr, t * 4 + h:t * 4 + h + 1], pst[sl, h * 128:(h + 1) * 128], OP.mult, OP.add, ["CN32", "GLA", pstk], ["CN32"])
                    act(CNB[:].rearrange("p a b -> p (a b)"), CN32[:].rearrange("p a b -> p (a b)"), AF.Copy, ["CN32"], ["CNB"])
                    pn3 = pn[:, :].rearrange("p (a two t) -> p two a t", two=2, t=128)
                    DNv = DN[:].rearrange("p (two a) t -> p two a t", two=2)
                    for par in range(2):
                        hb = par * 64
                        num = slice(hb, hb + 64)
                        den = slice(64 - hb, 128 - hb)
                        act(DNv[den, par, :, :], pn3[den, par, :, :], AF.Abs, [pnk], ["DN%d" % par])
                        ts("dve", DNv[den, par, :, :], DNv[den, par, :, :], 1.0, None, OP.max, None, ["DN%d" % par], ["DN%d" % par])
                        recip(DNv[den, par, :, :], DNv[den, par, :, :], ["DN%d" % par], ["DN%d" % par])
                        dst = HF[num, :, tsl] if dr == 0 else HT_[num, :].rearrange("p (a t) -> p a t", a=2)
                        tt("dve", dst, pn3[num, par, :, :], DNv[den, par, :, :], OP.mult, [pnk, "DN%d" % par], ["HF"] if dr == 0 else ["HTd_%d" % bp])
                    if dr == 1:
                        for pp in range(2):
                            tt("dve", HT_[:, pp * 128:(pp + 1) * 128], HT_[:, pp * 128:(pp + 1) * 128], HF[:, pp, tsl], OP.add,
                               ["HTd_%d" % bp, "HF"], ["HTd_%d" % bp])

                producer(order[0], 0)
                for i_ in range(len(order)):
                    if i_ + 1 < len(order):
                        producer(order[i_ + 1], (i_ + 1) % 2)
                    consumer(order[i_], i_ % 2)
                    if i_ >= 1:
                        gn_d(order[i_ - 1], (i_ - 1) % 2)
                gn_d(order[-1], (len(order) - 1) % 2)
        P.barrier()
        epilogue(3, extra_sig=True)
        if dbg:
            for t in range(NT):
                dma(dbg_x[l, t * 128:(t + 1) * 128, :], X[:, t, :], r=["X%d" % t], key="dbg")

    with contextlib.ExitStack() as fs:
        FG = sb("FG", [128, D], F32, fs)
        YO = [sb("YO%d" % i, [128, D], F32, fs) for i in range(2)]
        JUNK = sb("JUNKf", [128, D], BF16, fs)
        dma(FG[:], fing_d, w=["FG"])
        for t in range(2, NT):
            yo = YO[t % 2]
            yk = "YO%d" % (t % 2)
            act(JUNK[:], X[:, t, :], AF.Square, ["X%d" % t], ["JUNKf", "SS"], accum=SS[:, 0:1])
            act(SS[:, 1:2], SS[:, 0:1], AF.Ln, ["SS", "CC"], ["SS1"], bias=EPS_AP, scale=1.0 / D)
            act(SS[:, 2:3], SS[:, 1:2], AF.Exp, ["SS1"], ["SS2"], scale=-0.5)
            stt(yo[:], X[:, t, :], SS[:, 2:3], FG[:], OP.mult, OP.mult, ["X%d" % t, "SS2", "FG"], [yk])
            dma(out_d[(t - 2) * 128:(t - 1) * 128, :], yo[:], r=[yk], key="out%d" % (t % 2))
    P.emit(nc)
    st.close()
    return nc, P


_CACHE = {}


def _prep_shared(inputs):
    f = lambda a: np.ascontiguousarray(np.asarray(a, dtype=np.float32))
    w_in = f(inputs["w_in"]); b_in = f(inputs["b_in"])
    sh = {}
    sh["w_mod"] = f(inputs["w_mod"])
    b_mod = f(inputs["b_mod"])
    sh["bmodT"] = np.ascontiguousarray(b_mod.reshape(NL, 24, 128).transpose(2, 0, 1).reshape(128, NL * 24))
    sh["bgate"] = np.ascontiguousarray(np.broadcast_to(b_mod[:, None, 2048:3072], (NL, 128, D)))
    sh["normgT"] = np.ascontiguousarray(f(inputs["norm_g"]).reshape(NL, 8, 128).transpose(2, 0, 1).reshape(128, NL * 8))
    w2p = w_in[:, :, PERM].reshape(NL, 8, 128, NW // 128, 128)
    sh["w2"] = np.ascontiguousarray(w2p.transpose(0, 3, 2, 1, 4)).reshape(NL, NW // 128, 128, 8 * 128)
    b2 = b_in[:, PERM]
    nch = NW // 128
    b2T = np.zeros((128, NL, 64), np.float32)
    b2T[:, :, :nch] = b2.reshape(NL, nch, 128).transpose(2, 0, 1)
    sh["b2T"] = b2T.reshape(128, NL * 64)
    b2row = np.zeros((NL, 1, NTM), np.float32)
    for n_ in TMSEGS:
        s0, n = SEGS[n_]
        b2row[:, 0, TMOFF[n_]:TMOFF[n_] + n] = b2[:, s0:s0 + n]
    sh["b2row"] = b2row
    sh["w_out"] = f(inputs["w_out"])
    sh["fing"] = np.ascontiguousarray(np.broadcast_to(f(inputs["final_g"])[None, :], (128, D)))
    sh["lam"] = np.ascontiguousarray(np.broadcast_to(f(inputs["diff_lam"]).reshape(1, NL * 128), (128, NL * 128)))
    gcol = np.zeros((128, NL, 3), np.float32)
    for i, k in enumerate(["diff_g", "hg_g", "ml_g"]):
        g = f(inputs[k])
        gcol[:, :, i] = g[:, np.arange(128) % 64].T
    sh["gcol"] = gcol.reshape(128, NL * 3)
    sh["hglb"] = np.ascontiguousarray(np.broadcast_to(f(inputs["hg_lb"]).reshape(1, 512), (128, 512)))
    sh["sink"] = np.ascontiguousarray(np.broadcast_to(f(inputs["sw_sink"]).reshape(1, NL * 4), (128, NL * 4)))
    tabs = rope_tables()
    for k in ("Ac", "As", "Cc", "Cs"):
        sh["tab" + k] = tabs[k]
    ca = const_arrays()
    sh["cst"] = np.ascontiguousarray(np.stack([ca[n] for n in CONST_F32], 1).reshape(128, -1))
    hm = np.stack([(np.arange(128) // 64 == 0), (np.arange(128) // 64 == 1)], 1).astype(np.float32)
    sh["small"] = np.ascontiguousarray(np.concatenate([ca["segind"], ca["rowmask"], hm], 1))
    sh["wmask"] = ca["wmask"]
    return sh


def kernel(x, c, ctx, c_ctx, w_mod, b_mod, norm_g, w_in, b_in, diff_lam, diff_g, hg_lb, hg_g, sw_sink, ml_g,
           w_out, final_g, _dbg=False):
    inputs = dict(x=x, c=c, ctx=ctx, c_ctx=c_ctx, w_mod=w_mod, b_mod=b_mod, norm_g=norm_g, w_in=w_in, b_in=b_in,
                  diff_lam=diff_lam, diff_g=diff_g, hg_lb=hg_lb, hg_g=hg_g, sw_sink=sw_sink, ml_g=ml_g,
                  w_out=w_out, final_g=final_g)
    sh = _prep_shared(inputs)
    x = np.asarray(x, np.float32); ctx = np.asarray(ctx, np.float32)
    c = np.asarray(c, np.float32); c_ctx = np.asarray(c_ctx, np.float32)
    key = "dbg" if _dbg else "main"
    if key not in _CACHE:
        _CACHE[key] = build_program(dbg=_dbg)[0]
    nc = _CACHE[key]
    in_maps = []
    for b in range(8):
        m = dict(sh)
        m["xin"] = np.ascontiguousarray(np.concatenate([ctx[b], x[b]], 0))
        cT = np.concatenate([c[b].reshape(8, 128).T, c_ctx.reshape(8, 128).T], 1)
        m["cT"] = np.ascontiguousarray(cT)
        in_maps.append(m)
    res = run_bass_kernel_spmd(nc, in_maps, core_ids=list(range(8)))
    out = np.stack([np.asarray(r["out"], np.float32) for r in res.results], 0)
    if _dbg:
        return out, res.results
    return out
```

```python
import bisect
import contextlib
import math
import numpy as np
import ml_dtypes
import concourse.bass as bass
import concourse.mybir as mybir
from concourse.bass_utils import run_bass_kernel_spmd

F32 = mybir.dt.float32
BF16 = mybir.dt.bfloat16
AF = mybir.ActivationFunctionType
OP = mybir.AluOpType

D = 1024
NL = 2
T = 2304
NT = 18
EPS = 1e-6
DBG = False
import os
KSTOP = int(os.environ.get('KSTOP', '99'))
KSUB = float(os.environ.get('KSUB', '99'))
KTILES = int(os.environ.get('KTILES', '99'))
KF = os.environ.get('KF', '234')
KDIRS = int(os.environ.get('KDIRS', '2'))


class StopB(Exception):
    pass


def chk(n):
    if KSUB <= n:
        raise StopB()


class Prog:
    ENGS = ("pe", "act", "dve", "pool", "sp")

    def __init__(self):
        self.ops = []
        self.last_w = {}
        self.readers = {}
        self.dma_keys = {}
        self.last_on = {}
        self.sp_barrier = None

    def op(self, eng, fn, r=(), w=(), dma_key=None, extra=()):
        i = len(self.ops)
        deps = set()
        for k in list(r) + list(w):
            lw = self.last_w.get(k)
            if lw is not None:
                deps.add(lw)
        for k in w:
            for rd in self.readers.get(k, ()):
                deps.add(rd)
        keep = set(extra)
        rs = set(r)
        for d in deps:
            od = self.ops[d]
            if od["dma_key"] is None and dma_key is None and od["eng"] == eng:
                if eng == "pe":
                    continue
                if not (set(od["w"]) & rs):
                    continue
            keep.add(d)
        keep.discard(i)
        self.ops.append(dict(eng=eng, fn=fn, deps=keep, dma_key=dma_key, w=tuple(w)))
        for k in w:
            self.last_w[k] = i
            self.readers[k] = []
        for k in r:
            self.readers.setdefault(k, []).append(i)
        if dma_key is not None:
            self.dma_keys.setdefault(dma_key, []).append(i)
        self.last_on[eng] = i
        return i

    def barrier(self):
        lasts = [v for v in self.last_on.values()]
        for e in ("pe", "act", "dve", "pool"):
            self.op(e, lambda eng: eng.nop(), extra=lasts)
        if self.sp_barrier is not None:
            self.sp_barrier(lasts)

    def emit(self, nc):
        ops = self.ops
        n = len(ops)
        signaled = [False] * n
        for o in ops:
            for d in o["deps"]:
                if ops[d]["dma_key"] is None:
                    signaled[d] = True
        cnt = {}
        sigval = [0] * n
        for i, o in enumerate(ops):
            if o["dma_key"] is None and signaled[i]:
                cnt[o["eng"]] = cnt.get(o["eng"], 0) + 1
                sigval[i] = cnt[o["eng"]]
        self.max_counts = cnt
        ctxs = []
        sems = {}
        for e in self.ENGS:
            c = nc.semaphore("s_" + e)
            ctxs.append(c)
            sems[e] = c.__enter__()
        dsems = {}
        for k in self.dma_keys:
            c = nc.semaphore("d_" + str(k))
            ctxs.append(c)
            dsems[k] = c.__enter__()
        per_eng = {e: [] for e in self.ENGS}
        for i, o in enumerate(ops):
            per_eng[o["eng"]].append(i)
        blk = nc.Block()
        block = blk.__enter__()

        def make(e):
            def body(eng):
                waited = {}
                for i in per_eng[e]:
                    o = ops[i]
                    need = {}
                    for d in o["deps"]:
                        od = ops[d]
                        if od["dma_key"] is None:
                            key = ("e", od["eng"])
                            val = sigval[d]
                        else:
                            k = od["dma_key"]
                            key = ("d", k)
                            val = 16 * bisect.bisect_left(self.dma_keys[k], i)
                        if val > need.get(key, 0):
                            need[key] = val
                    for key, val in need.items():
                        if waited.get(key, 0) >= val:
                            continue
                        waited[key] = val
                        s = sems[key[1]] if key[0] == "e" else dsems[key[1]]
                        eng.wait_ge(s, val)
                    ins = o["fn"](eng)
                    if o["dma_key"] is not None:
                        ins.then_inc(dsems[o["dma_key"]], 16)
                    elif signaled[i]:
                        ins.then_inc(sems[e], 1)
                if e == "sp":
                    for k, lst in self.dma_keys.items():
                        eng.wait_ge(dsems[k], 16 * len(lst))
            return body

        block.tensor(make("pe"))
        block.scalar(make("act"))
        block.vector(make("dve"))
        block.gpsimd(make("pool"))
        block.sync(make("sp"))
        blk.__exit__(None, None, None)
        for c in reversed(ctxs):
            c.__exit__(None, None, None)


def _swap_idx(n, grp):
    j = np.arange(n)
    return ((j // grp) ^ 1) * grp + j % grp


def build_cols():
    perm = []
    segs = {}

    def add(name, cols):
        segs[name] = (len(perm), len(cols))
        perm.extend(list(cols))

    sw32 = _swap_idx(32, 8)
    sw64 = _swap_idx(64, 16)
    for pr in range(2):
        q = 0 + pr * 128 + np.arange(128)
        k = 256 + pr * 128 + np.arange(128)
        qs = 0 + pr * 128 + (np.arange(128) // 32) * 32 + sw32[np.arange(128) % 32]
        ks = 256 + pr * 128 + (np.arange(128) // 32) * 32 + sw32[np.arange(128) % 32]
        add("Aq%d" % pr, q); add("Aqs%d" % pr, qs); add("Ak%d" % pr, k); add("Aks%d" % pr, ks)
        add("Av%d" % pr, 512 + pr * 128 + np.arange(128))
    add("Bff", 1024 + np.arange(256)); add("Bq", 768 + np.arange(256))
    add("Bfb", 1280 + np.arange(256)); add("Bi", 1536 + np.arange(256))
    for pr in range(2):
        q = 1792 + pr * 128 + np.arange(128)
        qs = 1792 + pr * 128 + (np.arange(128) // 64) * 64 + sw64[np.arange(128) % 64]
        add("Cq%d" % pr, q); add("Cqs%d" % pr, qs)
    for kv in range(2):
        k = 2048 + kv * 64 + np.arange(128) % 64
        ks = 2048 + kv * 64 + sw64[np.arange(128) % 64]
        add("Ck%d" % kv, k); add("Cks%d" % kv, ks)
    add("Cv", 2176 + np.arange(128))
    for pr in range(2):
        add("Dq%d" % pr, 2304 + pr * 128 + np.arange(128))
    for pr in range(2):
        add("Dk%d" % pr, 2560 + pr * 128 + np.arange(128))
    add("Dkt", 2560 + np.arange(256)); add("Dvt", 2816 + np.arange(256)); add("Dg", 3072 + np.arange(16)); add("pad", 3072 + np.zeros(112, np.int64))
    for pr in range(2):
        add("Dog%d" % pr, 3088 + pr * 128 + np.arange(128))
    for c in range(8):
        add("G%d" % c, 3344 + c * 128 + np.arange(128))
    return np.array(perm), segs


PERM, SEGS = build_cols()
NW = len(PERM)
TMSEGS = ["Av0", "Av1", "Bff", "Bq", "Bfb", "Bi", "Cv", "Dkt", "Dvt", "Dg"]
TMOFF = {}
_o = 0
for _n in TMSEGS:
    TMOFF[_n] = _o
    _o += SEGS[_n][1]
NTM = 2048


def rope_tables():
    n = 2048
    row = np.repeat(np.arange(32, dtype=np.float32), 64)
    col = np.tile(np.arange(64, dtype=np.float32), 32)
    out = {}
    for nm, dim in (("A", 32), ("C", 64)):
        q = dim // 4
        half = dim // 2
        inv = (10000.0 ** (-np.arange(0, half, 2, dtype=np.float32) / np.float32(half))).astype(np.float32)
        tc = np.zeros((128, n), np.float32)
        ts = np.zeros((128, n), np.float32)
        for r in range(128):
            j = r % dim
            g = j // q
            i = j % q
            pos = row if g < 2 else col
            ang = (pos * inv[i]).astype(np.float32)
            tc[r] = np.cos(ang)
            ts[r] = np.sin(ang) * (-1.0 if g % 2 == 0 else 1.0)
        out[nm + "c"] = tc
        out[nm + "s"] = ts
    return out


def const_arrays():
    c = {}
    s = np.arange(128)[:, None]
    t = np.arange(128)[None, :]
    same = (s // 32) == (t // 32)
    c["ident"] = np.eye(128, dtype=np.float32)
    c["ones"] = np.ones((128, 128), np.float32)
    c["bd64"] = (((s // 64) == (t // 64)) / 64.0).astype(np.float32)
    c["tri32f"] = (same & (s <= t)).astype(np.float32)
    c["tri32b"] = (same & (s >= t)).astype(np.float32)
    c["rem32f"] = (same & (s > t)).astype(np.float32)
    c["rem32b"] = (same & (s < t)).astype(np.float32)
    c["tri128f"] = (s <= t).astype(np.float32)
    c["tri128b"] = (s >= t).astype(np.float32)
    c["rem128f"] = (s > t).astype(np.float32)
    c["rem128b"] = (s < t).astype(np.float32)
    c["negf"] = np.where(s <= t, 0.0, -30000.0).astype(np.float32)
    c["negb"] = np.where(s >= t, 0.0, -30000.0).astype(np.float32)
    seg = np.zeros((128, 4), np.float32)
    seg[np.arange(128), np.arange(128) // 32] = 1.0
    c["segind"] = seg
    rm = np.zeros((128, 2), np.float32)
    rm[:, 0] = ((np.arange(128) % 64) < 32)
    rm[:, 1] = ((np.arange(128) % 64) >= 32)
    c["rowmask"] = rm
    k = np.arange(128)[:, None]
    q = np.arange(512)[None, :]
    wm = np.stack([(np.abs(q - r * 128 - k) <= 128) for r in range(-1, 5)], 0).astype(np.float32)
    c["wmask"] = np.ascontiguousarray(wm.transpose(1, 0, 2)).reshape(128, 6 * 512)
    return c


CONST_F32 = ["ident", "ones", "bd64", "tri32f", "tri32b", "rem32f", "rem32b", "tri128f", "tri128b",
             "rem128f", "rem128b", "negf", "negb"]


def build_program(dbg=False):
    nc = bass.Bass("TRN2", target_bir_lowering=False)
    P = Prog()
    st = contextlib.ExitStack()

    def din(name, shape, dt=F32):
        return nc.dram_tensor(name, list(shape), dt, kind="ExternalInput").ap()

    xin = din("xin", [T, D])
    cT_d = din("cT", [128, 16])
    wmod_d = din("w_mod", [NL, D, 3 * D])
    bmodT_d = din("bmodT", [128, NL * 24])
    bgate_d = din("bgate", [NL, 128, D])
    normg_d = din("normgT", [128, NL * 8])
    w2_d = din("w2", [NL, NW // 128, 128, 8 * 128])
    b2T_d = din("b2T", [128, NL * 64])
    b2row_d = din("b2row", [NL, 1, NTM])
    wout_d = din("w_out", [NL, D, D])
    fing_d = din("fing", [128, D])
    lam_d = din("lam", [128, NL * 128])
    gcol_d = din("gcol", [128, NL * 3])
    hglb_d = din("hglb", [128, 512])
    sink_d = din("sink", [128, NL * 4])
    tab_d = {k: din("tab" + k, [128, 2048]) for k in ("Ac", "As", "Cc", "Cs")}
    cst_d = din("cst", [128, len(CONST_F32) * 128])
    small_d = din("small", [128, 8])
    wmask_d = din("wmask", [128, 6 * 512])
    out_d = nc.dram_tensor("out", [2048, D], F32, kind="ExternalOutput").ap()
    if dbg:
        dbg_mix = nc.dram_tensor("dbg_mix", [NL * 4, 128, 2 * T], BF16, kind="ExternalOutput").ap()
        dbg_x = nc.dram_tensor("dbg_x", [NL, T, D], F32, kind="ExternalOutput").ap()

    uniq = [0]

    def sb(name, shape, dt, stack=None):
        uniq[0] += 1
        return (stack or st).enter_context(nc.sbuf_tensor("%s_%d" % (name, uniq[0]), list(shape), dt))

    X = sb("X", [128, NT, D], F32)
    HT = sb("HT", [128, 8, T], BF16)
    MIX = sb("MIX", [128, 2, T], BF16)
    CST = sb("CST", [128, len(CONST_F32), 128], F32)
    SMALL = sb("SMALL", [128, 8], F32)
    IDB = sb("IDB", [128, 128], BF16)
    ONESROW = sb("ONESROW", [1, 512], BF16)
    CTs = sb("CTs", [128, 16], F32)
    SC = sb("SC", [128, 16], F32)
    BMODT = sb("BMODT", [128, NL * 24], F32)
    NORMG = sb("NORMG", [128, NL * 8], F32)
    B2T = sb("B2T", [128, NL * 64], F32)
    GCOL = sb("GCOL", [128, NL * 3], F32)
    GCOLA = sb("GCOLA", [128, NL], F32)
    LB = sb("LB", [128, 256], F32)
    OML = sb("OML", [128, 256], F32)
    SINK = sb("SINK", [128, NL * 4], F32)
    ESINK = sb("ESINK", [128, NL * 4], F32)
    NLAM = sb("NLAM", [128, NL], F32)
    MODP = sb("MODP", [128, 32], F32)
    HS = sb("HS", [128, 16], F32)
    SH = sb("SH", [128, 16], F32)
    GATEB = sb("GATEB", [128, 2, D], F32)
    B2ROW = sb("B2ROW", [1, NTM], BF16)
    STG = [sb("STG%d" % i, [128, 8, 128], F32) for i in range(2)]
    SS = sb("SS", [128, 4], F32)
    CC = sb("CC", [128, 2], F32)
    P.op("pool", lambda e: e.memset(CC[:, 0:1], EPS), w=["CC"])
    P.op("pool", lambda e: e.memset(CC[:, 1:2], 1.0), w=["CC"])
    EPS_AP = CC[:, 0:1]
    ONE_AP = CC[:, 1:2]
    BSCR = sb("BSCR", [1, 8], F32)
    P.sp_barrier = lambda lasts: P.op("sp", lambda e: e.dma_start(out=BSCR[0:1, 0:8], in_=small_d[0:1, 0:8]), w=["BSCR"], dma_key="bar", extra=lasts)
    ist = contextlib.ExitStack()
    LAMIN = sb("LAMIN", [128, NL * 128], F32, ist)
    HGLB = sb("HGLB", [128, 512], F32, ist)
    EH = sb("EH", [128, 512], F32, ist)
    LT = sb("LT", [128, 64], F32, ist)
    LS = sb("LS", [128, 4], F32, ist)

    PB = [st.enter_context(nc.psum_tensor("PB%d" % i, [128, 512], F32)) for i in range(7)]
    PBT = st.enter_context(nc.psum_tensor("PBT", [128, 1024], BF16))
    bank_ctr = [0]

    def nb():
        i = bank_ctr[0] % 5
        bank_ctr[0] += 1
        return PB[i], "PB%d" % i

    def bk(i):
        return PB[i], "PB%d" % i

    acc_ctr = [0]

    def nacc():
        i = 5 + acc_ctr[0] % 2
        acc_ctr[0] += 1
        return PB[i], "PB%d" % i

    cidx = {n: i for i, n in enumerate(CONST_F32)}

    def C(name):
        return CST[:, cidx[name], :]

    dma_ctr = [0]

    def dma(out, in_, r=(), w=(), key=None):
        if key is None:
            key = "m%d" % (dma_ctr[0] % 4)
            dma_ctr[0] += 1
        return P.op("sp", lambda e: e.dma_start(out=out, in_=in_), r=r, w=w, dma_key=key)

    def mm(out, lhsT, rhs, start, stop, r, w, tp=None):
        if tp is None:
            return P.op("pe", lambda e: e.matmul(out, lhsT=lhsT, rhs=rhs, start=start, stop=stop), r=r, w=w)
        return P.op("pe", lambda e: e.matmul(out, lhsT=lhsT, rhs=rhs, start=start, stop=stop, tile_position=tp), r=r, w=w)

    def act(out, in_, func, r, w, bias=None, scale=None, accum=None):
        kw = {}
        if bias is not None:
            kw["bias"] = bias
        if scale is not None:
            kw["scale"] = scale
        if accum is not None:
            kw["accum_out"] = accum
        return P.op("act", lambda e: e.activation(out=out, in_=in_, func=func, **kw), r=r, w=w)

    def tt(eng, out, in0, in1, op, r, w):
        return P.op(eng, lambda e: e.tensor_tensor(out=out, in0=in0, in1=in1, op=op), r=r, w=w)

    def ts(eng, out, in0, s1, s2, op0, op1, r, w):
        if op1 is None and eng == "pool" and op0 == OP.mult:
            s2, op1 = 0.0, OP.add
        if op1 is None:
            return P.op(eng, lambda e: e.tensor_scalar(out=out, in0=in0, scalar1=s1, scalar2=None, op0=op0), r=r, w=w)
        return P.op(eng, lambda e: e.tensor_scalar(out=out, in0=in0, scalar1=s1, scalar2=s2, op0=op0, op1=op1), r=r, w=w)

    def stt(out, in0, scalar, in1, op0, op1, r, w):
        return P.op("dve", lambda e: e.scalar_tensor_tensor(out=out, in0=in0, scalar=scalar, in1=in1, op0=op0, op1=op1), r=r, w=w)

    def cp(eng, out, in_, r, w):
        return P.op(eng, lambda e: e.tensor_copy(out=out, in_=in_), r=r, w=w)

    def memset(eng, ap, val, w):
        return P.op(eng, lambda e: e.memset(ap, val), w=w)

    def tss(out, in_, scalar, op, r, w):
        return P.op("dve", lambda e: e.tensor_single_scalar(out=out, in_=in_, scalar=scalar, op=op), r=r, w=w)

    def recip(out, in_, r, w):
        return P.op("dve", lambda e: e.reciprocal(out=out, in_=in_), r=r, w=w)

    for t in range(NT):
        dma(X[:, t, :], xin[t * 128:(t + 1) * 128, :], w=["X%d" % t], key="x%d" % (t % 4))
    dma(CST[:].rearrange("p a b -> p (a b)"), cst_d, w=["CST"])
    dma(SMALL[:], small_d, w=["SMALL"])
    dma(CTs[:], cT_d, w=["CTs"])
    dma(BMODT[:], bmodT_d, w=["BMODT"])
    dma(NORMG[:], normg_d, w=["NORMG"])
    dma(B2T[:], b2T_d, w=["B2T"])
    dma(LAMIN[:], lam_d, w=["LAMIN"])
    dma(GCOL[:], gcol_d, w=["GCOL"])
    dma(HGLB[:], hglb_d, w=["HGLB"])
    dma(SINK[:], sink_d, w=["SINK"])
    SEGIND = SMALL[:, 0:4]
    ROWM = SMALL[:, 4:6]
    HEADM = SMALL[:, 6:8]

    cp("dve", IDB[:], C("ident"), ["CST"], ["IDB"])
    memset("pool", ONESROW[:], 1.0, ["ONESROW"])
    act(SC[:], CTs[:], AF.Silu, ["CTs"], ["SC"])
    act(ESINK[:], SINK[:], AF.Exp, ["SINK"], ["ESINK"])
    act(EH[:], HGLB[:], AF.Exp, ["HGLB"], ["EH"])
    tt("dve", OML[:], EH[:, 0:256], EH[:, 256:512], OP.add, ["EH"], ["OML"])
    recip(OML[:], OML[:], ["OML"], ["OML"])
    tt("dve", LB[:], EH[:, 256:512], OML[:], OP.mult, ["EH", "OML"], ["LB"])
    ts("dve", OML[:], LB[:], -1.0, 1.0, OP.mult, OP.add, ["LB"], ["OML"])
    for l in range(NL):
        lam_init = 0.8 - 0.6 * math.exp(-0.3 * l)
        for j in range(2):
            tt("dve", LT[:, j * 32:(j + 1) * 32], LAMIN[:, l * 128 + j * 64:l * 128 + j * 64 + 32],
               LAMIN[:, l * 128 + j * 64 + 32:l * 128 + j * 64 + 64], OP.mult, ["LAMIN"], ["LT"])
            P.op("dve", lambda e, j=j: e.reduce_sum(out=LS[:, j:j + 1], in_=LT[:, j * 32:(j + 1) * 32], axis=mybir.AxisListType.X),
                 r=["LT"], w=["LS"])
        act(LS[:, 2:4], LS[:, 0:2], AF.Exp, ["LS"], ["LS"])
        tt("dve", LS[:, 0:1], LS[:, 3:4], LS[:, 2:3], OP.subtract, ["LS"], ["LS"])
        ts("dve", NLAM[:, l:l + 1], LS[:, 0:1], -lam_init, None, OP.add, None, ["LS"], ["NLAM"])
        ts("dve", GCOLA[:, l:l + 1], GCOL[:, l * 3:l * 3 + 1], 1.0 - lam_init, None, OP.mult, None, ["GCOL"], ["GCOLA"])

    P.barrier()
    ist.close()
    stg_ctr = [0]

    def load_w(l, seg, dst, dkey):
        s0, n = SEGS[seg]
        for c0 in range(0, n, 128):
            cn = min(128, n - c0)
            i = stg_ctr[0] % 2
            stg_ctr[0] += 1
            src = w2_d[l, (s0 + c0) // 128, :, :].rearrange("p (kc n) -> p kc n", kc=8)[:, :, 0:cn]
            dma(STG[i][:, :, 0:cn], src, w=["STG%d" % i], key="stg%d" % i)
            if i == 0:
                act(dst[:, :, c0:c0 + cn], STG[i][:, :, 0:cn], AF.Copy, ["STG%d" % i], [dkey])
            else:
                cp("dve", dst[:, :, c0:c0 + cn], STG[i][:, :, 0:cn], ["STG%d" % i], [dkey])

    def proj_fm(ps, pkey, wt, wkey, tok0, ntok):
        for kc in range(8):
            mm(ps[:, 0:ntok], wt[:, kc, :], HT[:, kc, tok0:tok0 + ntok], kc == 0, kc == 7,
               [wkey, "HT"], [pkey])

    def proj_tm(ps_ap, pkey, wt, wkey, c0, ncol, tile, s0):
        for kc in range(8):
            mm(ps_ap, HT[:, kc, tile * 128:(tile + 1) * 128], wt[:, kc, c0:c0 + ncol], kc == 0, False,
               [wkey, "HT"], [pkey])
        mm(ps_ap, ONESROW[0:1, 0:128], B2ROW[0:1, s0 + c0:s0 + c0 + ncol], False, True, ["ONESROW", "B2ROW"], [pkey])

    BLOCKS = [(0, 256, True)] + [(256 + 512 * j, 512, False) for j in range(4)]

    for l in range(NL if KSTOP >= 10 else 1):
        need_ctx = l < NL - 1
        blocks = BLOCKS if need_ctx else BLOCKS[1:]
        tiles_out = list(range(NT)) if need_ctx else list(range(2, NT))
        for hh_ in range(2):
            B2ROWF = STG[hh_][0:1, :, :].rearrange("p a b -> p (a b)")
            dma(B2ROWF[:, 0:1024], b2row_d[l, :, hh_ * 1024:(hh_ + 1) * 1024], w=["STG%d" % hh_], key="stg%d" % hh_)
            cp("dve", B2ROW[:, hh_ * 1024:(hh_ + 1) * 1024], B2ROWF, ["STG%d" % hh_], ["B2ROW"])
        dma(GATEB[:, 0, :], bgate_d[l], w=["GATEB"])
        cp("pool", GATEB[:, 1, :], GATEB[:, 0, :], ["GATEB"], ["GATEB"])

        if KSTOP < 1:
            break
        with contextlib.ExitStack() as ms:
            SCB = sb("SCB", [128, 16, 128], F32, ms)
            WM = [sb("WMs%d" % i, [128, 8, 512], F32, ms) for i in range(3)]
            for kc in range(16):
                ts("dve", SCB[:, kc, :], C("ones"), SC[:, kc:kc + 1], None, OP.mult, None, ["CST", "SC"], ["SCB"])
            sc3 = SC[:].rearrange("p (a k) -> p a k", a=2)
            pm, pmk = nacc()
            for j in range(6):
                wm = WM[j % 3]
                wmk = "WM%d" % (j % 3)
                for kh in range(2):
                    dma(wm[:, kh * 4:(kh + 1) * 4, :],
                        wmod_d[l, kh * 512:(kh + 1) * 512, j * 512:(j + 1) * 512].rearrange("(kc p) n -> p kc n", p=128),
                        w=[wmk], key="wm%d_%d" % (j % 3, kh))
                if j < 4:
                    for c4 in range(4):
                        ch = j * 4 + c4
                        for kc in range(8):
                            mm(pm[:, ch * 2:ch * 2 + 2], wm[:, kc, c4 * 128:(c4 + 1) * 128], sc3[:, :, kc], kc == 0, kc == 7,
                               [wmk, "SC"], [pmk])
                else:
                    for which in range(2):
                        pg, pgk = nb()
                        for kc in range(8):
                            mm(pg[:, :], SCB[:, which * 8 + kc, :], wm[:, kc, :], kc == 0, kc == 7, [wmk, "SCB"], [pgk])
                        tt("dve", GATEB[:, which, (j - 4) * 512:(j - 3) * 512], pg[:, :], GATEB[:, which, (j - 4) * 512:(j - 3) * 512],
                           OP.add, [pgk, "GATEB"], ["GATEB"])
            cp("dve", MODP[:], pm[:, 0:32], [pmk], ["MODP"])
            mp3 = MODP[:].rearrange("p (c a) -> p c a", a=2)
            for a in range(2):
                tt("dve", SH[:, a * 8:(a + 1) * 8], mp3[:, 0:8, a], BMODT[:, l * 24:l * 24 + 8], OP.add, ["MODP", "BMODT"], ["SH"])
                tt("dve", HS[:, a * 8:(a + 1) * 8], mp3[:, 8:16, a], BMODT[:, l * 24 + 8:l * 24 + 16], OP.add, ["MODP", "BMODT"], ["HS"])
                stt(HS[:, a * 8:(a + 1) * 8], HS[:, a * 8:(a + 1) * 8], 1.0, NORMG[:, l * 8:(l + 1) * 8], OP.add, OP.mult,
                    ["HS", "NORMG"], ["HS"])
        P.barrier()

        if KSTOP < 2:
            break
        with contextlib.ExitStack() as ns:
            XN = [sb("XN%d" % i, [128, D], BF16, ns) for i in range(2)]
            JUNK = sb("JUNK", [128, D], BF16, ns)
            SSA = sb("SSA", [128, 2 * NT], F32, ns)
            for t in range(NT):
                act(JUNK[:], X[:, t, :], AF.Square, ["X%d" % t], ["JUNK", "SSA"], accum=SSA[:, t:t + 1])
            act(SSA[:, NT:2 * NT], SSA[:, 0:NT], AF.Ln, ["SSA", "CC"], ["SSA1"], bias=EPS_AP, scale=1.0 / D)
            act(SSA[:, NT:2 * NT], SSA[:, NT:2 * NT], AF.Exp, ["SSA1"], ["SSA1"], scale=-0.5)
            for t in range(NT):
                a = 1 if t < 2 else 0
                xn = XN[t % 2]
                xk = "XN%d" % (t % 2)
                ts("dve", xn[:], X[:, t, :], SSA[:, NT + t:NT + t + 1], None, OP.mult, None, ["X%d" % t, "SSA1"], [xk])
                for kc in range(8):
                    P.op("pe", lambda e, kc=kc, xn=xn: e.transpose(out=PBT[:, kc * 128:(kc + 1) * 128], in_=xn[:, kc * 128:(kc + 1) * 128], identity=IDB[:]),
                         r=[xk, "IDB"], w=["PBT"])
                for kc in range(8):
                    eng = "dve" if kc % 2 == 0 else "pool"
                    if eng == "dve":
                        ts("dve", HT[:, kc, t * 128:(t + 1) * 128], PBT[:, kc * 128:(kc + 1) * 128], HS[:, a * 8 + kc:a * 8 + kc + 1],
                           SH[:, a * 8 + kc:a * 8 + kc + 1], OP.mult, OP.add, ["PBT", "HS", "SH"], ["HT"])
                    else:
                        act(HT[:, kc, t * 128:(t + 1) * 128], PBT[:, kc * 128:(kc + 1) * 128], AF.Identity, ["PBT", "HS", "SH"], ["HT"],
                            bias=SH[:, a * 8 + kc:a * 8 + kc + 1], scale=HS[:, a * 8 + kc:a * 8 + kc + 1])
        P.barrier()

        def epilogue(mx, extra_sig=None):
            if dbg and l == 0 and KSTOP < 10:
                dma(dbg_mix[4 + mx], MIX[:].rearrange("p a t -> p (a t)"), r=["MIX"], key="dbg")
            with contextlib.ExitStack() as es:
                WG = sb("WG", [128, 8, 256], BF16, es)
                SG = [sb("SGt%d" % i, [128, 512], BF16, es) for i in range(2)]
                WOS = sb("WOS", [128, 2, D], F32, es)
                WO = sb("WO", [128, 2, 2, D], BF16, es)
                load_w(l, "G%d" % (2 * mx), WG[:, :, 0:128], "WG")
                load_w(l, "G%d" % (2 * mx + 1), WG[:, :, 128:256], "WG")
                if extra_sig is not None:
                    WS = sb("WSg", [128, 8, 256], BF16, es)
                    load_w(l, "Dog0", WS[:, :, 0:128], "WSg")
                    load_w(l, "Dog1", WS[:, :, 128:256], "WSg")
                dma(WOS[:], wout_d[l, mx * 256:(mx + 1) * 256, :].rearrange("(pc p) n -> p pc n", p=128), w=["WOS"])
                for pc in range(2):
                    for a in range(2 if need_ctx else 1):
                        tt("dve", WO[:, pc, a, :], WOS[:, pc, :], GATEB[:, a, :], OP.mult, ["WOS", "GATEB"], ["WO"])
                for (tok0, ntok, isctx) in blocks:
                    for pc in range(2):
                        ps, pk = nb()
                        proj_fm(ps, pk, WG[:, :, pc * 128:(pc + 1) * 128], "WG", tok0, ntok)
                        sg = SG[pc]
                        gi = SEGS["G%d" % (2 * mx + pc)][0] // 128
                        act(sg[:, 0:ntok], ps[:, 0:ntok], AF.Silu, [pk, "B2T"], ["SG%d" % pc], bias=B2T[:, l * 64 + gi:l * 64 + gi + 1])
                        tt("dve", MIX[:, pc, tok0:tok0 + ntok], MIX[:, pc, tok0:tok0 + ntok], sg[:, 0:ntok], OP.mult, ["MIX", "SG%d" % pc], ["MIX"])
                        if extra_sig is not None:
                            ps2, pk2 = nb()
                            proj_fm(ps2, pk2, WS[:, :, pc * 128:(pc + 1) * 128], "WSg", tok0, ntok)
                            oi = SEGS["Dog%d" % pc][0] // 128
                            act(sg[:, 0:ntok], ps2[:, 0:ntok], AF.Sigmoid, [pk2, "B2T"], ["SG%d" % pc], bias=B2T[:, l * 64 + oi:l * 64 + oi + 1])
                            tt("dve", MIX[:, pc, tok0:tok0 + ntok], MIX[:, pc, tok0:tok0 + ntok], sg[:, 0:ntok], OP.mult, ["MIX", "SG%d" % pc], ["MIX"])
                    for tl in range(tok0 // 128, (tok0 + ntok) // 128):
                        a = 1 if isctx else 0
                        for half in range(2):
                            po, pok = nb()
                            for pc in range(2):
                                mm(po[:, :], MIX[:, pc, tl * 128:(tl + 1) * 128], WO[:, pc, a, half * 512:(half + 1) * 512], pc == 0, pc == 1,
                                   ["MIX", "WO"], [pok])
                            tt("dve", X[:, tl, half * 512:(half + 1) * 512], po[:, :], X[:, tl, half * 512:(half + 1) * 512], OP.add,
                               [pok, "X%d" % tl], ["X%d" % tl])
                if dbg:
                    dma(dbg_mix[l * 4 + mx], MIX[:].rearrange("p a t -> p (a t)"), r=["MIX"], key="dbg")
            P.barrier()

        def bias_col(seg):
            gi = SEGS[seg][0] // 128
            return B2T[:, l * 64 + gi:l * 64 + gi + 1]

        def group_norm_store(oa_ap, oakey, gcol_ap, ntok, dst_ap, scr, tagr, bank=None, split=None):
            SQ, RS = scr
            act(SQ[:, 0:ntok], oa_ap, AF.Square, [oakey], ["SQ" + tagr])
            pn, pnk = nb() if bank is None else bk(bank)
            mm(pn[:, 0:ntok], C("bd64"), SQ[:, 0:ntok], True, True, ["CST", "SQ" + tagr], [pnk])
            act(RS[:, 0:ntok], pn[:, 0:ntok], AF.Ln, [pnk], ["RS" + tagr], bias=EPS_AP, scale=1.0)
            act(RS[:, 0:ntok], RS[:, 0:ntok], AF.Exp, ["RS" + tagr], ["RS" + tagr], scale=-0.5)
            if split is None:
                stt(dst_ap, oa_ap, gcol_ap, RS[:, 0:ntok], OP.mult, OP.mult, [oakey, "RS" + tagr, "GCOL", "GCOLA"], ["MIX"])
            else:
                stt(dst_ap, oa_ap.rearrange("p (a b) -> p a b", a=split), gcol_ap, RS[:, 0:ntok].rearrange("p (a b) -> p a b", a=split),
                    OP.mult, OP.mult, [oakey, "RS" + tagr, "GCOL", "GCOLA"], ["MIX"])

        if KSTOP < 3:
            break
        for pr in range(2):
            with contextlib.ExitStack() as ws:
                WA = sb("WA", [128, 8, 512], BF16, ws)
                WV = sb("WV", [128, 8, 128], BF16, ws)
                QZ = [sb("QZ%d" % m, [128, T], BF16, ws) for m in range(2)]
                KT = sb("KTa", [128, 2, T], BF16, ws)
                VA = sb("VAa", [128, NT, 2, 192], BF16, ws)
                PT = [sb("PTa%d" % i, [128, 512], BF16, ws) for i in range(4)]
                OM = [sb("OM%d" % i, [128, 512], F32, ws) for i in range(2)]
                TB = OM
                OA = sb("OA", [128, 512], F32, ws)
                REC = sb("REC", [128, 512], F32, ws)
                SQ = sb("SQa", [128, 512], F32, ws)
                RS = sb("RSa", [128, 512], F32, ws)
                RT = [SQ, RS, OA]
                for i, sg in enumerate(["Aq%d" % pr, "Aqs%d" % pr, "Ak%d" % pr, "Aks%d" % pr]):
                    load_w(l, sg, WA[:, :, i * 128:(i + 1) * 128], "WA")
                load_w(l, "Av%d" % pr, WV[:, :, :], "WV")
                memset("pool", VA[:].rearrange("p a b c -> p (a b c)"), 1.0, ["VA"])
                for t in range(NT):
                    pv, pvk = nb()
                    proj_tm(pv[:, 0:128], pvk, WV, "WV", 0, 128, t, TMOFF["Av%d" % pr])
                    cp("dve", VA[:, t, :, 64:128], pv[:, 0:128].rearrange("p (h d) -> p h d", h=2), [pvk], ["VA"])
                for (tok0, ntok, isctx) in BLOCKS:
                    if not isctx:
                        dma(TB[0][:], tab_d["Ac"][:, tok0 - 256:tok0 - 256 + 512], w=["TB0"], key="tb0")
                        dma(TB[1][:], tab_d["As"][:, tok0 - 256:tok0 - 256 + 512], w=["TB1"], key="tb1")
                    for qk in range(2):
                        if qk == 0 and isctx and not need_ctx:
                            continue
                        ps, pk = nb()
                        proj_fm(ps, pk, WA[:, :, qk * 256:qk * 256 + 128], "WA", tok0, ntok)
                        bc = bias_col(("Aq%d" if qk == 0 else "Ak%d") % pr)
                        if isctx:
                            act(RT[2][:, 0:ntok], ps[:, 0:ntok], AF.Identity, [pk, "B2T"], ["RT2"], bias=bc)
                        else:
                            ps2, pk2 = nb()
                            proj_fm(ps2, pk2, WA[:, :, qk * 256 + 128:qk * 256 + 256], "WA", tok0, ntok)
                            bcs = bias_col(("Aqs%d" if qk == 0 else "Aks%d") % pr)
                            stt(RT[0][:, :], ps[:, :], bc, TB[0][:, :], OP.add, OP.mult, [pk, "TB0", "B2T"], ["RT0"])
                            stt(RT[1][:, :], ps2[:, :], bcs, TB[1][:, :], OP.add, OP.mult, [pk2, "TB1", "B2T"], ["RT1"])
                            tt("pool", RT[2][:, :], RT[0][:, :], RT[1][:, :], OP.add, ["RT0", "RT1"], ["RT2"])
                        if qk == 0:
                            for m in range(2):
                                if m == 0:
                                    act(QZ[m][:, tok0:tok0 + ntok], RT[2][:, 0:ntok], AF.Identity, ["RT2", "SMALL"], ["QZ%d" % m], scale=ROWM[:, m:m + 1])
                                else:
                                    ts("pool", QZ[m][:, tok0:tok0 + ntok], RT[2][:, 0:ntok], ROWM[:, m:m + 1], None, OP.mult, None,
                                       ["RT2", "SMALL"], ["QZ%d" % m])
                        else:
                            for hh_ in range(2):
                                if hh_ == 0:
                                    act(KT[:, hh_, tok0:tok0 + ntok], RT[2][:, 0:ntok], AF.Identity, ["RT2", "SMALL"], ["KTa"], scale=HEADM[:, hh_:hh_ + 1])
                                else:
                                    ts("pool", KT[:, hh_, tok0:tok0 + ntok], RT[2][:, 0:ntok], HEADM[:, hh_:hh_ + 1], None, OP.mult, None,
                                       ["RT2", "SMALL"], ["KTa"])
                P.barrier()
                items = []
                for (tok0, ntok, isctx) in blocks:
                    kts = [0, 1] if isctx else list(range(NT))
                    for hh in range(2):
                        for m in range(2):
                            pacc, pak = nacc()
                            for ki, kt in enumerate(kts):
                                items.append(dict(tok0=tok0, ntok=ntok, hh=hh, m=m, kt=kt, first=(ki == 0), last=(ki == len(kts) - 1),
                                                  pacc=pacc, pak=pak))

                rotA = [0]

                def S_a(it):
                    hb = it["hh"] * 64
                    ps, pk = bk(rotA[0] % 4)
                    rotA[0] += 1
                    it["ps"], it["pk"] = ps, pk
                    mm(ps[:, 0:it["ntok"]], KT[:, it["hh"], it["kt"] * 128:(it["kt"] + 1) * 128],
                       QZ[it["m"]][:, it["tok0"]:it["tok0"] + it["ntok"]], True, True, ["KTa", "QZ%d" % it["m"]], [pk])

                pti_a = [0]

                def EV_a(it):
                    hh, m, ntok, tok0 = it["hh"], it["m"], it["ntok"], it["tok0"]
                    hb = hh * 64
                    num = slice(hb, hb + 64)
                    den = slice(64 - hb, 128 - hb)
                    pacc, pak = it["pacc"], it["pak"]
                    pt = PT[pti_a[0] % 4]
                    ptk = "PT%d" % (pti_a[0] % 4)
                    pti_a[0] += 1
                    act(pt[:, 0:ntok], it["ps"][:, 0:ntok], AF.Exp, [it["pk"]], [ptk], scale=32 ** -0.5)
                    va = VA[:, it["kt"], hh, 64:192] if hh == 0 else VA[:, it["kt"], hh, 0:128]
                    mm(pacc[:, 0:ntok], va, pt[:, 0:ntok], it["first"], it["last"], ["VA", ptk], [pak])
                    if it["last"]:
                        recip(REC[den, 0:ntok], pacc[den, 0:ntok], [pak], ["REC"])
                        tt("dve", OM[m][num, 0:ntok], pacc[num, 0:ntok], REC[den, 0:ntok], OP.mult, [pak, "REC"], ["OM%d" % m])
                        if m == 1:
                            stt(OA[num, 0:ntok], OM[1][num, 0:ntok], NLAM[num, l:l + 1], OM[0][num, 0:ntok], OP.mult, OP.add,
                                ["OM0", "OM1", "NLAM"], ["OA"])
                            if hh == 1:
                                group_norm_store(OA[:, 0:ntok], "OA", GCOLA[:, l:l + 1], ntok, MIX[:, pr, tok0:tok0 + ntok], (SQ, RS), "a", bank=4)

                LA = 3
                for i_ in range(len(items) + LA):
                    if i_ < len(items):
                        S_a(items[i_])
                    if i_ >= LA:
                        EV_a(items[i_ - LA])
            P.barrier()
        epilogue(0)

        if KSTOP < 4:
            break
        with contextlib.ExitStack() as ws:
            WB = sb("WB", [128, 8, 1024], BF16, ws)
            M4F = sb("M4F", [128, 4, 128], BF16, ws)
            M4B = sb("M4B", [128, 4, 128], BF16, ws)
            for h in range(4):
                cp("dve", M4F[:, h, :], C("tri32f"), ["CST"], ["M4F"])
                cp("pool", M4B[:, h, :], C("tri32b"), ["CST"], ["M4B"])
            U = sb("Ub", [128, 256], F32, ws)
            FF = sb("Fb", [128, 256], F32, ws)
            LOGF = sb("LOGFb", [128, 256], F32, ws)
            KK = sb("KKb", [128, 256], F32, ws)
            VB = sb("VBb", [128, 256], BF16, ws)
            E = sb("Eb", [128, 256], F32, ws)
            EI = sb("EIb", [128, 256], F32, ws)
            ER = sb("ERb", [128, 256], F32, ws)
            G = sb("Gb", [128, 8], F32, ws)
            QE = sb("QEb", [128, 256], BF16, ws)
            KE = sb("KEb", [128, 256], BF16, ws)
            KEND = sb("KENDb", [128, 256], BF16, ws)
            KM = sb("KMb", [128, 4, 256], BF16, ws)
            QKT = sb("QKTb", [128, 4, 128], BF16, ws)
            AM = sb("AMb", [128, 4, 128], BF16, ws)
            S32S = sb("S32Sb", [128, 5, 2, 64], F32, ws)
            SBFS = sb("SBFSb", [128, 4, 2, 64], BF16, ws)
            OF = sb("OFb", [128, 2, T], BF16, ws)
            OT = sb("OTb", [128, 256], F32, ws)
            SQ = sb("SQb", [128, 512], F32, ws)
            RS = sb("RSb", [128, 512], F32, ws)
            for i, sg in enumerate(["Bff", "Bq", "Bfb", "Bi"]):
                load_w(l, sg, WB[:, :, i * 256:(i + 1) * 256], "WB")
            s0q = TMOFF["Bff"]
            try:
              chk(-1)
              for dr in range(2):
                if KSUB < 99 and dr == 1:
                    break
                tri = C("tri32f") if dr == 0 else C("tri32b")
                rem = C("rem32f") if dr == 0 else C("rem32b")
                m4 = M4F if dr == 0 else M4B
                order = list(range(NT)) if dr == 0 else [1, 0] + list(range(NT - 1, 1, -1))
                segs = [0, 1, 2, 3] if dr == 0 else [3, 2, 1, 0]
                order = order[:KTILES]
                if dr >= KDIRS:
                    break
                chk(0.1)
                memset("pool", S32S[:, 0, :, :].rearrange("p a b -> p (a b)"), 0.0, ["S32_0"])
                for t in order:
                    chk(0.2)
                    pq, pqk = bk(0)
                    proj_tm(pq[:, 0:512], pqk, WB, "WB", dr * 256, 512, t, s0q)
                    qsl = slice(256, 512) if dr == 0 else slice(0, 256)
                    zsl = slice(0, 256) if dr == 0 else slice(256, 512)
                    chk(0.3)
                    pv_, pvk_ = bk(2)
                    proj_tm(pv_[:, 0:256], pvk_, WB, "WB", 768, 256, t, s0q)
                    chk(0.4)
                    act(U[:], pq[:, zsl], AF.Exp, [pqk], ["U"], scale=-1.0)
                    chk(0.5)
                    act(U[:], U[:], AF.Ln, ["U", "CC"], ["U"], bias=ONE_AP, scale=1.0)
                    act(FF[:], U[:], AF.Exp, ["U"], ["FF"], scale=-1.0)
                    chk(0.6)
                    if l > 0:
                        tt("dve", FF[:], FF[:], OML[:], OP.mult, ["FF", "OML"], ["FF"])
                        tt("dve", FF[:], FF[:], LB[:], OP.add, ["FF", "LB"], ["FF"])
                        act(LOGF[:], FF[:], AF.Ln, ["FF"], ["LOGF"])
                    else:
                        ts("dve", LOGF[:], U[:], -1.0, None, OP.mult, None, ["U"], ["LOGF"])
                    ts("dve", KK[:], FF[:], -1.0, 1.0, OP.mult, OP.add, ["FF"], ["KK"])
                    cp("dve", VB[:], pv_[:, 0:256], [pvk_], ["VB"])
                    chk(1)
                    pc_, pck = bk(3)
                    mm(pc_[:, 0:256], tri, LOGF[:], True, True, ["CST", "LOGF"], [pck])
                    pr_, prk_ = bk(4)
                    mm(pr_[:, 0:256], rem, LOGF[:], True, True, ["CST", "LOGF"], [prk_])
                    pg, pgk = bk(1)
                    for hf in range(2):
                        mm(pg[:, hf * 4:(hf + 1) * 4], LOGF[:, hf * 128:(hf + 1) * 128], SEGIND, True, True, ["LOGF", "SMALL"], [pgk])
                    act(E[:], pc_[:, 0:256], AF.Exp, [pck], ["E"])
                    act(EI[:], pc_[:, 0:256], AF.Exp, [pck], ["EI"], scale=-1.0)
                    act(ER[:], pr_[:, 0:256], AF.Exp, [prk_], ["ER"])
                    act(G[:], pg[:, 0:8], AF.Exp, [pgk], ["G"])
                    chk(2)
                    stt(QE[:], pq[:, qsl], 0.125, E[:], OP.mult, OP.mult, [pqk, "E"], ["QE"])
                    tt("dve", KE[:], KK[:], EI[:], OP.mult, ["KK", "EI"], ["KE"])
                    tt("dve", KEND[:], KK[:], ER[:], OP.mult, ["KK", "ER"], ["KEND"])
                    for c4 in range(4):
                        ts("dve", KM[:, c4, :], KEND[:], SEGIND[:, c4:c4 + 1], None, OP.mult, None, ["KEND", "SMALL"], ["KM"])
                    for i4, (src, sk) in enumerate([(QE, "QE"), (QE, "QE"), (KE, "KE"), (KE, "KE")]):
                        hf = i4 % 2
                        P.op("pe", lambda e, i4=i4, src=src, hf=hf: e.transpose(out=PBT[:, i4 * 128:(i4 + 1) * 128], in_=src[:, hf * 128:(hf + 1) * 128], identity=IDB[:]),
                             r=[sk, "IDB"], w=["PBT"])
                    pd, pdk = bk(4)
                    for si, c4 in enumerate(segs):
                        for h in range(4):
                            hb = (h % 2) * 64
                            pp = h // 2
                            mm(pd[hb:hb + 64, si * 128 + pp * 64:si * 128 + (pp + 1) * 64], KM[:, c4, h * 64:(h + 1) * 64], VB[:, h * 64:(h + 1) * 64],
                               True, True, ["KM", "VB"], [pdk], tp=(0, hb))
                    for si, c4 in enumerate(segs):
                        for pp in range(2):
                            stt(S32S[:, si + 1, pp, :], S32S[:, si, pp, :], G[:, pp * 4 + c4:pp * 4 + c4 + 1], pd[:, si * 128 + pp * 64:si * 128 + (pp + 1) * 64],
                                OP.mult, OP.add, ["S32_%d" % si, "G", pdk], ["S32_%d" % (si + 1)])
                    act(SBFS[:].rearrange("p a b c -> p (a b c)"), S32S[:, 0:4, :, :].rearrange("p a b c -> p (a b c)"), AF.Copy,
                        ["S32_0", "S32_1", "S32_2", "S32_3"], ["SBFS"])
                    cp("pool", S32S[:, 0, :, :].rearrange("p a b -> p (a b)"), S32S[:, 4, :, :].rearrange("p a b -> p (a b)"), ["S32_4"], ["S32_0"])
                    chk(3)
                    cp("dve", QKT[:].rearrange("p a b -> p (a b)"), PBT[:, 0:512], ["PBT"], ["QKT"])
                    chk(4)
                    pa0, pak0 = bk(2)
                    pa1, pak1 = bk(3)
                    pas = [pa0, pa1]
                    paks = [pak0, pak1]
                    for h in range(4):
                        hb = (h % 2) * 64
                        pp = h // 2
                        mm(pas[h % 2][:, pp * 128:(pp + 1) * 128], QKT[hb:hb + 64, 2 + pp, :], QKT[hb:hb + 64, pp, :], True, True, ["QKT"], [paks[h % 2]])
                    for par in range(2):
                        tt("dve", AM[:, par::2, :] if False else AM[:].rearrange("p (a two) b -> p two a b", two=2)[:, par, :, :],
                           pas[par][:, 0:256].rearrange("p (a b) -> p a b", a=2), m4[:, 0:2, :], OP.mult, [paks[par], "M4F", "M4B"], ["AM"])
                    chk(5)
                    po0, pok0 = bk(5)
                    po1, pok1 = bk(6)
                    pos = [po0, po1]
                    poks = [pok0, pok1]
                    for h in range(4):
                        hb = (h % 2) * 64
                        pp = h // 2
                        po = pos[pp]
                        pok = poks[pp]
                        mm(po[hb:hb + 64, 0:128], VB[:, h * 64:(h + 1) * 64], AM[:, h, :], True, False, ["VB", "AM"],
                           [pok], tp=(0, hb))
                    chk(6)
                    for si, c4 in enumerate(segs):
                        for h in range(4):
                            hb = (h % 2) * 64
                            pp = h // 2
                            mm(pos[pp][hb:hb + 64, c4 * 32:c4 * 32 + 32], SBFS[hb:hb + 64, si, pp, :], QKT[hb:hb + 64, pp, c4 * 32:(c4 + 1) * 32],
                               False, si == 3, ["SBFS", "QKT"], [poks[pp]], tp=(hb, hb))
                    chk(7)
                    if dr == 0:
                        for pp in range(2):
                            cp("dve", OF[:, pp, t * 128:(t + 1) * 128], pos[pp][:, 0:128], [poks[pp]], ["OF"])
                        chk(8)
                    else:
                        for pp in range(2):
                            tt("dve", OT[:, pp * 128:(pp + 1) * 128], pos[pp][:, 0:128], OF[:, pp, t * 128:(t + 1) * 128], OP.add, [poks[pp], "OF"], ["OT"])
                        if need_ctx or t >= 2:
                            if "2" in KF:
                                group_norm_store(OT[:, 0:256], "OT", GCOL[:, l * 3 + 1:l * 3 + 2], 256,
                                                 MIX[:, :, t * 128:(t + 1) * 128], (SQ, RS), "b", bank=3, split=2)
                            else:
                                for pp in range(2):
                                    group_norm_store(OT[:, pp * 128:(pp + 1) * 128], "OT", GCOL[:, l * 3 + 1:l * 3 + 2], 128,
                                                     MIX[:, pp, t * 128:(t + 1) * 128], (SQ, RS), "b", bank=3)
            except StopB:
                pass
        P.barrier()
        epilogue(1)

        if KSTOP < 5:
            break
        with contextlib.ExitStack() as ws:
            QT = sb("QTc", [128, 4, T], BF16, ws)
            KT = sb("KTc", [128, 2, T], BF16, ws)
            VA = sb("VAc", [128, NT, 2, 192], BF16, ws)
            ws2 = contextlib.ExitStack()
            WC = sb("WC", [128, 8, 512], BF16, ws2)
            WV = sb("WVc", [128, 8, 128], BF16, ws2)
            TB = [sb("TBc%d" % i, [128, 512], F32, ws2) for i in range(2)]
            RT = [sb("RTc%d" % i, [128, 512], F32, ws2) for i in range(2)]
            names = ["Cq0", "Cqs0", "Cq1", "Cqs1", "Ck0", "Cks0", "Ck1", "Cks1"]
            load_w(l, "Cv", WV[:, :, :], "WVc")
            memset("pool", VA[:].rearrange("p a b c -> p (a b c)"), 1.0, ["VA"])
            for t in range(NT):
                pv, pvk = nb()
                proj_tm(pv[:, 0:128], pvk, WV, "WVc", 0, 128, t, TMOFF["Cv"])
                cp("dve", VA[:, t, :, 64:128], pv[:, 0:128].rearrange("p (h d) -> p h d", h=2), [pvk], ["VA"])
            for grp, (tok0, ntok, isctx) in [(g_, b_) for g_ in range(2) for b_ in BLOCKS]:
                if (tok0, ntok, isctx) == BLOCKS[0]:
                    for i in range(4):
                        load_w(l, names[grp * 4 + i], WC[:, :, i * 128:(i + 1) * 128], "WC")
                if not isctx:
                    dma(TB[0][:], tab_d["Cc"][:, tok0 - 256:tok0 - 256 + 512], w=["TB0"], key="tb0")
                    dma(TB[1][:], tab_d["Cs"][:, tok0 - 256:tok0 - 256 + 512], w=["TB1"], key="tb1")
                for ci in range(2 * grp, 2 * grp + 2):
                    if ci < 2 and isctx and not need_ctx:
                        continue
                    dst = RT[0][:, 0:ntok] if ci < 2 else KT[:, ci - 2, tok0:tok0 + ntok]
                    dk = "RT0" if ci < 2 else "KTc"
                    ps, pk = nb()
                    proj_fm(ps, pk, WC[:, :, (ci % 2) * 256:(ci % 2) * 256 + 128], "WC", tok0, ntok)
                    bc = bias_col(names[ci * 2])
                    if isctx:
                        act(dst, ps[:, 0:ntok], AF.Identity, [pk, "B2T"], [dk], bias=bc)
                    else:
                        ps2, pk2 = nb()
                        proj_fm(ps2, pk2, WC[:, :, (ci % 2) * 256 + 128:(ci % 2) * 256 + 256], "WC", tok0, ntok)
                        bcs = bias_col(names[ci * 2 + 1])
                        stt(RT[0][:, :], ps[:, :], bc, TB[0][:, :], OP.add, OP.mult, [pk, "TB0", "B2T"], ["RT0"])
                        stt(RT[1][:, :], ps2[:, :], bcs, TB[1][:, :], OP.add, OP.mult, [pk2, "TB1", "B2T"], ["RT1"])
                        tt("pool", dst, RT[0][:, :], RT[1][:, :], OP.add, ["RT0", "RT1"], [dk])
                    if ci < 2:
                        for half in range(2):
                            if half == 0:
                                act(QT[:, 2 * ci + half, tok0:tok0 + ntok], RT[0][:, 0:ntok], AF.Identity, ["RT0", "SMALL"], ["QTc"], scale=HEADM[:, half:half + 1])
                            else:
                                ts("pool", QT[:, 2 * ci + half, tok0:tok0 + ntok], RT[0][:, 0:ntok], HEADM[:, half:half + 1], None, OP.mult, None,
                                   ["RT0", "SMALL"], ["QTc"])
            P.barrier()
            ws2.close()
            PT = [sb("PTc%d" % i, [128, 512], BF16, ws) for i in range(4)]
            REC = sb("RECc", [128, 512], F32, ws)
            WMASK = sb("WMASK", [128, 6, 512], BF16, ws)
            for r6 in range(6):
                stg = STG[r6 % 2]
                sv = stg[:, 0:4, :].rearrange("p a b -> p (a b)")
                dma(sv, wmask_d[:, r6 * 512:(r6 + 1) * 512], w=["STG%d" % (r6 % 2)], key="stg%d" % (r6 % 2))
                cp("pool", WMASK[:, r6, :], sv, ["STG%d" % (r6 % 2)], ["WMASK"])
            items = []
            for (tok0, ntok, isctx) in blocks:
                if isctx:
                    kts = [(0, None), (1, None)]
                else:
                    J = (tok0 - 256) // 512
                    kts = [(0, None), (1, None)] + [(2 + lt, lt - 4 * J + 1) for lt in range(4 * J - 1, 4 * J + 5) if 0 <= lt < 16]
                for h in range(4):
                    pacc, pak = nacc()
                    for ki, (kt, mr) in enumerate(kts):
                        items.append(dict(tok0=tok0, ntok=ntok, h=h, kt=kt, mr=mr, first=(ki == 0), last=(ki == len(kts) - 1), pacc=pacc, pak=pak))

            def S_c(it):
                h = it["h"]
                hb = (h % 2) * 64
                ps, pk = nb()
                it["ps"], it["pk"] = ps, pk
                mm(ps[:, 0:it["ntok"]], KT[:, h // 2, it["kt"] * 128:(it["kt"] + 1) * 128],
                   QT[:, h, it["tok0"]:it["tok0"] + it["ntok"]], True, True, ["KTc", "QTc"], [pk])

            pti_c = [0]

            def EV_c(it):
                h, ntok, tok0 = it["h"], it["ntok"], it["tok0"]
                hb = (h % 2) * 64
                pp = h // 2
                kv = h // 2
                num = slice(hb, hb + 64)
                den = slice(64 - hb, 128 - hb)
                pacc, pak = it["pacc"], it["pak"]
                pt = PT[pti_c[0] % 4]
                ptk = "PT%d" % (pti_c[0] % 4)
                pti_c[0] += 1
                act(pt[:, 0:ntok], it["ps"][:, 0:ntok], AF.Exp, [it["pk"]], [ptk], scale=0.125)
                if it["mr"] is not None:
                    tt("dve", pt[:, 0:ntok], pt[:, 0:ntok], WMASK[:, it["mr"], 0:ntok], OP.mult, [ptk, "WMASK"], [ptk])
                va = VA[:, it["kt"], kv, 64:192] if hb == 0 else VA[:, it["kt"], kv, 0:128]
                mm(pacc[:, 0:ntok], va, pt[:, 0:ntok], it["first"], it["last"], ["VA", ptk], [pak])
                if it["last"]:
                    ts("dve", REC[den, 0:ntok], pacc[den, 0:ntok], ESINK[den, l * 4 + h:l * 4 + h + 1], None, OP.add, None, [pak, "ESINK"], ["REC"])
                    recip(REC[den, 0:ntok], REC[den, 0:ntok], ["REC"], ["REC"])
                    tt("dve", MIX[num, pp, tok0:tok0 + ntok], pacc[num, 0:ntok], REC[den, 0:ntok], OP.mult, [pak, "REC"], ["MIX"])

            LA = 4
            for i_ in range(len(items) + LA):
                if i_ < len(items):
                    S_c(items[i_])
                if i_ >= LA:
                    EV_c(items[i_ - LA])
        P.barrier()
        epilogue(2)

        if KSTOP < 6:
            break
        with contextlib.ExitStack() as ws:
            QKT = sb("QKTd", [128, 4, T], BF16, ws)
            ws2 = contextlib.ExitStack()
            WD = sb("WD", [128, 8, 512], BF16, ws2)
            for i, sg in enumerate(["Dq0", "Dq1", "Dk0", "Dk1"]):
                load_w(l, sg, WD[:, :, i * 128:(i + 1) * 128], "WD")
            for (tok0, ntok, isctx) in BLOCKS:
                for ci, sg in enumerate(["Dq0", "Dq1", "Dk0", "Dk1"]):
                    ps, pk = nb()
                    proj_fm(ps, pk, WD[:, :, ci * 128:(ci + 1) * 128], "WD", tok0, ntok)
                    if ci % 2 == 0:
                        act(QKT[:, ci, tok0:tok0 + ntok], ps[:, 0:ntok], AF.Identity, [pk, "B2T"], ["QKTd"], bias=bias_col(sg))
                    else:
                        ts("dve", QKT[:, ci, tok0:tok0 + ntok], ps[:, 0:ntok], bias_col(sg), None, OP.add, None, [pk, "B2T"], ["QKTd"])
            P.barrier()
            ws2.close()
            WT = sb("WTd", [128, 8, 528], BF16, ws)
            VA2 = [sb("VAd%d" % i, [128, 4, 192], BF16, ws) for i in range(2)]
            NEGB = sb("NEGBd", [128, 2, 128], BF16, ws)
            cp("dve", NEGB[:, 0, :], C("negf"), ["CST"], ["NEGB"])
            cp("dve", NEGB[:, 1, :], C("negb"), ["CST"], ["NEGB"])
            LBH = sb("LBHd", [128, 4, 128], F32, ws)
            DT = sb("DTd", [128, 4, 128], F32, ws)
            EROW = sb("EROWd", [128, 2, 128], F32, ws)
            QS2 = [sb("QSd%d" % i, [128, 2, 128], BF16, ws) for i in range(2)]
            SM2 = [sb("SMd%d" % i, [128, 4, 128], BF16, ws) for i in range(2)]
            KW2 = [sb("KWd%d" % i, [128, 4, 64], BF16, ws) for i in range(2)]
            CN32 = sb("CN32d", [128, 4, 128], F32, ws)
            CNB = sb("CNBd", [128, 4, 128], BF16, ws)
            DN = sb("DNd", [128, 4, 128], F32, ws)
            HF = sb("HFd", [128, 2, T], BF16, ws)
            HT2 = [sb("HTd%d" % i, [128, 256], F32, ws) for i in range(2)]
            SQ = sb("SQd", [128, 256], F32, ws)
            RS = sb("RSd", [128, 256], F32, ws)
            load_w(l, "Dkt", WT[:, :, 0:256], "WT")
            load_w(l, "Dvt", WT[:, :, 256:512], "WT")
            load_w(l, "Dg", WT[:, :, 512:528], "WT")
            s0k = TMOFF["Dkt"]
            for i_ in range(2):
                memset("pool", VA2[i_][:].rearrange("p a b -> p (a b)"), 1.0, ["VAd%d" % i_])
            LOGFA = sb("LOGFAd", [128, NT, 8], F32, ws)
            IGA = sb("IGAd", [128, NT, 8], F32, ws)
            BIASA = sb("BIASAd", [128, 2, NT * 4], F32, ws)
            WWA = sb("WWAd", [128, 2, NT * 4], F32, ws)
            GLA = sb("GLAd", [128, 2, NT * 4], F32, ws)
            pga, pgak = bk(1)
            for t in range(NT):
                proj_tm(pga[:, t * 16:(t + 1) * 16], pgak, WT, "WT", 512, 16, t, s0k)
            pga3 = pga[:, 0:NT * 16].rearrange("p (t g) -> p t g", g=16)
            act(LOGFA[:], pga3[:, :, 8:16], AF.Exp, [pgak], ["LOGFA"], scale=-1.0)
            act(LOGFA[:].rearrange("p t g -> p (t g)"), LOGFA[:].rearrange("p t g -> p (t g)"), AF.Ln, ["LOGFA", "CC"], ["LOGFA"], bias=ONE_AP, scale=1.0)
            ts("dve", LOGFA[:].rearrange("p t g -> p (t g)"), LOGFA[:].rearrange("p t g -> p (t g)"), -1.0, None, OP.mult, None, ["LOGFA"], ["LOGFA"])
            cp("dve", IGA[:], pga3[:, :, 0:8], [pgak], ["IGA"])
            for dr_ in range(2):
                tri_ = C("tri128f") if dr_ == 0 else C("tri128b")
                rem_ = C("rem128f") if dr_ == 0 else C("rem128b")
                pca, pcak = bk(2 + dr_)
                rhs_ = LOGFA[:, :, dr_ * 4:(dr_ + 1) * 4]
                mm(pca[:, 0:NT * 4].rearrange("p (t h) -> p t h", h=4), tri_, rhs_, True, True, ["CST", "LOGFA"], [pcak])
                mm(pca[:, NT * 4:2 * NT * 4].rearrange("p (t h) -> p t h", h=4), rem_, rhs_, True, True, ["CST", "LOGFA"], [pcak])
                mm(pca[:, 2 * NT * 4:3 * NT * 4].rearrange("p (t h) -> p t h", h=4), C("ones"), rhs_, True, True, ["CST", "LOGFA"], [pcak])
                tt("dve", BIASA[:, dr_, :].rearrange("p (t h) -> p t h", h=4), IGA[:, :, dr_ * 4:(dr_ + 1) * 4],
                   pca[:, 0:NT * 4].rearrange("p (t h) -> p t h", h=4), OP.subtract, ["IGA", pcak], ["BIASA"])
                tt("dve", WWA[:, dr_, :].rearrange("p (t h) -> p t h", h=4), IGA[:, :, dr_ * 4:(dr_ + 1) * 4],
                   pca[:, NT * 4:2 * NT * 4].rearrange("p (t h) -> p t h", h=4), OP.add, ["IGA", pcak], ["WWA"])
                act(WWA[:, dr_, :], WWA[:, dr_, :], AF.Exp, ["WWA"], ["WWA"])
                ts("dve", WWA[:, dr_, :], WWA[:, dr_, :], 0.125, None, OP.mult, None, ["WWA"], ["WWA"])
                act(GLA[:, dr_, :], pca[:, 2 * NT * 4:3 * NT * 4], AF.Exp, [pcak], ["GLA"])
            for dr in range(2):
                tri = C("tri128f") if dr == 0 else C("tri128b")
                rem = C("rem128f") if dr == 0 else C("rem128b")
                neg = C("negf") if dr == 0 else C("negb")
                order = list(range(NT)) if dr == 0 else [1, 0] + list(range(NT - 1, 1, -1))
                memset("pool", CN32[:].rearrange("p a b -> p (a b)"), 0.0, ["CN32"])
                memset("pool", CNB[:].rearrange("p a b -> p (a b)"), 0.0, ["CNB"])
                def producer(t, bp):
                    tsl = slice(t * 128, (t + 1) * 128)
                    VAp, KWp, QSp, SMp = VA2[bp], KW2[bp], QS2[bp], SM2[bp]
                    pk_, pkk = bk(0)
                    proj_tm(pk_[:, 0:512], pkk, WT, "WT", 0, 512, t, s0k)
                    cp("dve", VAp[:, :, 64:128], pk_[:, 256:512].rearrange("p (h d) -> p h d", h=4), [pkk], ["VAd%d" % bp])
                    tt("dve", KWp[:], pk_[:, 0:256].rearrange("p (h d) -> p h d", h=4),
                       WWA[:, dr, t * 4:(t + 1) * 4].rearrange("p (h o) -> p h o", o=1).to_broadcast([128, 4, 64]), OP.mult,
                       [pkk, "WWA"], ["KW%d_%d" % (h_, bp) for h_ in range(4)])
                    pf, pfk = bk(3)
                    pe2, pe2k = bk(4)
                    cp("dve", LBH[:], LOGFA[:, t, dr * 4:dr * 4 + 4].rearrange("p (h o) -> p h o", o=1).to_broadcast([128, 4, 128]),
                       ["LOGFA"], ["LBH%d" % h_ for h_ in range(4)])
                    for h in range(4):
                        hb = (h % 2) * 64
                        pp = h // 2
                        mm(pf[:, h * 128:(h + 1) * 128], LBH[:, h, :], tri, True, False, ["LBH%d" % h, "CST"], [pfk])
                        mm(pf[:, h * 128:(h + 1) * 128], IDB[:], NEGB[:, dr, :], False, True, ["IDB", "NEGB"], [pfk])
                        mm(pe2[hb:hb + 64, pp * 128:(pp + 1) * 128], LBH[:, h, 0:64], tri, True, True, ["LBH%d" % h, "CST"], [pe2k], tp=(0, hb))
                    for h in range(4):
                        act(DT[:, h, :], pf[:, h * 128:(h + 1) * 128], AF.Exp, [pfk, "BIASA"], ["DT"], bias=BIASA[:, dr, t * 4 + h:t * 4 + h + 1], scale=1.0)
                    act(EROW[:].rearrange("p a b -> p (a b)"), pe2[:, 0:256], AF.Exp, [pe2k], ["EROW"])
                    tt("dve", QSp[:], QKT[:, 0:2, tsl], EROW[:], OP.mult, ["QKTd", "EROW"], ["QS%d" % bp])
                    pkq0, pkqk0 = bk(3)
                    pkq1, pkqk1 = bk(4)
                    pkqs = [pkq0, pkq1]
                    pkqks = [pkqk0, pkqk1]
                    for h in range(4):
                        hb = (h % 2) * 64
                        pp = h // 2
                        mm(pkqs[h % 2][:, pp * 128:(pp + 1) * 128], QKT[hb:hb + 64, 2 + pp, tsl], QKT[hb:hb + 64, pp, tsl], True, True, ["QKTd"], [pkqks[h % 2]])
                    for par in range(2):
                        stt(SMp[:].rearrange("p (a two) b -> p two a b", two=2)[:, par, :, :], pkqs[par][:, 0:256].rearrange("p (a b) -> p a b", a=2), 0.125,
                            DT[:].rearrange("p (a two) b -> p two a b", two=2)[:, par, :, :], OP.mult, OP.mult, [pkqks[par], "DT"], ["SM%d" % bp])

                def gn_d(t, bp):
                    tsl = slice(t * 128, (t + 1) * 128)
                    if dr == 1 and (need_ctx or t >= 2):
                        group_norm_store(HT2[bp][:, 0:256], "HTd_%d" % bp, GCOL[:, l * 3 + 2:l * 3 + 3], 256,
                                         MIX[:, :, tsl], (SQ, RS), "d", bank=2, split=2)

                def consumer(t, bp):
                    HT_ = HT2[bp]
                    tsl = slice(t * 128, (t + 1) * 128)
                    VAp, KWp, QSp, SMp = VA2[bp], KW2[bp], QS2[bp], SM2[bp]
                    pn, pnk = bk(5 if bp == 0 else 1)
                    for h in range(4):
                        hb = (h % 2) * 64
                        pp = h // 2
                        va = VAp[:, h, 64:192] if hb == 0 else VAp[:, h, 0:128]
                        mm(pn[:, h * 128:(h + 1) * 128], va, SMp[:, h, :], True, False, ["VAd%d" % bp, "SM%d" % bp], [pnk])
                        mm(pn[:, h * 128:(h + 1) * 128], CNB[hb:hb + 64, h, :], QSp[hb:hb + 64, pp, :], False, True, ["CNB", "QS%d" % bp], [pnk])
                    pst, pstk = bk(6)
                    for h in range(4):
                        hb = (h % 2) * 64
                        va = VAp[:, h, 64:192] if hb == 0 else VAp[:, h, 0:128]
                        mm(pst[hb:hb + 64, h * 128:(h + 1) * 128], KWp[:, h, :], va, True, True, ["KW%d_%d" % (h, bp), "VAd%d" % bp], [pstk], tp=(0, hb))
                    for h in range(4):
                        hb = (h % 2) * 64
                        sl = slice(hb, hb + 64)
                        stt(CN32[sl, h, :], CN32[sl, h, :], GLA[sl, dr, t * 4 + h:t * 4 + h + 1], pst[sl, h * 128:(h + 1) * 128], OP.mult, OP.add, ["CN32", "GLA", pstk], ["CN32"])
                    act(CNB[:].rearrange("p a b -> p (a b)"), CN32[:].rearrange("p a b -> p (a b)"), AF.Copy, ["CN32"], ["CNB"])
                    pn3 = pn[:, :].rearrange("p (a two t) -> p two a t", two=2, t=128)
                    DNv = DN[:].rearrange("p (two a) t -> p two a t", two=2)
                    for par in range(2):
                        hb = par * 64
                        num = slice(hb, hb + 64)
                        den = slice(64 - hb, 128 - hb)
                        act(DNv[den, par, :, :], pn3[den, par, :, :], AF.Abs, [pnk], ["DN%d" % par])
                        ts("dve", DNv[den, par, :, :], DNv[den, par, :, :], 1.0, None, OP.max, None, ["DN%d" % par], ["DN%d" % par])
                        recip(DNv[den, par, :, :], DNv[den, par, :, :], ["DN%d" % par], ["DN%d" % par])
                        dst = HF[num, :, tsl] if dr == 0 else HT_[num, :].rearrange("p (a t) -> p a t", a=2)
                        tt("dve", dst, pn3[num, par, :, :], DNv[den, par, :, :], OP.mult, [pnk, "DN%d" % par], ["HF"] if dr == 0 else ["HTd_%d" % bp])
                    if dr == 1:
                        for pp in range(2):
                            tt("dve", HT_[:, pp * 128:(pp + 1) * 128], HT_[:, pp * 128:(pp + 1) * 128], HF[:, pp, tsl], OP.add,
                               ["HTd_%d" % bp, "HF"], ["HTd_%d" % bp])

                producer(order[0], 0)
                for i_ in range(len(order)):
                    if i_ + 1 < len(order):
                        producer(order[i_ + 1], (i_ + 1) % 2)
                    consumer(order[i_], i_ % 2)
                    if i_ >= 1:
                        gn_d(order[i_ - 1], (i_ - 1) % 2)
                gn_d(order[-1], (len(order) - 1) % 2)
        P.barrier()
        epilogue(3, extra_sig=True)
        if dbg:
            for t in range(NT):
                dma(dbg_x[l, t * 128:(t + 1) * 128, :], X[:, t, :], r=["X%d" % t], key="dbg")

    with contextlib.ExitStack() as fs:
        FG = sb("FG", [128, D], F32, fs)
        YO = [sb("YO%d" % i, [128, D], F32, fs) for i in range(2)]
        JUNK = sb("JUNKf", [128, D], BF16, fs)
        dma(FG[:], fing_d, w=["FG"])
        for t in range(2, NT):
            yo = YO[t % 2]
            yk = "YO%d" % (t % 2)
            act(JUNK[:], X[:, t, :], AF.Square, ["X%d" % t], ["JUNKf", "SS"], accum=SS[:, 0:1])
            act(SS[:, 1:2], SS[:, 0:1], AF.Ln, ["SS", "CC"], ["SS1"], bias=EPS_AP, scale=1.0 / D)
            act(SS[:, 2:3], SS[:, 1:2], AF.Exp, ["SS1"], ["SS2"], scale=-0.5)
            stt(yo[:], X[:, t, :], SS[:, 2:3], FG[:], OP.mult, OP.mult, ["X%d" % t, "SS2", "FG"], [yk])
            dma(out_d[(t - 2) * 128:(t - 1) * 128, :], yo[:], r=[yk], key="out%d" % (t % 2))
    P.emit(nc)
    st.close()
    return nc, P


_CACHE = {}


def _prep_shared(inputs):
    f = lambda a: np.ascontiguousarray(np.asarray(a, dtype=np.float32))
    w_in = f(inputs["w_in"]); b_in = f(inputs["b_in"])
    sh = {}
    sh["w_mod"] = f(inputs["w_mod"])
    b_mod = f(inputs["b_mod"])
    sh["bmodT"] = np.ascontiguousarray(b_mod.reshape(NL, 24, 128).transpose(2, 0, 1).reshape(128, NL * 24))
    sh["bgate"] = np.ascontiguousarray(np.broadcast_to(b_mod[:, None, 2048:3072], (NL, 128, D)))
    sh["normgT"] = np.ascontiguousarray(f(inputs["norm_g"]).reshape(NL, 8, 128).transpose(2, 0, 1).reshape(128, NL * 8))
    w2p = w_in[:, :, PERM].reshape(NL, 8, 128, NW // 128, 128)
    sh["w2"] = np.ascontiguousarray(w2p.transpose(0, 3, 2, 1, 4)).reshape(NL, NW // 128, 128, 8 * 128)
    b2 = b_in[:, PERM]
    nch = NW // 128
    b2T = np.zeros((128, NL, 64), np.float32)
    b2T[:, :, :nch] = b2.reshape(NL, nch, 128).transpose(2, 0, 1)
    sh["b2T"] = b2T.reshape(128, NL * 64)
    b2row = np.zeros((NL, 1, NTM), np.float32)
    for n_ in TMSEGS:
        s0, n = SEGS[n_]
        b2row[:, 0, TMOFF[n_]:TMOFF[n_] + n] = b2[:, s0:s0 + n]
    sh["b2row"] = b2row
    sh["w_out"] = f(inputs["w_out"])
    sh["fing"] = np.ascontiguousarray(np.broadcast_to(f(inputs["final_g"])[None, :], (128, D)))
    sh["lam"] = np.ascontiguousarray(np.broadcast_to(f(inputs["diff_lam"]).reshape(1, NL * 128), (128, NL * 128)))
    gcol = np.zeros((128, NL, 3), np.float32)
    for i, k in enumerate(["diff_g", "hg_g", "ml_g"]):
        g = f(inputs[k])
        gcol[:, :, i] = g[:, np.arange(128) % 64].T
    sh["gcol"] = gcol.reshape(128, NL * 3)
    sh["hglb"] = np.ascontiguousarray(np.broadcast_to(f(inputs["hg_lb"]).reshape(1, 512), (128, 512)))
    sh["sink"] = np.ascontiguousarray(np.broadcast_to(f(inputs["sw_sink"]).reshape(1, NL * 4), (128, NL * 4)))
    tabs = rope_tables()
    for k in ("Ac", "As", "Cc", "Cs"):
        sh["tab" + k] = tabs[k]
    ca = const_arrays()
    sh["cst"] = np.ascontiguousarray(np.stack([ca[n] for n in CONST_F32], 1).reshape(128, -1))
    hm = np.stack([(np.arange(128) // 64 == 0), (np.arange(128) // 64 == 1)], 1).astype(np.float32)
    sh["small"] = np.ascontiguousarray(np.concatenate([ca["segind"], ca["rowmask"], hm], 1))
    sh["wmask"] = ca["wmask"]
    return sh


def kernel(x, c, ctx, c_ctx, w_mod, b_mod, norm_g, w_in, b_in, diff_lam, diff_g, hg_lb, hg_g, sw_sink, ml_g,
           w_out, final_g, _dbg=False):
    inputs = dict(x=x, c=c, ctx=ctx, c_ctx=c_ctx, w_mod=w_mod, b_mod=b_mod, norm_g=norm_g, w_in=w_in, b_in=b_in,
                  diff_lam=diff_lam, diff_g=diff_g, hg_lb=hg_lb, hg_g=hg_g, sw_sink=sw_sink, ml_g=ml_g,
                  w_out=w_out, final_g=final_g)
    sh = _prep_shared(inputs)
    x = np.asarray(x, np.float32); ctx = np.asarray(ctx, np.float32)
    c = np.asarray(c, np.float32); c_ctx = np.asarray(c_ctx, np.float32)
    key = "dbg" if _dbg else "main"
    if key not in _CACHE:
        _CACHE[key] = build_program(dbg=_dbg)[0]
    nc = _CACHE[key]
    in_maps = []
    for b in range(8):
        m = dict(sh)
        m["xin"] = np.ascontiguousarray(np.concatenate([ctx[b], x[b]], 0))
        cT = np.concatenate([c[b].reshape(8, 128).T, c_ctx.reshape(8, 128).T], 1)
        m["cT"] = np.ascontiguousarray(cT)
        in_maps.append(m)
    res = run_bass_kernel_spmd(nc, in_maps, core_ids=list(range(8)))
    out = np.stack([np.asarray(r["out"], np.float32) for r in res.results], 0)
    if _dbg:
        return out, res.results
    return out
```

```python
import bisect
import contextlib
import math
import numpy as np
import ml_dtypes
import concourse.bass as bass
import concourse.mybir as mybir
from concourse.bass_utils import run_bass_kernel_spmd

F32 = mybir.dt.float32
BF16 = mybir.dt.bfloat16
AF = mybir.ActivationFunctionType
OP = mybir.AluOpType

D = 1024
NL = 2
T = 2304
NT = 18
EPS = 1e-6
DBG = False
import os
KSTOP = int(os.environ.get('KSTOP', '99'))
KSUB = float(os.environ.get('KSUB', '99'))
KTILES = int(os.environ.get('KTILES', '99'))
KF = os.environ.get('KF', '234')
KDIRS = int(os.environ.get('KDIRS', '2'))


class StopB(Exception):
    pass


def chk(n):
    if KSUB <= n:
        raise StopB()


class Prog:
    ENGS = ("pe", "act", "dve", "pool", "sp")

    def __init__(self):
        self.ops = []
        self.last_w = {}
        self.readers = {}
        self.dma_keys = {}
        self.last_on = {}
        self.sp_barrier = None

    def op(self, eng, fn, r=(), w=(), dma_key=None, extra=()):
        i = len(self.ops)
        deps = set()
        for k in list(r) + list(w):
            lw = self.last_w.get(k)
            if lw is not None:
                deps.add(lw)
        for k in w:
            for rd in self.readers.get(k, ()):
                deps.add(rd)
        keep = set(extra)
        rs = set(r)
        for d in deps:
            od = self.ops[d]
            if od["dma_key"] is None and dma_key is None and od["eng"] == eng:
                if eng == "pe":
                    continue
                if not (set(od["w"]) & rs):
                    continue
            keep.add(d)
        keep.discard(i)
        self.ops.append(dict(eng=eng, fn=fn, deps=keep, dma_key=dma_key, w=tuple(w)))
        for k in w:
            self.last_w[k] = i
            self.readers[k] = []
        for k in r:
            self.readers.setdefault(k, []).append(i)
        if dma_key is not None:
            self.dma_keys.setdefault(dma_key, []).append(i)
        self.last_on[eng] = i
        return i

    def barrier(self):
        lasts = [v for v in self.last_on.values()]
        for e in ("pe", "act", "dve", "pool"):
            self.op(e, lambda eng: eng.nop(), extra=lasts)
        if self.sp_barrier is not None:
            self.sp_barrier(lasts)

    def emit(self, nc):
        ops = self.ops
        n = len(ops)
        signaled = [False] * n
        for o in ops:
            for d in o["deps"]:
                if ops[d]["dma_key"] is None:
                    signaled[d] = True
        cnt = {}
        sigval = [0] * n
        for i, o in enumerate(ops):
            if o["dma_key"] is None and signaled[i]:
                cnt[o["eng"]] = cnt.get(o["eng"], 0) + 1
                sigval[i] = cnt[o["eng"]]
        self.max_counts = cnt
        ctxs = []
        sems = {}
        for e in self.ENGS:
            c = nc.semaphore("s_" + e)
            ctxs.append(c)
            sems[e] = c.__enter__()
        dsems = {}
        for k in self.dma_keys:
            c = nc.semaphore("d_" + str(k))
            ctxs.append(c)
            dsems[k] = c.__enter__()
        per_eng = {e: [] for e in self.ENGS}
        for i, o in enumerate(ops):
            per_eng[o["eng"]].append(i)
        blk = nc.Block()
        block = blk.__enter__()

        def make(e):
            def body(eng):
                waited = {}
                for i in per_eng[e]:
                    o = ops[i]
                    need = {}
                    for d in o["deps"]:
                        od = ops[d]
                        if od["dma_key"] is None:
                            key = ("e", od["eng"])
                            val = sigval[d]
                        else:
                            k = od["dma_key"]
                            key = ("d", k)
                            val = 16 * bisect.bisect_left(self.dma_keys[k], i)
                        if val > need.get(key, 0):
                            need[key] = val
                    for key, val in need.items():
                        if waited.get(key, 0) >= val:
                            continue
                        waited[key] = val
                        s = sems[key[1]] if key[0] == "e" else dsems[key[1]]
                        eng.wait_ge(s, val)
                    ins = o["fn"](eng)
                    if o["dma_key"] is not None:
                        ins.then_inc(dsems[o["dma_key"]], 16)
                    elif signaled[i]:
                        ins.then_inc(sems[e], 1)
                if e == "sp":
                    for k, lst in self.dma_keys.items():
                        eng.wait_ge(dsems[k], 16 * len(lst))
            return body

        block.tensor(make("pe"))
        block.scalar(make("act"))
        block.vector(make("dve"))
        block.gpsimd(make("pool"))
        block.sync(make("sp"))
        blk.__exit__(None, None, None)
        for c in reversed(ctxs):
            c.__exit__(None, None, None)


def _swap_idx(n, grp):
    j = np.arange(n)
    return ((j // grp) ^ 1) * grp + j % grp


def build_cols():
    perm = []
    segs = {}

    def add(name, cols):
        segs[name] = (len(perm), len(cols))
        perm.extend(list(cols))

    sw32 = _swap_idx(32, 8)
    sw64 = _swap_idx(64, 16)
    for pr in range(2):
        q = 0 + pr * 128 + np.arange(128)
        k = 256 + pr * 128 + np.arange(128)
        qs = 0 + pr * 128 + (np.arange(128) // 32) * 32 + sw32[np.arange(128) % 32]
        ks = 256 + pr * 128 + (np.arange(128) // 32) * 32 + sw32[np.arange(128) % 32]
        add("Aq%d" % pr, q); add("Aqs%d" % pr, qs); add("Ak%d" % pr, k); add("Aks%d" % pr, ks)
        add("Av%d" % pr, 512 + pr * 128 + np.arange(128))
    add("Bff", 1024 + np.arange(256)); add("Bq", 768 + np.arange(256))
    add("Bfb", 1280 + np.arange(256)); add("Bi", 1536 + np.arange(256))
    for pr in range(2):
        q = 1792 + pr * 128 + np.arange(128)
        qs = 1792 + pr * 128 + (np.arange(128) // 64) * 64 + sw64[np.arange(128) % 64]
        add("Cq%d" % pr, q); add("Cqs%d" % pr, qs)
    for kv in range(2):
        k = 2048 + kv * 64 + np.arange(128) % 64
        ks = 2048 + kv * 64 + sw64[np.arange(128) % 64]
        add("Ck%d" % kv, k); add("Cks%d" % kv, ks)
    add("Cv", 2176 + np.arange(128))
    for pr in range(2):
        add("Dq%d" % pr, 2304 + pr * 128 + np.arange(128))
    for pr in range(2):
        add("Dk%d" % pr, 2560 + pr * 128 + np.arange(128))
    add("Dkt", 2560 + np.arange(256)); add("Dvt", 2816 + np.arange(256)); add("Dg", 3072 + np.arange(16)); add("pad", 3072 + np.zeros(112, np.int64))
    for pr in range(2):
        add("Dog%d" % pr, 3088 + pr * 128 + np.arange(128))
    for c in range(8):
        add("G%d" % c, 3344 + c * 128 + np.arange(128))
    return np.array(perm), segs


PERM, SEGS = build_cols()
NW = len(PERM)
TMSEGS = ["Av0", "Av1", "Bff", "Bq", "Bfb", "Bi", "Cv", "Dkt", "Dvt", "Dg"]
TMOFF = {}
_o = 0
for _n in TMSEGS:
    TMOFF[_n] = _o
    _o += SEGS[_n][1]
NTM = 2048


def rope_tables():
    n = 2048
    row = np.repeat(np.arange(32, dtype=np.float32), 64)
    col = np.tile(np.arange(64, dtype=np.float32), 32)
    out = {}
    for nm, dim in (("A", 32), ("C", 64)):
        q = dim // 4
        half = dim // 2
        inv = (10000.0 ** (-np.arange(0, half, 2, dtype=np.float32) / np.float32(half))).astype(np.float32)
        tc = np.zeros((128, n), np.float32)
        ts = np.zeros((128, n), np.float32)
        for r in range(128):
            j = r % dim
            g = j // q
            i = j % q
            pos = row if g < 2 else col
            ang = (pos * inv[i]).astype(np.float32)
            tc[r] = np.cos(ang)
            ts[r] = np.sin(ang) * (-1.0 if g % 2 == 0 else 1.0)
        out[nm + "c"] = tc
        out[nm + "s"] = ts
    return out


def const_arrays():
    c = {}
    s = np.arange(128)[:, None]
    t = np.arange(128)[None, :]
    same = (s // 32) == (t // 32)
    c["ident"] = np.eye(128, dtype=np.float32)
    c["ones"] = np.ones((128, 128), np.float32)
    c["bd64"] = (((s // 64) == (t // 64)) / 64.0).astype(np.float32)
    c["tri32f"] = (same & (s <= t)).astype(np.float32)
    c["tri32b"] = (same & (s >= t)).astype(np.float32)
    c["rem32f"] = (same & (s > t)).astype(np.float32)
    c["rem32b"] = (same & (s < t)).astype(np.float32)
    c["tri128f"] = (s <= t).astype(np.float32)
    c["tri128b"] = (s >= t).astype(np.float32)
    c["rem128f"] = (s > t).astype(np.float32)
    c["rem128b"] = (s < t).astype(np.float32)
    c["negf"] = np.where(s <= t, 0.0, -30000.0).astype(np.float32)
    c["negb"] = np.where(s >= t, 0.0, -30000.0).astype(np.float32)
    seg = np.zeros((128, 4), np.float32)
    seg[np.arange(128), np.arange(128) // 32] = 1.0
    c["segind"] = seg
    rm = np.zeros((128, 2), np.float32)
    rm[:, 0] = ((np.arange(128) % 64) < 32)
    rm[:, 1] = ((np.arange(128) % 64) >= 32)
    c["rowmask"] = rm
    k = np.arange(128)[:, None]
    q = np.arange(512)[None, :]
    wm = np.stack([(np.abs(q - r * 128 - k) <= 128) for r in range(-1, 5)], 0).astype(np.float32)
    c["wmask"] = np.ascontiguousarray(wm.transpose(1, 0, 2)).reshape(128, 6 * 512)
    return c


CONST_F32 = ["ident", "ones", "bd64", "tri32f", "tri32b", "rem32f", "rem32b", "tri128f", "tri128b",
             "rem128f", "rem128b", "negf", "negb"]


def build_program(dbg=False):
    nc = bass.Bass("TRN2", target_bir_lowering=False)
    P = Prog()
    st = contextlib.ExitStack()

    def din(name, shape, dt=F32):
        return nc.dram_tensor(name, list(shape), dt, kind="ExternalInput").ap()

    xin = din("xin", [T, D])
    cT_d = din("cT", [128, 16])
    wmod_d = din("w_mod", [NL, D, 3 * D])
    bmodT_d = din("bmodT", [128, NL * 24])
    bgate_d = din("bgate", [NL, 128, D])
    normg_d = din("normgT", [128, NL * 8])
    w2_d = din("w2", [NL, NW // 128, 128, 8 * 128])
    b2T_d = din("b2T", [128, NL * 64])
    b2row_d = din("b2row", [NL, 1, NTM])
    wout_d = din("w_out", [NL, D, D])
    fing_d = din("fing", [128, D])
    lam_d = din("lam", [128, NL * 128])
    gcol_d = din("gcol", [128, NL * 3])
    hglb_d = din("hglb", [128, 512])
    sink_d = din("sink", [128, NL * 4])
    tab_d = {k: din("tab" + k, [128, 2048]) for k in ("Ac", "As", "Cc", "Cs")}
    cst_d = din("cst", [128, len(CONST_F32) * 128])
    small_d = din("small", [128, 8])
    wmask_d = din("wmask", [128, 6 * 512])
    out_d = nc.dram_tensor("out", [2048, D], F32, kind="ExternalOutput").ap()
    if dbg:
        dbg_mix = nc.dram_tensor("dbg_mix", [NL * 4, 128, 2 * T], BF16, kind="ExternalOutput").ap()
        dbg_x = nc.dram_tensor("dbg_x", [NL, T, D], F32, kind="ExternalOutput").ap()

    uniq = [0]

    def sb(name, shape, dt, stack=None):
        uniq[0] += 1
        return (stack or st).enter_context(nc.sbuf_tensor("%s_%d" % (name, uniq[0]), list(shape), dt))

    X = sb("X", [128, NT, D], F32)
    HT = sb("HT", [128, 8, T], BF16)
    MIX = sb("MIX", [128, 2, T], BF16)
    CST = sb("CST", [128, len(CONST_F32), 128], F32)
    SMALL = sb("SMALL", [128, 8], F32)
    IDB = sb("IDB", [128, 128], BF16)
    ONESROW = sb("ONESROW", [1, 512], BF16)
    CTs = sb("CTs", [128, 16], F32)
    SC = sb("SC", [128, 16], F32)
    BMODT = sb("BMODT", [128, NL * 24], F32)
    NORMG = sb("NORMG", [128, NL * 8], F32)
    B2T = sb("B2T", [128, NL * 64], F32)
    GCOL = sb("GCOL", [128, NL * 3], F32)
    GCOLA = sb("GCOLA", [128, NL], F32)
    LB = sb("LB", [128, 256], F32)
    OML = sb("OML", [128, 256], F32)
    SINK = sb("SINK", [128, NL * 4], F32)
    ESINK = sb("ESINK", [128, NL * 4], F32)
    NLAM = sb("NLAM", [128, NL], F32)
    MODP = sb("MODP", [128, 32], F32)
    HS = sb("HS", [128, 16], F32)
    SH = sb("SH", [128, 16], F32)
    GATEB = sb("GATEB", [128, 2, D], F32)
    B2ROW = sb("B2ROW", [1, NTM], BF16)
    STG = [sb("STG%d" % i, [128, 8, 128], F32) for i in range(2)]
    SS = sb("SS", [128, 4], F32)
    CC = sb("CC", [128, 2], F32)
    P.op("pool", lambda e: e.memset(CC[:, 0:1], EPS), w=["CC"])
    P.op("pool", lambda e: e.memset(CC[:, 1:2], 1.0), w=["CC"])
    EPS_AP = CC[:, 0:1]
    ONE_AP = CC[:, 1:2]
    BSCR = sb("BSCR", [1, 8], F32)
    P.sp_barrier = lambda lasts: P.op("sp", lambda e: e.dma_start(out=BSCR[0:1, 0:8], in_=small_d[0:1, 0:8]), w=["BSCR"], dma_key="bar", extra=lasts)
    ist = contextlib.ExitStack()
    LAMIN = sb("LAMIN", [128, NL * 128], F32, ist)
    HGLB = sb("HGLB", [128, 512], F32, ist)
    EH = sb("EH", [128, 512], F32, ist)
    LT = sb("LT", [128, 64], F32, ist)
    LS = sb("LS", [128, 4], F32, ist)

    PB = [st.enter_context(nc.psum_tensor("PB%d" % i, [128, 512], F32)) for i in range(7)]
    PBT = st.enter_context(nc.psum_tensor("PBT", [128, 1024], BF16))
    bank_ctr = [0]

    def nb():
        i = bank_ctr[0] % 5
        bank_ctr[0] += 1
        return PB[i], "PB%d" % i

    def bk(i):
        return PB[i], "PB%d" % i

    acc_ctr = [0]

    def nacc():
        i = 5 + acc_ctr[0] % 2
        acc_ctr[0] += 1
        return PB[i], "PB%d" % i

    cidx = {n: i for i, n in enumerate(CONST_F32)}

    def C(name):
        return CST[:, cidx[name], :]

    dma_ctr = [0]

    def dma(out, in_, r=(), w=(), key=None):
        if key is None:
            key = "m%d" % (dma_ctr[0] % 4)
            dma_ctr[0] += 1
        return P.op("sp", lambda e: e.dma_start(out=out, in_=in_), r=r, w=w, dma_key=key)

    def mm(out, lhsT, rhs, start, stop, r, w, tp=None):
        if tp is None:
            return P.op("pe", lambda e: e.matmul(out, lhsT=lhsT, rhs=rhs, start=start, stop=stop), r=r, w=w)
        return P.op("pe", lambda e: e.matmul(out, lhsT=lhsT, rhs=rhs, start=start, stop=stop, tile_position=tp), r=r, w=w)

    def act(out, in_, func, r, w, bias=None, scale=None, accum=None):
        kw = {}
        if bias is not None:
            kw["bias"] = bias
        if scale is not None:
            kw["scale"] = scale
        if accum is not None:
            kw["accum_out"] = accum
        return P.op("act", lambda e: e.activation(out=out, in_=in_, func=func, **kw), r=r, w=w)

    def tt(eng, out, in0, in1, op, r, w):
        return P.op(eng, lambda e: e.tensor_tensor(out=out, in0=in0, in1=in1, op=op), r=r, w=w)

    def ts(eng, out, in0, s1, s2, op0, op1, r, w):
        if op1 is None and eng == "pool" and op0 == OP.mult:
            s2, op1 = 0.0, OP.add
        if op1 is None:
            return P.op(eng, lambda e: e.tensor_scalar(out=out, in0=in0, scalar1=s1, scalar2=None, op0=op0), r=r, w=w)
        return P.op(eng, lambda e: e.tensor_scalar(out=out, in0=in0, scalar1=s1, scalar2=s2, op0=op0, op1=op1), r=r, w=w)

    def stt(out, in0, scalar, in1, op0, op1, r, w):
        return P.op("dve", lambda e: e.scalar_tensor_tensor(out=out, in0=in0, scalar=scalar, in1=in1, op0=op0, op1=op1), r=r, w=w)

    def cp(eng, out, in_, r, w):
        return P.op(eng, lambda e: e.tensor_copy(out=out, in_=in_), r=r, w=w)

    def memset(eng, ap, val, w):
        return P.op(eng, lambda e: e.memset(ap, val), w=w)

    def tss(out, in_, scalar, op, r, w):
        return P.op("dve", lambda e: e.tensor_single_scalar(out=out, in_=in_, scalar=scalar, op=op), r=r, w=w)

    def recip(out, in_, r, w):
        return P.op("dve", lambda e: e.reciprocal(out=out, in_=in_), r=r, w=w)

    for t in range(NT):
        dma(X[:, t, :], xin[t * 128:(t + 1) * 128, :], w=["X%d" % t], key="x%d" % (t % 4))
    dma(CST[:].rearrange("p a b -> p (a b)"), cst_d, w=["CST"])
    dma(SMALL[:], small_d, w=["SMALL"])
    dma(CTs[:], cT_d, w=["CTs"])
    dma(BMODT[:], bmodT_d, w=["BMODT"])
    dma(NORMG[:], normg_d, w=["NORMG"])
    dma(B2T[:], b2T_d, w=["B2T"])
    dma(LAMIN[:], lam_d, w=["LAMIN"])
    dma(GCOL[:], gcol_d, w=["GCOL"])
    dma(HGLB[:], hglb_d, w=["HGLB"])
    dma(SINK[:], sink_d, w=["SINK"])
    SEGIND = SMALL[:, 0:4]
    ROWM = SMALL[:, 4:6]
    HEADM = SMALL[:, 6:8]

    cp("dve", IDB[:], C("ident"), ["CST"], ["IDB"])
    memset("pool", ONESROW[:], 1.0, ["ONESROW"])
    act(SC[:], CTs[:], AF.Silu, ["CTs"], ["SC"])
    act(ESINK[:], SINK[:], AF.Exp, ["SINK"], ["ESINK"])
    act(EH[:], HGLB[:], AF.Exp, ["HGLB"], ["EH"])
    tt("dve", OML[:], EH[:, 0:256], EH[:, 256:512], OP.add, ["EH"], ["OML"])
    recip(OML[:], OML[:], ["OML"], ["OML"])
    tt("dve", LB[:], EH[:, 256:512], OML[:], OP.mult, ["EH", "OML"], ["LB"])
    ts("dve", OML[:], LB[:], -1.0, 1.0, OP.mult, OP.add, ["LB"], ["OML"])
    for l in range(NL):
        lam_init = 0.8 - 0.6 * math.exp(-0.3 * l)
        for j in range(2):
            tt("dve", LT[:, j * 32:(j + 1) * 32], LAMIN[:, l * 128 + j * 64:l * 128 + j * 64 + 32],
               LAMIN[:, l * 128 + j * 64 + 32:l * 128 + j * 64 + 64], OP.mult, ["LAMIN"], ["LT"])
            P.op("dve", lambda e, j=j: e.reduce_sum(out=LS[:, j:j + 1], in_=LT[:, j * 32:(j + 1) * 32], axis=mybir.AxisListType.X),
                 r=["LT"], w=["LS"])
        act(LS[:, 2:4], LS[:, 0:2], AF.Exp, ["LS"], ["LS"])
        tt("dve", LS[:, 0:1], LS[:, 3:4], LS[:, 2:3], OP.subtract, ["LS"], ["LS"])
        ts("dve", NLAM[:, l:l + 1], LS[:, 0:1], -lam_init, None, OP.add, None, ["LS"], ["NLAM"])
        ts("dve", GCOLA[:, l:l + 1], GCOL[:, l * 3:l * 3 + 1], 1.0 - lam_init, None, OP.mult, None, ["GCOL"], ["GCOLA"])

    P.barrier()
    ist.close()
    stg_ctr = [0]

    def load_w(l, seg, dst, dkey):
        s0, n = SEGS[seg]
        for c0 in range(0, n, 128):
            cn = min(128, n - c0)
            i = stg_ctr[0] % 2
            stg_ctr[0] += 1
            src = w2_d[l, (s0 + c0) // 128, :, :].rearrange("p (kc n) -> p kc n", kc=8)[:, :, 0:cn]
            dma(STG[i][:, :, 0:cn], src, w=["STG%d" % i], key="stg%d" % i)
            if i == 0:
                act(dst[:, :, c0:c0 + cn], STG[i][:, :, 0:cn], AF.Copy, ["STG%d" % i], [dkey])
            else:
                cp("dve", dst[:, :, c0:c0 + cn], STG[i][:, :, 0:cn], ["STG%d" % i], [dkey])

    def proj_fm(ps, pkey, wt, wkey, tok0, ntok):
        for kc in range(8):
            mm(ps[:, 0:ntok], wt[:, kc, :], HT[:, kc, tok0:tok0 + ntok], kc == 0, kc == 7,
               [wkey, "HT"], [pkey])

    def proj_tm(ps_ap, pkey, wt, wkey, c0, ncol, tile, s0):
        for kc in range(8):
            mm(ps_ap, HT[:, kc, tile * 128:(tile + 1) * 128], wt[:, kc, c0:c0 + ncol], kc == 0, False,
               [wkey, "HT"], [pkey])
        mm(ps_ap, ONESROW[0:1, 0:128], B2ROW[0:1, s0 + c0:s0 + c0 + ncol], False, True, ["ONESROW", "B2ROW"], [pkey])

    BLOCKS = [(0, 256, True)] + [(256 + 512 * j, 512, False) for j in range(4)]

    for l in range(NL if KSTOP >= 10 else 1):
        need_ctx = l < NL - 1
        blocks = BLOCKS if need_ctx else BLOCKS[1:]
        tiles_out = list(range(NT)) if need_ctx else list(range(2, NT))
        for hh_ in range(2):
            B2ROWF = STG[hh_][0:1, :, :].rearrange("p a b -> p (a b)")
            dma(B2ROWF[:, 0:1024], b2row_d[l, :, hh_ * 1024:(hh_ + 1) * 1024], w=["STG%d" % hh_], key="stg%d" % hh_)
            cp("dve", B2ROW[:, hh_ * 1024:(hh_ + 1) * 1024], B2ROWF, ["STG%d" % hh_], ["B2ROW"])
        dma(GATEB[:, 0, :], bgate_d[l], w=["GATEB"])
        cp("pool", GATEB[:, 1, :], GATEB[:, 0, :], ["GATEB"], ["GATEB"])

        if KSTOP < 1:
            break
        with contextlib.ExitStack() as ms:
            SCB = sb("SCB", [128, 16, 128], F32, ms)
            WM = [sb("WMs%d" % i, [128, 8, 512], F32, ms) for i in range(3)]
            for kc in range(16):
                ts("dve", SCB[:, kc, :], C("ones"), SC[:, kc:kc + 1], None, OP.mult, None, ["CST", "SC"], ["SCB"])
            sc3 = SC[:].rearrange("p (a k) -> p a k", a=2)
            pm, pmk = nacc()
            for j in range(6):
                wm = WM[j % 3]
                wmk = "WM%d" % (j % 3)
                for kh in range(2):
                    dma(wm[:, kh * 4:(kh + 1) * 4, :],
                        wmod_d[l, kh * 512:(kh + 1) * 512, j * 512:(j + 1) * 512].rearrange("(kc p) n -> p kc n", p=128),
                        w=[wmk], key="wm%d_%d" % (j % 3, kh))
                if j < 4:
                    for c4 in range(4):
                        ch = j * 4 + c4
                        for kc in range(8):
                            mm(pm[:, ch * 2:ch * 2 + 2], wm[:, kc, c4 * 128:(c4 + 1) * 128], sc3[:, :, kc], kc == 0, kc == 7,
                               [wmk, "SC"], [pmk])
                else:
                    for which in range(2):
                        pg, pgk = nb()
                        for kc in range(8):
                            mm(pg[:, :], SCB[:, which * 8 + kc, :], wm[:, kc, :], kc == 0, kc == 7, [wmk, "SCB"], [pgk])
                        tt("dve", GATEB[:, which, (j - 4) * 512:(j - 3) * 512], pg[:, :], GATEB[:, which, (j - 4) * 512:(j - 3) * 512],
                           OP.add, [pgk, "GATEB"], ["GATEB"])
            cp("dve", MODP[:], pm[:, 0:32], [pmk], ["MODP"])
            mp3 = MODP[:].rearrange("p (c a) -> p c a", a=2)
            for a in range(2):
                tt("dve", SH[:, a * 8:(a + 1) * 8], mp3[:, 0:8, a], BMODT[:, l * 24:l * 24 + 8], OP.add, ["MODP", "BMODT"], ["SH"])
                tt("dve", HS[:, a * 8:(a + 1) * 8], mp3[:, 8:16, a], BMODT[:, l * 24 + 8:l * 24 + 16], OP.add, ["MODP", "BMODT"], ["HS"])
                stt(HS[:, a * 8:(a + 1) * 8], HS[:, a * 8:(a + 1) * 8], 1.0, NORMG[:, l * 8:(l + 1) * 8], OP.add, OP.mult,
                    ["HS", "NORMG"], ["HS"])
        P.barrier()

        if KSTOP < 2:
            break
        with contextlib.ExitStack() as ns:
            XN = [sb("XN%d" % i, [128, D], BF16, ns) for i in range(2)]
            JUNK = sb("JUNK", [128, D], BF16, ns)
            SSA = sb("SSA", [128, 2 * NT], F32, ns)
            for t in range(NT):
                act(JUNK[:], X[:, t, :], AF.Square, ["X%d" % t], ["JUNK", "SSA"], accum=SSA[:, t:t + 1])
            act(SSA[:, NT:2 * NT], SSA[:, 0:NT], AF.Ln, ["SSA", "CC"], ["SSA1"], bias=EPS_AP, scale=1.0 / D)
            act(SSA[:, NT:2 * NT], SSA[:, NT:2 * NT], AF.Exp, ["SSA1"], ["SSA1"], scale=-0.5)
            for t in range(NT):
                a = 1 if t < 2 else 0
                xn = XN[t % 2]
                xk = "XN%d" % (t % 2)
                ts("dve", xn[:], X[:, t, :], SSA[:, NT + t:NT + t + 1], None, OP.mult, None, ["X%d" % t, "SSA1"], [xk])
                for kc in range(8):
                    P.op("pe", lambda e, kc=kc, xn=xn: e.transpose(out=PBT[:, kc * 128:(kc + 1) * 128], in_=xn[:, kc * 128:(kc + 1) * 128], identity=IDB[:]),
                         r=[xk, "IDB"], w=["PBT"])
                for kc in range(8):
                    eng = "dve" if kc % 2 == 0 else "pool"
                    if eng == "dve":
                        ts("dve", HT[:, kc, t * 128:(t + 1) * 128], PBT[:, kc * 128:(kc + 1) * 128], HS[:, a * 8 + kc:a * 8 + kc + 1],
                           SH[:, a * 8 + kc:a * 8 + kc + 1], OP.mult, OP.add, ["PBT", "HS", "SH"], ["HT"])
                    else:
                        act(HT[:, kc, t * 128:(t + 1) * 128], PBT[:, kc * 128:(kc + 1) * 128], AF.Identity, ["PBT", "HS", "SH"], ["HT"],
                            bias=SH[:, a * 8 + kc:a * 8 + kc + 1], scale=HS[:, a * 8 + kc:a * 8 + kc + 1])
        P.barrier()

        def epilogue(mx, extra_sig=None):
            if dbg and l == 0 and KSTOP < 10:
                dma(dbg_mix[4 + mx], MIX[:].rearrange("p a t -> p (a t)"), r=["MIX"], key="dbg")
            with contextlib.ExitStack() as es:
                WG = sb("WG", [128, 8, 256], BF16, es)
                SG = [sb("SGt%d" % i, [128, 512], BF16, es) for i in range(2)]
                WOS = sb("WOS", [128, 2, D], F32, es)
                WO = sb("WO", [128, 2, 2, D], BF16, es)
                load_w(l, "G%d" % (2 * mx), WG[:, :, 0:128], "WG")
                load_w(l, "G%d" % (2 * mx + 1), WG[:, :, 128:256], "WG")
                if extra_sig is not None:
                    WS = sb("WSg", [128, 8, 256], BF16, es)
                    load_w(l, "Dog0", WS[:, :, 0:128], "WSg")
                    load_w(l, "Dog1", WS[:, :, 128:256], "WSg")
                dma(WOS[:], wout_d[l, mx * 256:(mx + 1) * 256, :].rearrange("(pc p) n -> p pc n", p=128), w=["WOS"])
                for pc in range(2):
                    for a in range(2 if need_ctx else 1):
                        tt("dve", WO[:, pc, a, :], WOS[:, pc, :], GATEB[:, a, :], OP.mult, ["WOS", "GATEB"], ["WO"])
                for (tok0, ntok, isctx) in blocks:
                    for pc in range(2):
                        ps, pk = nb()
                        proj_fm(ps, pk, WG[:, :, pc * 128:(pc + 1) * 128], "WG", tok0, ntok)
                        sg = SG[pc]
                        gi = SEGS["G%d" % (2 * mx + pc)][0] // 128
                        act(sg[:, 0:ntok], ps[:, 0:ntok], AF.Silu, [pk, "B2T"], ["SG%d" % pc], bias=B2T[:, l * 64 + gi:l * 64 + gi + 1])
                        tt("dve", MIX[:, pc, tok0:tok0 + ntok], MIX[:, pc, tok0:tok0 + ntok], sg[:, 0:ntok], OP.mult, ["MIX", "SG%d" % pc], ["MIX"])
                        if extra_sig is not None:
                            ps2, pk2 = nb()
                            proj_fm(ps2, pk2, WS[:, :, pc * 128:(pc + 1) * 128], "WSg", tok0, ntok)
                            oi = SEGS["Dog%d" % pc][0] // 128
                            act(sg[:, 0:ntok], ps2[:, 0:ntok], AF.Sigmoid, [pk2, "B2T"], ["SG%d" % pc], bias=B2T[:, l * 64 + oi:l * 64 + oi + 1])
                            tt("dve", MIX[:, pc, tok0:tok0 + ntok], MIX[:, pc, tok0:tok0 + ntok], sg[:, 0:ntok], OP.mult, ["MIX", "SG%d" % pc], ["MIX"])
                    for tl in range(tok0 // 128, (tok0 + ntok) // 128):
                        a = 1 if isctx else 0
                        for half in range(2):
                            po, pok = nb()
                            for pc in range(2):
                                mm(po[:, :], MIX[:, pc, tl * 128:(tl + 1) * 128], WO[:, pc, a, half * 512:(half + 1) * 512], pc == 0, pc == 1,
                                   ["MIX", "WO"], [pok])
                            tt("dve", X[:, tl, half * 512:(half + 1) * 512], po[:, :], X[:, tl, half * 512:(half + 1) * 512], OP.add,
                               [pok, "X%d" % tl], ["X%d" % tl])
                if dbg:
                    dma(dbg_mix[l * 4 + mx], MIX[:].rearrange("p a t -> p (a t)"), r=["MIX"], key="dbg")
            P.barrier()

        def bias_col(seg):
            gi = SEGS[seg][0] // 128
            return B2T[:, l * 64 + gi:l * 64 + gi + 1]

        def group_norm_store(oa_ap, oakey, gcol_ap, ntok, dst_ap, scr, tagr, bank=None, split=None):
            SQ, RS = scr
            act(SQ[:, 0:ntok], oa_ap, AF.Square, [oakey], ["SQ" + tagr])
            pn, pnk = nb() if bank is None else bk(bank)
            mm(pn[:, 0:ntok], C("bd64"), SQ[:, 0:ntok], True, True, ["CST", "SQ" + tagr], [pnk])
            act(RS[:, 0:ntok], pn[:, 0:ntok], AF.Ln, [pnk], ["RS" + tagr], bias=EPS_AP, scale=1.0)
            act(RS[:, 0:ntok], RS[:, 0:ntok], AF.Exp, ["RS" + tagr], ["RS" + tagr], scale=-0.5)
            if split is None:
                stt(dst_ap, oa_ap, gcol_ap, RS[:, 0:ntok], OP.mult, OP.mult, [oakey, "RS" + tagr, "GCOL", "GCOLA"], ["MIX"])
            else:
                stt(dst_ap, oa_ap.rearrange("p (a b) -> p a b", a=split), gcol_ap, RS[:, 0:ntok].rearrange("p (a b) -> p a b", a=split),
                    OP.mult, OP.mult, [oakey, "RS" + tagr, "GCOL", "GCOLA"], ["MIX"])

        if KSTOP < 3:
            break
        for pr in range(2):
            with contextlib.ExitStack() as ws:
                WA = sb("WA", [128, 8, 512], BF16, ws)
                WV = sb("WV", [128, 8, 128], BF16, ws)
                QZ = [sb("QZ%d" % m, [128, T], BF16, ws) for m in range(2)]
                KT = sb("KTa", [128, 2, T], BF16, ws)
                VA = sb("VAa", [128, NT, 2, 192], BF16, ws)
                PT = [sb("PTa%d" % i, [128, 512], BF16, ws) for i in range(4)]
                OM = [sb("OM%d" % i, [128, 512], F32, ws) for i in range(2)]
                TB = OM
                OA = sb("OA", [128, 512], F32, ws)
                REC = sb("REC", [128, 512], F32, ws)
                SQ = sb("SQa", [128, 512], F32, ws)
                RS = sb("RSa", [128, 512], F32, ws)
                RT = [SQ, RS, OA]
                for i, sg in enumerate(["Aq%d" % pr, "Aqs%d" % pr, "Ak%d" % pr, "Aks%d" % pr]):
                    load_w(l, sg, WA[:, :, i * 128:(i + 1) * 128], "WA")
                load_w(l, "Av%d" % pr, WV[:, :, :], "WV")
                memset("pool", VA[:].rearrange("p a b c -> p (a b c)"), 1.0, ["VA"])
                for t in range(NT):
                    pv, pvk = nb()
                    proj_tm(pv[:, 0:128], pvk, WV, "WV", 0, 128, t, TMOFF["Av%d" % pr])
                    cp("dve", VA[:, t, :, 64:128], pv[:, 0:128].rearrange("p (h d) -> p h d", h=2), [pvk], ["VA"])
                for (tok0, ntok, isctx) in BLOCKS:
                    if not isctx:
                        dma(TB[0][:], tab_d["Ac"][:, tok0 - 256:tok0 - 256 + 512], w=["TB0"], key="tb0")
                        dma(TB[1][:], tab_d["As"][:, tok0 - 256:tok0 - 256 + 512], w=["TB1"], key="tb1")
                    for qk in range(2):
                        if qk == 0 and isctx and not need_ctx:
                            continue
                        ps, pk = nb()
                        proj_fm(ps, pk, WA[:, :, qk * 256:qk * 256 + 128], "WA", tok0, ntok)
                        bc = bias_col(("Aq%d" if qk == 0 else "Ak%d") % pr)
                        if isctx:
                            act(RT[2][:, 0:ntok], ps[:, 0:ntok], AF.Identity, [pk, "B2T"], ["RT2"], bias=bc)
                        else:
                            ps2, pk2 = nb()
                            proj_fm(ps2, pk2, WA[:, :, qk * 256 + 128:qk * 256 + 256], "WA", tok0, ntok)
                            bcs = bias_col(("Aqs%d" if qk == 0 else "Aks%d") % pr)
                            stt(RT[0][:, :], ps[:, :], bc, TB[0][:, :], OP.add, OP.mult, [pk, "TB0", "B2T"], ["RT0"])
                            stt(RT[1][:, :], ps2[:, :], bcs, TB[1][:, :], OP.add, OP.mult, [pk2, "TB1", "B2T"], ["RT1"])
                            tt("pool", RT[2][:, :], RT[0][:, :], RT[1][:, :], OP.add, ["RT0", "RT1"], ["RT2"])
                        if qk == 0:
                            for m in range(2):
                                if m == 0:
                                    act(QZ[m][:, tok0:tok0 + ntok], RT[2][:, 0:ntok], AF.Identity, ["RT2", "SMALL"], ["QZ%d" % m], scale=ROWM[:, m:m + 1])
                                else:
                                    ts("pool", QZ[m][:, tok0:tok0 + ntok], RT[2][:, 0:ntok], ROWM[:, m:m + 1], None, OP.mult, None,
                                       ["RT2", "SMALL"], ["QZ%d" % m])
                        else:
                            for hh_ in range(2):
                                if hh_ == 0:
                                    act(KT[:, hh_, tok0:tok0 + ntok], RT[2][:, 0:ntok], AF.Identity, ["RT2", "SMALL"], ["KTa"], scale=HEADM[:, hh_:hh_ + 1])
                                else:
                                    ts("pool", KT[:, hh_, tok0:tok0 + ntok], RT[2][:, 0:ntok], HEADM[:, hh_:hh_ + 1], None, OP.mult, None,
                                       ["RT2", "SMALL"], ["KTa"])
                P.barrier()
                items = []
                for (tok0, ntok, isctx) in blocks:
                    kts = [0, 1] if isctx else list(range(NT))
                    for hh in range(2):
                        for m in range(2):
                            pacc, pak = nacc()
                            for ki, kt in enumerate(kts):
                                items.append(dict(tok0=tok0, ntok=ntok, hh=hh, m=m, kt=kt, first=(ki == 0), last=(ki == len(kts) - 1),
                                                  pacc=pacc, pak=pak))

                rotA = [0]

                def S_a(it):
                    hb = it["hh"] * 64
                    ps, pk = bk(rotA[0] % 4)
                    rotA[0] += 1
                    it["ps"], it["pk"] = ps, pk
                    mm(ps[:, 0:it["ntok"]], KT[:, it["hh"], it["kt"] * 128:(it["kt"] + 1) * 128],
                       QZ[it["m"]][:, it["tok0"]:it["tok0"] + it["ntok"]], True, True, ["KTa", "QZ%d" % it["m"]], [pk])

                pti_a = [0]

                def EV_a(it):
                    hh, m, ntok, tok0 = it["hh"], it["m"], it["ntok"], it["tok0"]
                    hb = hh * 64
                    num = slice(hb, hb + 64)
                    den = slice(64 - hb, 128 - hb)
                    pacc, pak = it["pacc"], it["pak"]
                    pt = PT[pti_a[0] % 4]
                    ptk = "PT%d" % (pti_a[0] % 4)
                    pti_a[0] += 1
                    act(pt[:, 0:ntok], it["ps"][:, 0:ntok], AF.Exp, [it["pk"]], [ptk], scale=32 ** -0.5)
                    va = VA[:, it["kt"], hh, 64:192] if hh == 0 else VA[:, it["kt"], hh, 0:128]
                    mm(pacc[:, 0:ntok], va, pt[:, 0:ntok], it["first"], it["last"], ["VA", ptk], [pak])
                    if it["last"]:
                        recip(REC[den, 0:ntok], pacc[den, 0:ntok], [pak], ["REC"])
                        tt("dve", OM[m][num, 0:ntok], pacc[num, 0:ntok], REC[den, 0:ntok], OP.mult, [pak, "REC"], ["OM%d" % m])
                        if m == 1:
                            stt(OA[num, 0:ntok], OM[1][num, 0:ntok], NLAM[num, l:l + 1], OM[0][num, 0:ntok], OP.mult, OP.add,
                                ["OM0", "OM1", "NLAM"], ["OA"])
                            if hh == 1:
                                group_norm_store(OA[:, 0:ntok], "OA", GCOLA[:, l:l + 1], ntok, MIX[:, pr, tok0:tok0 + ntok], (SQ, RS), "a", bank=4)

                LA = 3
                for i_ in range(len(items) + LA):
                    if i_ < len(items):
                        S_a(items[i_])
                    if i_ >= LA:
                        EV_a(items[i_ - LA])
            P.barrier()
        epilogue(0)

        if KSTOP < 4:
            break
        with contextlib.ExitStack() as ws:
            WB = sb("WB", [128, 8, 1024], BF16, ws)
            M4F = sb("M4F", [128, 4, 128], BF16, ws)
            M4B = sb("M4B", [128, 4, 128], BF16, ws)
            for h in range(4):
                cp("dve", M4F[:, h, :], C("tri32f"), ["CST"], ["M4F"])
                cp("pool", M4B[:, h, :], C("tri32b"), ["CST"], ["M4B"])
            U = sb("Ub", [128, 256], F32, ws)
            FF = sb("Fb", [128, 256], F32, ws)
            LOGF = sb("LOGFb", [128, 256], F32, ws)
            KK = sb("KKb", [128, 256], F32, ws)
            VB = sb("VBb", [128, 256], BF16, ws)
            E = sb("Eb", [128, 256], F32, ws)
            EI = sb("EIb", [128, 256], F32, ws)
            ER = sb("ERb", [128, 256], F32, ws)
            G = sb("Gb", [128, 8], F32, ws)
            QE = sb("QEb", [128, 256], BF16, ws)
            KE = sb("KEb", [128, 256], BF16, ws)
            KEND = sb("KENDb", [128, 256], BF16, ws)
            KM = sb("KMb", [128, 4, 256], BF16, ws)
            QKT = sb("QKTb", [128, 4, 128], BF16, ws)
            AM = sb("AMb", [128, 4, 128], BF16, ws)
            S32S = sb("S32Sb", [128, 5, 2, 64], F32, ws)
            SBFS = sb("SBFSb", [128, 4, 2, 64], BF16, ws)
            OF = sb("OFb", [128, 2, T], BF16, ws)
            OT = sb("OTb", [128, 256], F32, ws)
            SQ = sb("SQb", [128, 512], F32, ws)
            RS = sb("RSb", [128, 512], F32, ws)
            for i, sg in enumerate(["Bff", "Bq", "Bfb", "Bi"]):
                load_w(l, sg, WB[:, :, i * 256:(i + 1) * 256], "WB")
            s0q = TMOFF["Bff"]
            try:
              chk(-1)
              for dr in range(2):
                if KSUB < 99 and dr == 1:
                    break
                tri = C("tri32f") if dr == 0 else C("tri32b")
                rem = C("rem32f") if dr == 0 else C("rem32b")
                m4 = M4F if dr == 0 else M4B
                order = list(range(NT)) if dr == 0 else [1, 0] + list(range(NT - 1, 1, -1))
                segs = [0, 1, 2, 3] if dr == 0 else [3, 2, 1, 0]
                order = order[:KTILES]
                if dr >= KDIRS:
                    break
                chk(0.1)
                memset("pool", S32S[:, 0, :, :].rearrange("p a b -> p (a b)"), 0.0, ["S32_0"])
                for t in order:
                    chk(0.2)
                    pq, pqk = bk(0)
                    proj_tm(pq[:, 0:512], pqk, WB, "WB", dr * 256, 512, t, s0q)
                    qsl = slice(256, 512) if dr == 0 else slice(0, 256)
                    zsl = slice(0, 256) if dr == 0 else slice(256, 512)
                    chk(0.3)
                    pv_, pvk_ = bk(2)
                    proj_tm(pv_[:, 0:256], pvk_, WB, "WB", 768, 256, t, s0q)
                    chk(0.4)
                    act(U[:], pq[:, zsl], AF.Exp, [pqk], ["U"], scale=-1.0)
                    chk(0.5)
                    act(U[:], U[:], AF.Ln, ["U", "CC"], ["U"], bias=ONE_AP, scale=1.0)
                    act(FF[:], U[:], AF.Exp, ["U"], ["FF"], scale=-1.0)
                    chk(0.6)
                    if l > 0:
                        tt("dve", FF[:], FF[:], OML[:], OP.mult, ["FF", "OML"], ["FF"])
                        tt("dve", FF[:], FF[:], LB[:], OP.add, ["FF", "LB"], ["FF"])
                        act(LOGF[:], FF[:], AF.Ln, ["FF"], ["LOGF"])
                    else:
                        ts("dve", LOGF[:], U[:], -1.0, None, OP.mult, None, ["U"], ["LOGF"])
                    ts("dve", KK[:], FF[:], -1.0, 1.0, OP.mult, OP.add, ["FF"], ["KK"])
                    cp("dve", VB[:], pv_[:, 0:256], [pvk_], ["VB"])
                    chk(1)
                    pc_, pck = bk(3)
                    mm(pc_[:, 0:256], tri, LOGF[:], True, True, ["CST", "LOGF"], [pck])
                    pr_, prk_ = bk(4)
                    mm(pr_[:, 0:256], rem, LOGF[:], True, True, ["CST", "LOGF"], [prk_])
                    pg, pgk = bk(1)
                    for hf in range(2):
                        mm(pg[:, hf * 4:(hf + 1) * 4], LOGF[:, hf * 128:(hf + 1) * 128], SEGIND, True, True, ["LOGF", "SMALL"], [pgk])
                    act(E[:], pc_[:, 0:256], AF.Exp, [pck], ["E"])
                    act(EI[:], pc_[:, 0:256], AF.Exp, [pck], ["EI"], scale=-1.0)
                    act(ER[:], pr_[:, 0:256], AF.Exp, [prk_], ["ER"])
                    act(G[:], pg[:, 0:8], AF.Exp, [pgk], ["G"])
                    chk(2)
                    stt(QE[:], pq[:, qsl], 0.125, E[:], OP.mult, OP.mult, [pqk, "E"], ["QE"])
                    tt("dve", KE[:], KK[:], EI[:], OP.mult, ["KK", "EI"], ["KE"])
                    tt("dve", KEND[:], KK[:], ER[:], OP.mult, ["KK", "ER"], ["KEND"])
                    for c4 in range(4):
                        ts("dve", KM[:, c4, :], KEND[:], SEGIND[:, c4:c4 + 1], None, OP.mult, None, ["KEND", "SMALL"], ["KM"])
                    for i4, (src, sk) in enumerate([(QE, "QE"), (QE, "QE"), (KE, "KE"), (KE, "KE")]):
                        hf = i4 % 2
                        P.op("pe", lambda e, i4=i4, src=src, hf=hf: e.transpose(out=PBT[:, i4 * 128:(i4 + 1) * 128], in_=src[:, hf * 128:(hf + 1) * 128], identity=IDB[:]),
                             r=[sk, "IDB"], w=["PBT"])
                    pd, pdk = bk(4)
                    for si, c4 in enumerate(segs):
                        for h in range(4):
                            hb = (h % 2) * 64
                            pp = h // 2
                            mm(pd[hb:hb + 64, si * 128 + pp * 64:si * 128 + (pp + 1) * 64], KM[:, c4, h * 64:(h + 1) * 64], VB[:, h * 64:(h + 1) * 64],
                               True, True, ["KM", "VB"], [pdk], tp=(0, hb))
                    for si, c4 in enumerate(segs):
                        for pp in range(2):
                            stt(S32S[:, si + 1, pp, :], S32S[:, si, pp, :], G[:, pp * 4 + c4:pp * 4 + c4 + 1], pd[:, si * 128 + pp * 64:si * 128 + (pp + 1) * 64],
                                OP.mult, OP.add, ["S32_%d" % si, "G", pdk], ["S32_%d" % (si + 1)])
                    act(SBFS[:].rearrange("p a b c -> p (a b c)"), S32S[:, 0:4, :, :].rearrange("p a b c -> p (a b c)"), AF.Copy,
                        ["S32_0", "S32_1", "S32_2", "S32_3"], ["SBFS"])
                    cp("pool", S32S[:, 0, :, :].rearrange("p a b -> p (a b)"), S32S[:, 4, :, :].rearrange("p a b -> p (a b)"), ["S32_4"], ["S32_0"])
                    chk(3)
                    cp("dve", QKT[:].rearrange("p a b -> p (a b)"), PBT[:, 0:512], ["PBT"], ["QKT"])
                    chk(4)
                    pa0, pak0 = bk(2)
                    pa1, pak1 = bk(3)
                    pas = [pa0, pa1]
                    paks = [pak0, pak1]
                    for h in range(4):
                        hb = (h % 2) * 64
                        pp = h // 2
                        mm(pas[h % 2][:, pp * 128:(pp + 1) * 128], QKT[hb:hb + 64, 2 + pp, :], QKT[hb:hb + 64, pp, :], True, True, ["QKT"], [paks[h % 2]])
                    for par in range(2):
                        tt("dve", AM[:, par::2, :] if False else AM[:].rearrange("p (a two) b -> p two a b", two=2)[:, par, :, :],
                           pas[par][:, 0:256].rearrange("p (a b) -> p a b", a=2), m4[:, 0:2, :], OP.mult, [paks[par], "M4F", "M4B"], ["AM"])
                    chk(5)
                    po0, pok0 = bk(5)
                    po1, pok1 = bk(6)
                    pos = [po0, po1]
                    poks = [pok0, pok1]
                    for h in range(4):
                        hb = (h % 2) * 64
                        pp = h // 2
                        po = pos[pp]
                        pok = poks[pp]
                        mm(po[hb:hb + 64, 0:128], VB[:, h * 64:(h + 1) * 64], AM[:, h, :], True, False, ["VB", "AM"],
                           [pok], tp=(0, hb))
                    chk(6)
                    for si, c4 in enumerate(segs):
                        for h in range(4):
                            hb = (h % 2) * 64
                            pp = h // 2
                            mm(pos[pp][hb:hb + 64, c4 * 32:c4 * 32 + 32], SBFS[hb:hb + 64, si, pp, :], QKT[hb:hb + 64, pp, c4 * 32:(c4 + 1) * 32],
                               False, si == 3, ["SBFS", "QKT"], [poks[pp]], tp=(hb, hb))
                    chk(7)
                    if dr == 0:
                        for pp in range(2):
                            cp("dve", OF[:, pp, t * 128:(t + 1) * 128], pos[pp][:, 0:128], [poks[pp]], ["OF"])
                        chk(8)
                    else:
                        for pp in range(2):
                            tt("dve", OT[:, pp * 128:(pp + 1) * 128], pos[pp][:, 0:128], OF[:, pp, t * 128:(t + 1) * 128], OP.add, [poks[pp], "OF"], ["OT"])
                        if need_ctx or t >= 2:
                            if "2" in KF:
                                group_norm_store(OT[:, 0:256], "OT", GCOL[:, l * 3 + 1:l * 3 + 2], 256,
                                                 MIX[:, :, t * 128:(t + 1) * 128], (SQ, RS), "b", bank=3, split=2)
                            else:
                                for pp in range(2):
                                    group_norm_store(OT[:, pp * 128:(pp + 1) * 128], "OT", GCOL[:, l * 3 + 1:l * 3 + 2], 128,
                                                     MIX[:, pp, t * 128:(t + 1) * 128], (SQ, RS), "b", bank=3)
            except StopB:
                pass
        P.barrier()
        epilogue(1)

        if KSTOP < 5:
            break
        with contextlib.ExitStack() as ws:
            QT = sb("QTc", [128, 4, T], BF16, ws)
            KT = sb("KTc", [128, 2, T], BF16, ws)
            VA = sb("VAc", [128, NT, 2, 192], BF16, ws)
            ws2 = contextlib.ExitStack()
            WC = sb("WC", [128, 8, 512], BF16, ws2)
            WV = sb("WVc", [128, 8, 128], BF16, ws2)
            TB = [sb("TBc%d" % i, [128, 512], F32, ws2) for i in range(2)]
            RT = [sb("RTc%d" % i, [128, 512], F32, ws2) for i in range(2)]
            names = ["Cq0", "Cqs0", "Cq1", "Cqs1", "Ck0", "Cks0", "Ck1", "Cks1"]
            load_w(l, "Cv", WV[:, :, :], "WVc")
            memset("pool", VA[:].rearrange("p a b c -> p (a b c)"), 1.0, ["VA"])
            for t in range(NT):
                pv, pvk = nb()
                proj_tm(pv[:, 0:128], pvk, WV, "WVc", 0, 128, t, TMOFF["Cv"])
                cp("dve", VA[:, t, :, 64:128], pv[:, 0:128].rearrange("p (h d) -> p h d", h=2), [pvk], ["VA"])
            for grp, (tok0, ntok, isctx) in [(g_, b_) for g_ in range(2) for b_ in BLOCKS]:
                if (tok0, ntok, isctx) == BLOCKS[0]:
                    for i in range(4):
                        load_w(l, names[grp * 4 + i], WC[:, :, i * 128:(i + 1) * 128], "WC")
                if not isctx:
                    dma(TB[0][:], tab_d["Cc"][:, tok0 - 256:tok0 - 256 + 512], w=["TB0"], key="tb0")
                    dma(TB[1][:], tab_d["Cs"][:, tok0 - 256:tok0 - 256 + 512], w=["TB1"], key="tb1")
                for ci in range(2 * grp, 2 * grp + 2):
                    if ci < 2 and isctx and not need_ctx:
                        continue
                    dst = RT[0][:, 0:ntok] if ci < 2 else KT[:, ci - 2, tok0:tok0 + ntok]
                    dk = "RT0" if ci < 2 else "KTc"
                    ps, pk = nb()
                    proj_fm(ps, pk, WC[:, :, (ci % 2) * 256:(ci % 2) * 256 + 128], "WC", tok0, ntok)
                    bc = bias_col(names[ci * 2])
                    if isctx:
                        act(dst, ps[:, 0:ntok], AF.Identity, [pk, "B2T"], [dk], bias=bc)
                    else:
                        ps2, pk2 = nb()
                        proj_fm(ps2, pk2, WC[:, :, (ci % 2) * 256 + 128:(ci % 2) * 256 + 256], "WC", tok0, ntok)
                        bcs = bias_col(names[ci * 2 + 1])
                        stt(RT[0][:, :], ps[:, :], bc, TB[0][:, :], OP.add, OP.mult, [pk, "TB0", "B2T"], ["RT0"])
                        stt(RT[1][:, :], ps2[:, :], bcs, TB[1][:, :], OP.add, OP.mult, [pk2, "TB1", "B2T"], ["RT1"])
                        tt("pool", dst, RT[0][:, :], RT[1][:, :], OP.add, ["RT0", "RT1"], [dk])
                    if ci < 2:
                        for half in range(2):
                            if half == 0:
                                act(QT[:, 2 * ci + half, tok0:tok0 + ntok], RT[0][:, 0:ntok], AF.Identity, ["RT0", "SMALL"], ["QTc"], scale=HEADM[:, half:half + 1])
                            else:
                                ts("pool", QT[:, 2 * ci + half, tok0:tok0 + ntok], RT[0][:, 0:ntok], HEADM[:, half:half + 1], None, OP.mult, None,
                                   ["RT0", "SMALL"], ["QTc"])
            P.barrier()
            ws2.close()
            PT = [sb("PTc%d" % i, [128, 512], BF16, ws) for i in range(4)]
            REC = sb("RECc", [128, 512], F32, ws)
            WMASK = sb("WMASK", [128, 6, 512], BF16, ws)
            for r6 in range(6):
                stg = STG[r6 % 2]
                sv = stg[:, 0:4, :].rearrange("p a b -> p (a b)")
                dma(sv, wmask_d[:, r6 * 512:(r6 + 1) * 512], w=["STG%d" % (r6 % 2)], key="stg%d" % (r6 % 2))
                cp("pool", WMASK[:, r6, :], sv, ["STG%d" % (r6 % 2)], ["WMASK"])
            items = []
            for (tok0, ntok, isctx) in blocks:
                if isctx:
                    kts = [(0, None), (1, None)]
                else:
                    J = (tok0 - 256) // 512
                    kts = [(0, None), (1, None)] + [(2 + lt, lt - 4 * J + 1) for lt in range(4 * J - 1, 4 * J + 5) if 0 <= lt < 16]
                for h in range(4):
                    pacc, pak = nacc()
                    for ki, (kt, mr) in enumerate(kts):
                        items.append(dict(tok0=tok0, ntok=ntok, h=h, kt=kt, mr=mr, first=(ki == 0), last=(ki == len(kts) - 1), pacc=pacc, pak=pak))

            def S_c(it):
                h = it["h"]
                hb = (h % 2) * 64
                ps, pk = nb()
                it["ps"], it["pk"] = ps, pk
                mm(ps[:, 0:it["ntok"]], KT[:, h // 2, it["kt"] * 128:(it["kt"] + 1) * 128],
                   QT[:, h, it["tok0"]:it["tok0"] + it["ntok"]], True, True, ["KTc", "QTc"], [pk])

            pti_c = [0]

            def EV_c(it):
                h, ntok, tok0 = it["h"], it["ntok"], it["tok0"]
                hb = (h % 2) * 64
                pp = h // 2
                kv = h // 2
                num = slice(hb, hb + 64)
                den = slice(64 - hb, 128 - hb)
                pacc, pak = it["pacc"], it["pak"]
                pt = PT[pti_c[0] % 4]
                ptk = "PT%d" % (pti_c[0] % 4)
                pti_c[0] += 1
                act(pt[:, 0:ntok], it["ps"][:, 0:ntok], AF.Exp, [it["pk"]], [ptk], scale=0.125)
                if it["mr"] is not None:
                    tt("dve", pt[:, 0:ntok], pt[:, 0:ntok], WMASK[:, it["mr"], 0:ntok], OP.mult, [ptk, "WMASK"], [ptk])
                va = VA[:, it["kt"], kv, 64:192] if hb == 0 else VA[:, it["kt"], kv, 0:128]
                mm(pacc[:, 0:ntok], va, pt[:, 0:ntok], it["first"], it["last"], ["VA", ptk], [pak])
                if it["last"]:
                    ts("dve", REC[den, 0:ntok], pacc[den, 0:ntok], ESINK[den, l * 4 + h:l * 4 + h + 1], None, OP.add, None, [pak, "ESINK"], ["REC"])
                    recip(REC[den, 0:ntok], REC[den, 0:ntok], ["REC"], ["REC"])
                    tt("dve", MIX[num, pp, tok0:tok0 + ntok], pacc[num, 0:ntok], REC[den, 0:ntok], OP.mult, [pak, "REC"], ["MIX"])

            LA = 4
            for i_ in range(len(items) + LA):
                if i_ < len(items):
                    S_c(items[i_])
                if i_ >= LA:
                    EV_c(items[i_ - LA])
        P.barrier()
        epilogue(2)

        if KSTOP < 6:
            break
        with contextlib.ExitStack() as ws:
            QKT = sb("QKTd", [128, 4, T], BF16, ws)
            ws2 = contextlib.ExitStack()
            WD = sb("WD", [128, 8, 512], BF16, ws2)
            for i, sg in enumerate(["Dq0", "Dq1", "Dk0", "Dk1"]):
                load_w(l, sg, WD[:, :, i * 128:(i + 1) * 128], "WD")
            for (tok0, ntok, isctx) in BLOCKS:
                for ci, sg in enumerate(["Dq0", "Dq1", "Dk0", "Dk1"]):
                    ps, pk = nb()
                    proj_fm(ps, pk, WD[:, :, ci * 128:(ci + 1) * 128], "WD", tok0, ntok)
                    if ci % 2 == 0:
                        act(QKT[:, ci, tok0:tok0 + ntok], ps[:, 0:ntok], AF.Identity, [pk, "B2T"], ["QKTd"], bias=bias_col(sg))
                    else:
                        ts("dve", QKT[:, ci, tok0:tok0 + ntok], ps[:, 0:ntok], bias_col(sg), None, OP.add, None, [pk, "B2T"], ["QKTd"])
            P.barrier()
            ws2.close()
            WT = sb("WTd", [128, 8, 528], BF16, ws)
            VA2 = [sb("VAd%d" % i, [128, 4, 192], BF16, ws) for i in range(2)]
            NEGB = sb("NEGBd", [128, 2, 128], BF16, ws)
            cp("dve", NEGB[:, 0, :], C("negf"), ["CST"], ["NEGB"])
            cp("dve", NEGB[:, 1, :], C("negb"), ["CST"], ["NEGB"])
            LBH = sb("LBHd", [128, 4, 128], F32, ws)
            DT = sb("DTd", [128, 4, 128], F32, ws)
            EROW = sb("EROWd", [128, 2, 128], F32, ws)
            QS2 = [sb("QSd%d" % i, [128, 2, 128], BF16, ws) for i in range(2)]
            SM2 = [sb("SMd%d" % i, [128, 4, 128], BF16, ws) for i in range(2)]
            KW2 = [sb("KWd%d" % i, [128, 4, 64], BF16, ws) for i in range(2)]
            CN32 = sb("CN32d", [128, 4, 128], F32, ws)
            CNB = sb("CNBd", [128, 4, 128], BF16, ws)
            DN = sb("DNd", [128, 4, 128], F32, ws)
            HF = sb("HFd", [128, 2, T], BF16, ws)
            HT2 = [sb("HTd%d" % i, [128, 256], F32, ws) for i in range(2)]
            SQ = sb("SQd", [128, 256], F32, ws)
            RS = sb("RSd", [128, 256], F32, ws)
            load_w(l, "Dkt", WT[:, :, 0:256], "WT")
            load_w(l, "Dvt", WT[:, :, 256:512], "WT")
            load_w(l, "Dg", WT[:, :, 512:528], "WT")
            s0k = TMOFF["Dkt"]
            for i_ in range(2):
                memset("pool", VA2[i_][:].rearrange("p a b -> p (a b)"), 1.0, ["VAd%d" % i_])
            LOGFA = sb("LOGFAd", [128, NT, 8], F32, ws)
            IGA = sb("IGAd", [128, NT, 8], F32, ws)
            BIASA = sb("BIASAd", [128, 2, NT * 4], F32, ws)
            WWA = sb("WWAd", [128, 2, NT * 4], F32, ws)
            GLA = sb("GLAd", [128, 2, NT * 4], F32, ws)
            pga, pgak = bk(1)
            for t in range(NT):
                proj_tm(pga[:, t * 16:(t + 1) * 16], pgak, WT, "WT", 512, 16, t, s0k)
            pga3 = pga[:, 0:NT * 16].rearrange("p (t g) -> p t g", g=16)
            act(LOGFA[:], pga3[:, :, 8:16], AF.Exp, [pgak], ["LOGFA"], scale=-1.0)
            act(LOGFA[:].rearrange("p t g -> p (t g)"), LOGFA[:].rearrange("p t g -> p (t g)"), AF.Ln, ["LOGFA", "CC"], ["LOGFA"], bias=ONE_AP, scale=1.0)
            ts("dve", LOGFA[:].rearrange("p t g -> p (t g)"), LOGFA[:].rearrange("p t g -> p (t g)"), -1.0, None, OP.mult, None, ["LOGFA"], ["LOGFA"])
            cp("dve", IGA[:], pga3[:, :, 0:8], [pgak], ["IGA"])
            for dr_ in range(2):
                tri_ = C("tri128f") if dr_ == 0 else C("tri128b")
                rem_ = C("rem128f") if dr_ == 0 else C("rem128b")
                pca, pcak = bk(2 + dr_)
                rhs_ = LOGFA[:, :, dr_ * 4:(dr_ + 1) * 4]
                mm(pca[:, 0:NT * 4].rearrange("p (t h) -> p t h", h=4), tri_, rhs_, True, True, ["CST", "LOGFA"], [pcak])
                mm(pca[:, NT * 4:2 * NT * 4].rearrange("p (t h) -> p t h", h=4), rem_, rhs_, True, True, ["CST", "LOGFA"], [pcak])
                mm(pca[:, 2 * NT * 4:3 * NT * 4].rearrange("p (t h) -> p t h", h=4), C("ones"), rhs_, True, True, ["CST", "LOGFA"], [pcak])
                tt("dve", BIASA[:, dr_, :].rearrange("p (t h) -> p t h", h=4), IGA[:, :, dr_ * 4:(dr_ + 1) * 4],
                   pca[:, 0:NT * 4].rearrange("p (t h) -> p t h", h=4), OP.subtract, ["IGA", pcak], ["BIASA"])
                tt("dve", WWA[:, dr_, :].rearrange("p (t h) -> p t h", h=4), IGA[:, :, dr_ * 4:(dr_ + 1) * 4],
                   pca[:, NT * 4:2 * NT * 4].rearrange("p (t h) -> p t h", h=4), OP.add, ["IGA", pcak], ["WWA"])
                act(WWA[:, dr_, :], WWA[:, dr_, :], AF.Exp, ["WWA"], ["WWA"])
                act(GLA[:, dr_, :], pca[:, 2 * NT * 4:3 * NT * 4], AF.Exp, [pcak], ["GLA"])
            for dr in range(2):
                tri = C("tri128f") if dr == 0 else C("tri128b")
                rem = C("rem128f") if dr == 0 else C("rem128b")
                neg = C("negf") if dr == 0 else C("negb")
                order = list(range(NT)) if dr == 0 else [1, 0] + list(range(NT - 1, 1, -1))
                memset("pool", CN32[:].rearrange("p a b -> p (a b)"), 0.0, ["CN32"])
                memset("pool", CNB[:].rearrange("p a b -> p (a b)"), 0.0, ["CNB"])
                def producer(t, bp):
                    tsl = slice(t * 128, (t + 1) * 128)
                    VAp, KWp, QSp, SMp = VA2[bp], KW2[bp], QS2[bp], SM2[bp]
                    pk_, pkk = bk(0)
                    proj_tm(pk_[:, 0:512], pkk, WT, "WT", 0, 512, t, s0k)
                    cp("dve", VAp[:, :, 64:128], pk_[:, 256:512].rearrange("p (h d) -> p h d", h=4), [pkk], ["VAd%d" % bp])
                    for h in range(4):
                        ts("dve", KWp[:, h, :], pk_[:, h * 64:(h + 1) * 64], WWA[:, dr, t * 4 + h:t * 4 + h + 1], 0.125, OP.mult, OP.mult,
                           [pkk, "WWA"], ["KW%d_%d" % (h, bp)])
                    pf, pfk = bk(3)
                    pe2, pe2k = bk(4)
                    for h in range(4):
                        hb = (h % 2) * 64
                        pp = h // 2
                        ts("dve", LBH[:, h, :], C("ones"), LOGFA[:, t, dr * 4 + h:dr * 4 + h + 1], None, OP.mult, None, ["CST", "LOGFA"], ["LBH%d" % h])
                        mm(pf[:, h * 128:(h + 1) * 128], LBH[:, h, :], tri, True, False, ["LBH%d" % h, "CST"], [pfk])
                        mm(pf[:, h * 128:(h + 1) * 128], IDB[:], NEGB[:, dr, :], False, True, ["IDB", "NEGB"], [pfk])
                        mm(pe2[hb:hb + 64, pp * 128:(pp + 1) * 128], LBH[:, h, 0:64], tri, True, True, ["LBH%d" % h, "CST"], [pe2k], tp=(0, hb))
                    for h in range(4):
                        act(DT[:, h, :], pf[:, h * 128:(h + 1) * 128], AF.Exp, [pfk, "BIASA"], ["DT"], bias=BIASA[:, dr, t * 4 + h:t * 4 + h + 1], scale=1.0)
                    act(EROW[:].rearrange("p a b -> p (a b)"), pe2[:, 0:256], AF.Exp, [pe2k], ["EROW"])
                    tt("dve", QSp[:], QKT[:, 0:2, tsl], EROW[:], OP.mult, ["QKTd", "EROW"], ["QS%d" % bp])
                    pkq0, pkqk0 = bk(3)
                    pkq1, pkqk1 = bk(4)
                    pkqs = [pkq0, pkq1]
                    pkqks = [pkqk0, pkqk1]
                    for h in range(4):
                        hb = (h % 2) * 64
                        pp = h // 2
                        mm(pkqs[h % 2][:, pp * 128:(pp + 1) * 128], QKT[hb:hb + 64, 2 + pp, tsl], QKT[hb:hb + 64, pp, tsl], True, True, ["QKTd"], [pkqks[h % 2]])
                    for par in range(2):
                        stt(SMp[:].rearrange("p (a two) b -> p two a b", two=2)[:, par, :, :], pkqs[par][:, 0:256].rearrange("p (a b) -> p a b", a=2), 0.125,
                            DT[:].rearrange("p (a two) b -> p two a b", two=2)[:, par, :, :], OP.mult, OP.mult, [pkqks[par], "DT"], ["SM%d" % bp])

                def gn_d(t, bp):
                    tsl = slice(t * 128, (t + 1) * 128)
                    if dr == 1 and (need_ctx or t >= 2):
                        group_norm_store(HT2[bp][:, 0:256], "HTd_%d" % bp, GCOL[:, l * 3 + 2:l * 3 + 3], 256,
                                         MIX[:, :, tsl], (SQ, RS), "d", bank=2, split=2)

                def consumer(t, bp):
                    HT_ = HT2[bp]
                    tsl = slice(t * 128, (t + 1) * 128)
                    VAp, KWp, QSp, SMp = VA2[bp], KW2[bp], QS2[bp], SM2[bp]
                    pn, pnk = bk(5 if bp == 0 else 1)
                    for h in range(4):
                        hb = (h % 2) * 64
                        pp = h // 2
                        va = VAp[:, h, 64:192] if hb == 0 else VAp[:, h, 0:128]
                        mm(pn[:, h * 128:(h + 1) * 128], va, SMp[:, h, :], True, False, ["VAd%d" % bp, "SM%d" % bp], [pnk])
                        mm(pn[:, h * 128:(h + 1) * 128], CNB[hb:hb + 64, h, :], QSp[hb:hb + 64, pp, :], False, True, ["CNB", "QS%d" % bp], [pnk])
                    pst, pstk = bk(6)
                    for h in range(4):
                        hb = (h % 2) * 64
                        va = VAp[:, h, 64:192] if hb == 0 else VAp[:, h, 0:128]
                        mm(pst[hb:hb + 64, h * 128:(h + 1) * 128], KWp[:, h, :], va, True, True, ["KW%d_%d" % (h, bp), "VAd%d" % bp], [pstk], tp=(0, hb))
                    for h in range(4):
                        hb = (h % 2) * 64
                        sl = slice(hb, hb + 64)
                        stt(CN32[sl, h, :], CN32[sl, h, :], GLA[sl, dr, t * 4 + h:t * 4 + h + 1], pst[sl, h * 128:(h + 1) * 128], OP.mult, OP.add, ["CN32", "GLA", pstk], ["CN32"])
                    act(CNB[:].rearrange("p a b -> p (a b)"), CN32[:].rearrange("p a b -> p (a b)"), AF.Copy, ["CN32"], ["CNB"])

                def fin(t, bp):
                    HT_ = HT2[bp]
                    tsl = slice(t * 128, (t + 1) * 128)
                    pn, pnk = bk(5 if bp == 0 else 1)
                    pn3 = pn[:, :].rearrange("p (a two t) -> p two a t", two=2, t=128)
                    DNv = DN[:].rearrange("p (two a) t -> p two a t", two=2)
                    for par in range(2):
                        hb = par * 64
                        num = slice(hb, hb + 64)
                        den = slice(64 - hb, 128 - hb)
                        act(DNv[den, par, :, :], pn3[den, par, :, :], AF.Abs, [pnk], ["DN%d" % par])
                        ts("dve", DNv[den, par, :, :], DNv[den, par, :, :], 1.0, None, OP.max, None, ["DN%d" % par], ["DN%d" % par])
                        recip(DNv[den, par, :, :], DNv[den, par, :, :], ["DN%d" % par], ["DN%d" % par])
                        dst = HF[num, :, tsl] if dr == 0 else HT_[num, :].rearrange("p (a t) -> p a t", a=2)
                        tt("dve", dst, pn3[num, par, :, :], DNv[den, par, :, :], OP.mult, [pnk, "DN%d" % par], ["HF"] if dr == 0 else ["HTd_%d" % bp])
                    if dr == 1:
                        for pp in range(2):
                            tt("dve", HT_[:, pp * 128:(pp + 1) * 128], HT_[:, pp * 128:(pp + 1) * 128], HF[:, pp, tsl], OP.add,
                               ["HTd_%d" % bp, "HF"], ["HTd_%d" % bp])

                producer(order[0], 0)
                producer(order[1], 1)
                for i_ in range(len(order)):
                    consumer(order[i_], i_ % 2)
                    if i_ + 2 < len(order):
                        producer(order[i_ + 2], i_ % 2)
                    fin(order[i_], i_ % 2)
                    if i_ >= 1:
                        gn_d(order[i_ - 1], (i_ - 1) % 2)
                gn_d(order[-1], (len(order) - 1) % 2)
        P.barrier()
        epilogue(3, extra_sig=True)
        if dbg:
            for t in range(NT):
                dma(dbg_x[l, t * 128:(t + 1) * 128, :], X[:, t, :], r=["X%d" % t], key="dbg")

    with contextlib.ExitStack() as fs:
        FG = sb("FG", [128, D], F32, fs)
        YO = [sb("YO%d" % i, [128, D], F32, fs) for i in range(2)]
        JUNK = sb("JUNKf", [128, D], BF16, fs)
        dma(FG[:], fing_d, w=["FG"])
        for t in range(2, NT):
            yo = YO[t % 2]
            yk = "YO%d" % (t % 2)
            act(JUNK[:], X[:, t, :], AF.Square, ["X%d" % t], ["JUNKf", "SS"], accum=SS[:, 0:1])
            act(SS[:, 1:2], SS[:, 0:1], AF.Ln, ["SS", "CC"], ["SS1"], bias=EPS_AP, scale=1.0 / D)
            act(SS[:, 2:3], SS[:, 1:2], AF.Exp, ["SS1"], ["SS2"], scale=-0.5)
            stt(yo[:], X[:, t, :], SS[:, 2:3], FG[:], OP.mult, OP.mult, ["X%d" % t, "SS2", "FG"], [yk])
            dma(out_d[(t - 2) * 128:(t - 1) * 128, :], yo[:], r=[yk], key="out%d" % (t % 2))
    P.emit(nc)
    st.close()
    return nc, P


_CACHE = {}


def _prep_shared(inputs):
    f = lambda a: np.ascontiguousarray(np.asarray(a, dtype=np.float32))
    w_in = f(inputs["w_in"]); b_in = f(inputs["b_in"])
    sh = {}
    sh["w_mod"] = f(inputs["w_mod"])
    b_mod = f(inputs["b_mod"])
    sh["bmodT"] = np.ascontiguousarray(b_mod.reshape(NL, 24, 128).transpose(2, 0, 1).reshape(128, NL * 24))
    sh["bgate"] = np.ascontiguousarray(np.broadcast_to(b_mod[:, None, 2048:3072], (NL, 128, D)))
    sh["normgT"] = np.ascontiguousarray(f(inputs["norm_g"]).reshape(NL, 8, 128).transpose(2, 0, 1).reshape(128, NL * 8))
    w2p = w_in[:, :, PERM].reshape(NL, 8, 128, NW // 128, 128)
    sh["w2"] = np.ascontiguousarray(w2p.transpose(0, 3, 2, 1, 4)).reshape(NL, NW // 128, 128, 8 * 128)
    b2 = b_in[:, PERM]
    nch = NW // 128
    b2T = np.zeros((128, NL, 64), np.float32)
    b2T[:, :, :nch] = b2.reshape(NL, nch, 128).transpose(2, 0, 1)
    sh["b2T"] = b2T.reshape(128, NL * 64)
    b2row = np.zeros((NL, 1, NTM), np.float32)
    for n_ in TMSEGS:
        s0, n = SEGS[n_]
        b2row[:, 0, TMOFF[n_]:TMOFF[n_] + n] = b2[:, s0:s0 + n]
    sh["b2row"] = b2row
    sh["w_out"] = f(inputs["w_out"])
    sh["fing"] = np.ascontiguousarray(np.broadcast_to(f(inputs["final_g"])[None, :], (128, D)))
    sh["lam"] = np.ascontiguousarray(np.broadcast_to(f(inputs["diff_lam"]).reshape(1, NL * 128), (128, NL * 128)))
    gcol = np.zeros((128, NL, 3), np.float32)
    for i, k in enumerate(["diff_g", "hg_g", "ml_g"]):
        g = f(inputs[k])
        gcol[:, :, i] = g[:, np.arange(128) % 64].T
    sh["gcol"] = gcol.reshape(128, NL * 3)
    sh["hglb"] = np.ascontiguousarray(np.broadcast_to(f(inputs["hg_lb"]).reshape(1, 512), (128, 512)))
    sh["sink"] = np.ascontiguousarray(np.broadcast_to(f(inputs["sw_sink"]).reshape(1, NL * 4), (128, NL * 4)))
    tabs = rope_tables()
    for k in ("Ac", "As", "Cc", "Cs"):
        sh["tab" + k] = tabs[k]
    ca = const_arrays()
    sh["cst"] = np.ascontiguousarray(np.stack([ca[n] for n in CONST_F32], 1).reshape(128, -1))
    hm = np.stack([(np.arange(128) // 64 == 0), (np.arange(128) // 64 == 1)], 1).astype(np.float32)
    sh["small"] = np.ascontiguousarray(np.concatenate([ca["segind"], ca["rowmask"], hm], 1))
    sh["wmask"] = ca["wmask"]
    return sh


def kernel(x, c, ctx, c_ctx, w_mod, b_mod, norm_g, w_in, b_in, diff_lam, diff_g, hg_lb, hg_g, sw_sink, ml_g,
           w_out, final_g, _dbg=False):
    inputs = dict(x=x, c=c, ctx=ctx, c_ctx=c_ctx, w_mod=w_mod, b_mod=b_mod, norm_g=norm_g, w_in=w_in, b_in=b_in,
                  diff_lam=diff_lam, diff_g=diff_g, hg_lb=hg_lb, hg_g=hg_g, sw_sink=sw_sink, ml_g=ml_g,
                  w_out=w_out, final_g=final_g)
    sh = _prep_shared(inputs)
    x = np.asarray(x, np.float32); ctx = np.asarray(ctx, np.float32)
    c = np.asarray(c, np.float32); c_ctx = np.asarray(c_ctx, np.float32)
    key = "dbg" if _dbg else "main"
    if key not in _CACHE:
        _CACHE[key] = build_program(dbg=_dbg)[0]
    nc = _CACHE[key]
    in_maps = []
    for b in range(8):
        m = dict(sh)
        m["xin"] = np.ascontiguousarray(np.concatenate([ctx[b], x[b]], 0))
        cT = np.concatenate([c[b].reshape(8, 128).T, c_ctx.reshape(8, 128).T], 1)
        m["cT"] = np.ascontiguousarray(cT)
        in_maps.append(m)
    res = run_bass_kernel_spmd(nc, in_maps, core_ids=list(range(8)))
    out = np.stack([np.asarray(r["out"], np.float32) for r in res.results], 0)
    if _dbg:
        return out, res.results
    return out
```

```python
import bisect
import contextlib
import math
import numpy as np
import ml_dtypes
import concourse.bass as bass
import concourse.mybir as mybir
from concourse.bass_utils import run_bass_kernel_spmd

F32 = mybir.dt.float32
BF16 = mybir.dt.bfloat16
AF = mybir.ActivationFunctionType
OP = mybir.AluOpType

D = 1024
NL = 2
T = 2304
NT = 18
EPS = 1e-6
DBG = False
import os
KSTOP = int(os.environ.get('KSTOP', '99'))
KSUB = float(os.environ.get('KSUB', '99'))
KTILES = int(os.environ.get('KTILES', '99'))
KF = os.environ.get('KF', '234')
KDIRS = int(os.environ.get('KDIRS', '2'))


class StopB(Exception):
    pass


def chk(n):
    if KSUB <= n:
        raise StopB()


class Prog:
    ENGS = ("pe", "act", "dve", "pool", "sp")

    def __init__(self):
        self.ops = []
        self.last_w = {}
        self.readers = {}
        self.dma_keys = {}
        self.last_on = {}
        self.sp_barrier = None

    def op(self, eng, fn, r=(), w=(), dma_key=None, extra=()):
        i = len(self.ops)
        deps = set()
        for k in list(r) + list(w):
            lw = self.last_w.get(k)
            if lw is not None:
                deps.add(lw)
        for k in w:
            for rd in self.readers.get(k, ()):
                deps.add(rd)
        keep = set(extra)
        rs = set(r)
        for d in deps:
            od = self.ops[d]
            if od["dma_key"] is None and dma_key is None and od["eng"] == eng:
                if eng == "pe":
                    continue
                if not (set(od["w"]) & rs):
                    continue
            keep.add(d)
        keep.discard(i)
        self.ops.append(dict(eng=eng, fn=fn, deps=keep, dma_key=dma_key, w=tuple(w)))
        for k in w:
            self.last_w[k] = i
            self.readers[k] = []
        for k in r:
            self.readers.setdefault(k, []).append(i)
        if dma_key is not None:
            self.dma_keys.setdefault(dma_key, []).append(i)
        self.last_on[eng] = i
        return i

    def barrier(self):
        lasts = [v for v in self.last_on.values()]
        for e in ("pe", "act", "dve", "pool"):
            self.op(e, lambda eng: eng.nop(), extra=lasts)
        if self.sp_barrier is not None:
            self.sp_barrier(lasts)

    def emit(self, nc):
        ops = self.ops
        n = len(ops)
        signaled = [False] * n
        for o in ops:
            for d in o["deps"]:
                if ops[d]["dma_key"] is None:
                    signaled[d] = True
        cnt = {}
        sigval = [0] * n
        for i, o in enumerate(ops):
            if o["dma_key"] is None and signaled[i]:
                cnt[o["eng"]] = cnt.get(o["eng"], 0) + 1
                sigval[i] = cnt[o["eng"]]
        self.max_counts = cnt
        ctxs = []
        sems = {}
        for e in self.ENGS:
            c = nc.semaphore("s_" + e)
            ctxs.append(c)
            sems[e] = c.__enter__()
        dsems = {}
        for k in self.dma_keys:
            c = nc.semaphore("d_" + str(k))
            ctxs.append(c)
            dsems[k] = c.__enter__()
        per_eng = {e: [] for e in self.ENGS}
        for i, o in enumerate(ops):
            per_eng[o["eng"]].append(i)
        blk = nc.Block()
        block = blk.__enter__()

        def make(e):
            def body(eng):
                waited = {}
                for i in per_eng[e]:
                    o = ops[i]
                    need = {}
                    for d in o["deps"]:
                        od = ops[d]
                        if od["dma_key"] is None:
                            key = ("e", od["eng"])
                            val = sigval[d]
                        else:
                            k = od["dma_key"]
                            key = ("d", k)
                            val = 16 * bisect.bisect_left(self.dma_keys[k], i)
                        if val > need.get(key, 0):
                            need[key] = val
                    for key, val in need.items():
                        if waited.get(key, 0) >= val:
                            continue
                        waited[key] = val
                        s = sems[key[1]] if key[0] == "e" else dsems[key[1]]
                        eng.wait_ge(s, val)
                    ins = o["fn"](eng)
                    if o["dma_key"] is not None:
                        ins.then_inc(dsems[o["dma_key"]], 16)
                    elif signaled[i]:
                        ins.then_inc(sems[e], 1)
                if e == "sp":
                    for k, lst in self.dma_keys.items():
                        eng.wait_ge(dsems[k], 16 * len(lst))
            return body

        block.tensor(make("pe"))
        block.scalar(make("act"))
        block.vector(make("dve"))
        block.gpsimd(make("pool"))
        block.sync(make("sp"))
        blk.__exit__(None, None, None)
        for c in reversed(ctxs):
            c.__exit__(None, None, None)


def _swap_idx(n, grp):
    j = np.arange(n)
    return ((j // grp) ^ 1) * grp + j % grp


def build_cols():
    perm = []
    segs = {}

    def add(name, cols):
        segs[name] = (len(perm), len(cols))
        perm.extend(list(cols))

    sw32 = _swap_idx(32, 8)
    sw64 = _swap_idx(64, 16)
    for pr in range(2):
        q = 0 + pr * 128 + np.arange(128)
        k = 256 + pr * 128 + np.arange(128)
        qs = 0 + pr * 128 + (np.arange(128) // 32) * 32 + sw32[np.arange(128) % 32]
        ks = 256 + pr * 128 + (np.arange(128) // 32) * 32 + sw32[np.arange(128) % 32]
        add("Aq%d" % pr, q); add("Aqs%d" % pr, qs); add("Ak%d" % pr, k); add("Aks%d" % pr, ks)
        add("Av%d" % pr, 512 + pr * 128 + np.arange(128))
    add("Bff", 1024 + np.arange(256)); add("Bq", 768 + np.arange(256))
    add("Bfb", 1280 + np.arange(256)); add("Bi", 1536 + np.arange(256))
    for pr in range(2):
        q = 1792 + pr * 128 + np.arange(128)
        qs = 1792 + pr * 128 + (np.arange(128) // 64) * 64 + sw64[np.arange(128) % 64]
        add("Cq%d" % pr, q); add("Cqs%d" % pr, qs)
    for kv in range(2):
        k = 2048 + kv * 64 + np.arange(128) % 64
        ks = 2048 + kv * 64 + sw64[np.arange(128) % 64]
        add("Ck%d" % kv, k); add("Cks%d" % kv, ks)
    add("Cv", 2176 + np.arange(128))
    for pr in range(2):
        add("Dq%d" % pr, 2304 + pr * 128 + np.arange(128))
    for pr in range(2):
        add("Dk%d" % pr, 2560 + pr * 128 + np.arange(128))
    add("Dkt", 2560 + np.arange(256)); add("Dvt", 2816 + np.arange(256)); add("Dg", 3072 + np.arange(16)); add("pad", 3072 + np.zeros(112, np.int64))
    for pr in range(2):
        add("Dog%d" % pr, 3088 + pr * 128 + np.arange(128))
    for c in range(8):
        add("G%d" % c, 3344 + c * 128 + np.arange(128))
    return np.array(perm), segs


PERM, SEGS = build_cols()
NW = len(PERM)
TMSEGS = ["Av0", "Av1", "Bff", "Bq", "Bfb", "Bi", "Cv", "Dkt", "Dvt", "Dg"]
TMOFF = {}
_o = 0
for _n in TMSEGS:
    TMOFF[_n] = _o
    _o += SEGS[_n][1]
NTM = 2048


def rope_tables():
    n = 2048
    row = np.repeat(np.arange(32, dtype=np.float32), 64)
    col = np.tile(np.arange(64, dtype=np.float32), 32)
    out = {}
    for nm, dim in (("A", 32), ("C", 64)):
        q = dim // 4
        half = dim // 2
        inv = (10000.0 ** (-np.arange(0, half, 2, dtype=np.float32) / np.float32(half))).astype(np.float32)
        tc = np.zeros((128, n), np.float32)
        ts = np.zeros((128, n), np.float32)
        for r in range(128):
            j = r % dim
            g = j // q
            i = j % q
            pos = row if g < 2 else col
            ang = (pos * inv[i]).astype(np.float32)
            tc[r] = np.cos(ang)
            ts[r] = np.sin(ang) * (-1.0 if g % 2 == 0 else 1.0)
        out[nm + "c"] = tc
        out[nm + "s"] = ts
    return out


def const_arrays():
    c = {}
    s = np.arange(128)[:, None]
    t = np.arange(128)[None, :]
    same = (s // 32) == (t // 32)
    c["ident"] = np.eye(128, dtype=np.float32)
    c["ones"] = np.ones((128, 128), np.float32)
    c["bd64"] = (((s // 64) == (t // 64)) / 64.0).astype(np.float32)
    c["tri32f"] = (same & (s <= t)).astype(np.float32)
    c["tri32b"] = (same & (s >= t)).astype(np.float32)
    c["rem32f"] = (same & (s > t)).astype(np.float32)
    c["rem32b"] = (same & (s < t)).astype(np.float32)
    c["tri128f"] = (s <= t).astype(np.float32)
    c["tri128b"] = (s >= t).astype(np.float32)
    c["rem128f"] = (s > t).astype(np.float32)
    c["rem128b"] = (s < t).astype(np.float32)
    c["negf"] = np.where(s <= t, 0.0, -30000.0).astype(np.float32)
    c["negb"] = np.where(s >= t, 0.0, -30000.0).astype(np.float32)
    seg = np.zeros((128, 4), np.float32)
    seg[np.arange(128), np.arange(128) // 32] = 1.0
    c["segind"] = seg
    rm = np.zeros((128, 2), np.float32)
    rm[:, 0] = ((np.arange(128) % 64) < 32)
    rm[:, 1] = ((np.arange(128) % 64) >= 32)
    c["rowmask"] = rm
    k = np.arange(128)[:, None]
    q = np.arange(512)[None, :]
    wm = np.stack([(np.abs(q - r * 128 - k) <= 128) for r in range(-1, 5)], 0).astype(np.float32)
    c["wmask"] = np.ascontiguousarray(wm.transpose(1, 0, 2)).reshape(128, 6 * 512)
    return c


CONST_F32 = ["ident", "ones", "bd64", "tri32f", "tri32b", "rem32f", "rem32b", "tri128f", "tri128b",
             "rem128f", "rem128b", "negf", "negb"]


def build_program(dbg=False):
    nc = bass.Bass("TRN2", target_bir_lowering=False)
    P = Prog()
    st = contextlib.ExitStack()

    def din(name, shape, dt=F32):
        return nc.dram_tensor(name, list(shape), dt, kind="ExternalInput").ap()

    xin = din("xin", [T, D])
    cT_d = din("cT", [128, 16])
    wmod_d = din("w_mod", [NL, D, 3 * D])
    bmodT_d = din("bmodT", [128, NL * 24])
    bgate_d = din("bgate", [NL, 128, D])
    normg_d = din("normgT", [128, NL * 8])
    w2_d = din("w2", [NL, NW // 128, 128, 8 * 128])
    b2T_d = din("b2T", [128, NL * 64])
    b2row_d = din("b2row", [NL, 1, NTM])
    wout_d = din("w_out", [NL, D, D])
    fing_d = din("fing", [128, D])
    lam_d = din("lam", [128, NL * 128])
    gcol_d = din("gcol", [128, NL * 3])
    hglb_d = din("hglb", [128, 512])
    sink_d = din("sink", [128, NL * 4])
    tab_d = {k: din("tab" + k, [128, 2048]) for k in ("Ac", "As", "Cc", "Cs")}
    cst_d = din("cst", [128, len(CONST_F32) * 128])
    small_d = din("small", [128, 8])
    wmask_d = din("wmask", [128, 6 * 512])
    out_d = nc.dram_tensor("out", [2048, D], F32, kind="ExternalOutput").ap()
    if dbg:
        dbg_mix = nc.dram_tensor("dbg_mix", [NL * 4, 128, 2 * T], BF16, kind="ExternalOutput").ap()
        dbg_x = nc.dram_tensor("dbg_x", [NL, T, D], F32, kind="ExternalOutput").ap()

    uniq = [0]

    def sb(name, shape, dt, stack=None):
        uniq[0] += 1
        return (stack or st).enter_context(nc.sbuf_tensor("%s_%d" % (name, uniq[0]), list(shape), dt))

    X = sb("X", [128, NT, D], F32)
    HT = sb("HT", [128, 8, T], BF16)
    MIX = sb("MIX", [128, 2, T], BF16)
    CST = sb("CST", [128, len(CONST_F32), 128], F32)
    SMALL = sb("SMALL", [128, 8], F32)
    IDB = sb("IDB", [128, 128], BF16)
    ONESROW = sb("ONESROW", [1, 512], BF16)
    CTs = sb("CTs", [128, 16], F32)
    SC = sb("SC", [128, 16], F32)
    BMODT = sb("BMODT", [128, NL * 24], F32)
    NORMG = sb("NORMG", [128, NL * 8], F32)
    B2T = sb("B2T", [128, NL * 64], F32)
    GCOL = sb("GCOL", [128, NL * 3], F32)
    GCOLA = sb("GCOLA", [128, NL], F32)
    LB = sb("LB", [128, 256], F32)
    OML = sb("OML", [128, 256], F32)
    SINK = sb("SINK", [128, NL * 4], F32)
    ESINK = sb("ESINK", [128, NL * 4], F32)
    NLAM = sb("NLAM", [128, NL], F32)
    MODP = sb("MODP", [128, 32], F32)
    HS = sb("HS", [128, 16], F32)
    SH = sb("SH", [128, 16], F32)
    GATEB = sb("GATEB", [128, 2, D], F32)
    B2ROW = sb("B2ROW", [1, NTM], BF16)
    STG = [sb("STG%d" % i, [128, 8, 128], F32) for i in range(2)]
    SS = sb("SS", [128, 4], F32)
    CC = sb("CC", [128, 2], F32)
    P.op("pool", lambda e: e.memset(CC[:, 0:1], EPS), w=["CC"])
    P.op("pool", lambda e: e.memset(CC[:, 1:2], 1.0), w=["CC"])
    EPS_AP = CC[:, 0:1]
    ONE_AP = CC[:, 1:2]
    BSCR = sb("BSCR", [1, 8], F32)
    P.sp_barrier = lambda lasts: P.op("sp", lambda e: e.dma_start(out=BSCR[0:1, 0:8], in_=small_d[0:1, 0:8]), w=["BSCR"], dma_key="bar", extra=lasts)
    ist = contextlib.ExitStack()
    LAMIN = sb("LAMIN", [128, NL * 128], F32, ist)
    HGLB = sb("HGLB", [128, 512], F32, ist)
    EH = sb("EH", [128, 512], F32, ist)
    LT = sb("LT", [128, 64], F32, ist)
    LS = sb("LS", [128, 4], F32, ist)

    PB = [st.enter_context(nc.psum_tensor("PB%d" % i, [128, 512], F32)) for i in range(7)]
    PBT = st.enter_context(nc.psum_tensor("PBT", [128, 1024], BF16))
    bank_ctr = [0]

    def nb():
        i = bank_ctr[0] % 5
        bank_ctr[0] += 1
        return PB[i], "PB%d" % i

    def bk(i):
        return PB[i], "PB%d" % i

    acc_ctr = [0]

    def nacc():
        i = 5 + acc_ctr[0] % 2
        acc_ctr[0] += 1
        return PB[i], "PB%d" % i

    cidx = {n: i for i, n in enumerate(CONST_F32)}

    def C(name):
        return CST[:, cidx[name], :]

    dma_ctr = [0]

    def dma(out, in_, r=(), w=(), key=None):
        if key is None:
            key = "m%d" % (dma_ctr[0] % 4)
            dma_ctr[0] += 1
        return P.op("sp", lambda e: e.dma_start(out=out, in_=in_), r=r, w=w, dma_key=key)

    def mm(out, lhsT, rhs, start, stop, r, w, tp=None):
        if tp is None:
            return P.op("pe", lambda e: e.matmul(out, lhsT=lhsT, rhs=rhs, start=start, stop=stop), r=r, w=w)
        return P.op("pe", lambda e: e.matmul(out, lhsT=lhsT, rhs=rhs, start=start, stop=stop, tile_position=tp), r=r, w=w)

    def act(out, in_, func, r, w, bias=None, scale=None, accum=None):
        kw = {}
        if bias is not None:
            kw["bias"] = bias
        if scale is not None:
            kw["scale"] = scale
        if accum is not None:
            kw["accum_out"] = accum
        return P.op("act", lambda e: e.activation(out=out, in_=in_, func=func, **kw), r=r, w=w)

    def tt(eng, out, in0, in1, op, r, w):
        return P.op(eng, lambda e: e.tensor_tensor(out=out, in0=in0, in1=in1, op=op), r=r, w=w)

    def ts(eng, out, in0, s1, s2, op0, op1, r, w):
        if op1 is None and eng == "pool" and op0 == OP.mult:
            s2, op1 = 0.0, OP.add
        if op1 is None:
            return P.op(eng, lambda e: e.tensor_scalar(out=out, in0=in0, scalar1=s1, scalar2=None, op0=op0), r=r, w=w)
        return P.op(eng, lambda e: e.tensor_scalar(out=out, in0=in0, scalar1=s1, scalar2=s2, op0=op0, op1=op1), r=r, w=w)

    def stt(out, in0, scalar, in1, op0, op1, r, w):
        return P.op("dve", lambda e: e.scalar_tensor_tensor(out=out, in0=in0, scalar=scalar, in1=in1, op0=op0, op1=op1), r=r, w=w)

    def cp(eng, out, in_, r, w):
        return P.op(eng, lambda e: e.tensor_copy(out=out, in_=in_), r=r, w=w)

    def memset(eng, ap, val, w):
        return P.op(eng, lambda e: e.memset(ap, val), w=w)

    def tss(out, in_, scalar, op, r, w):
        return P.op("dve", lambda e: e.tensor_single_scalar(out=out, in_=in_, scalar=scalar, op=op), r=r, w=w)

    def recip(out, in_, r, w):
        return P.op("dve", lambda e: e.reciprocal(out=out, in_=in_), r=r, w=w)

    for t in range(NT):
        dma(X[:, t, :], xin[t * 128:(t + 1) * 128, :], w=["X%d" % t], key="x%d" % (t % 4))
    dma(CST[:].rearrange("p a b -> p (a b)"), cst_d, w=["CST"])
    dma(SMALL[:], small_d, w=["SMALL"])
    dma(CTs[:], cT_d, w=["CTs"])
    dma(BMODT[:], bmodT_d, w=["BMODT"])
    dma(NORMG[:], normg_d, w=["NORMG"])
    dma(B2T[:], b2T_d, w=["B2T"])
    dma(LAMIN[:], lam_d, w=["LAMIN"])
    dma(GCOL[:], gcol_d, w=["GCOL"])
    dma(HGLB[:], hglb_d, w=["HGLB"])
    dma(SINK[:], sink_d, w=["SINK"])
    SEGIND = SMALL[:, 0:4]
    ROWM = SMALL[:, 4:6]
    HEADM = SMALL[:, 6:8]

    cp("dve", IDB[:], C("ident"), ["CST"], ["IDB"])
    memset("pool", ONESROW[:], 1.0, ["ONESROW"])
    act(SC[:], CTs[:], AF.Silu, ["CTs"], ["SC"])
    act(ESINK[:], SINK[:], AF.Exp, ["SINK"], ["ESINK"])
    act(EH[:], HGLB[:], AF.Exp, ["HGLB"], ["EH"])
    tt("dve", OML[:], EH[:, 0:256], EH[:, 256:512], OP.add, ["EH"], ["OML"])
    recip(OML[:], OML[:], ["OML"], ["OML"])
    tt("dve", LB[:], EH[:, 256:512], OML[:], OP.mult, ["EH", "OML"], ["LB"])
    ts("dve", OML[:], LB[:], -1.0, 1.0, OP.mult, OP.add, ["LB"], ["OML"])
    for l in range(NL):
        lam_init = 0.8 - 0.6 * math.exp(-0.3 * l)
        for j in range(2):
            tt("dve", LT[:, j * 32:(j + 1) * 32], LAMIN[:, l * 128 + j * 64:l * 128 + j * 64 + 32],
               LAMIN[:, l * 128 + j * 64 + 32:l * 128 + j * 64 + 64], OP.mult, ["LAMIN"], ["LT"])
            P.op("dve", lambda e, j=j: e.reduce_sum(out=LS[:, j:j + 1], in_=LT[:, j * 32:(j + 1) * 32], axis=mybir.AxisListType.X),
                 r=["LT"], w=["LS"])
        act(LS[:, 2:4], LS[:, 0:2], AF.Exp, ["LS"], ["LS"])
        tt("dve", LS[:, 0:1], LS[:, 3:4], LS[:, 2:3], OP.subtract, ["LS"], ["LS"])
        ts("dve", NLAM[:, l:l + 1], LS[:, 0:1], -lam_init, None, OP.add, None, ["LS"], ["NLAM"])
        ts("dve", GCOLA[:, l:l + 1], GCOL[:, l * 3:l * 3 + 1], 1.0 - lam_init, None, OP.mult, None, ["GCOL"], ["GCOLA"])

    P.barrier()
    ist.close()
    stg_ctr = [0]

    def load_w(l, seg, dst, dkey):
        s0, n = SEGS[seg]
        for c0 in range(0, n, 128):
            cn = min(128, n - c0)
            i = stg_ctr[0] % 2
            stg_ctr[0] += 1
            src = w2_d[l, (s0 + c0) // 128, :, :].rearrange("p (kc n) -> p kc n", kc=8)[:, :, 0:cn]
            dma(STG[i][:, :, 0:cn], src, w=["STG%d" % i], key="stg%d" % i)
            if i == 0:
                act(dst[:, :, c0:c0 + cn], STG[i][:, :, 0:cn], AF.Copy, ["STG%d" % i], [dkey])
            else:
                cp("dve", dst[:, :, c0:c0 + cn], STG[i][:, :, 0:cn], ["STG%d" % i], [dkey])

    def proj_fm(ps, pkey, wt, wkey, tok0, ntok):
        for kc in range(8):
            mm(ps[:, 0:ntok], wt[:, kc, :], HT[:, kc, tok0:tok0 + ntok], kc == 0, kc == 7,
               [wkey, "HT"], [pkey])

    def proj_tm(ps_ap, pkey, wt, wkey, c0, ncol, tile, s0):
        for kc in range(8):
            mm(ps_ap, HT[:, kc, tile * 128:(tile + 1) * 128], wt[:, kc, c0:c0 + ncol], kc == 0, False,
               [wkey, "HT"], [pkey])
        mm(ps_ap, ONESROW[0:1, 0:128], B2ROW[0:1, s0 + c0:s0 + c0 + ncol], False, True, ["ONESROW", "B2ROW"], [pkey])

    BLOCKS = [(0, 256, True)] + [(256 + 512 * j, 512, False) for j in range(4)]

    for l in range(NL if KSTOP >= 10 else 1):
        need_ctx = l < NL - 1
        blocks = BLOCKS if need_ctx else BLOCKS[1:]
        tiles_out = list(range(NT)) if need_ctx else list(range(2, NT))
        for hh_ in range(2):
            B2ROWF = STG[hh_][0:1, :, :].rearrange("p a b -> p (a b)")
            dma(B2ROWF[:, 0:1024], b2row_d[l, :, hh_ * 1024:(hh_ + 1) * 1024], w=["STG%d" % hh_], key="stg%d" % hh_)
            cp("dve", B2ROW[:, hh_ * 1024:(hh_ + 1) * 1024], B2ROWF, ["STG%d" % hh_], ["B2ROW"])
        dma(GATEB[:, 0, :], bgate_d[l], w=["GATEB"])
        cp("pool", GATEB[:, 1, :], GATEB[:, 0, :], ["GATEB"], ["GATEB"])

        if KSTOP < 1:
            break
        with contextlib.ExitStack() as ms:
            SCB = sb("SCB", [128, 16, 128], F32, ms)
            WM = [sb("WMs%d" % i, [128, 8, 512], F32, ms) for i in range(3)]
            for kc in range(16):
                ts("dve", SCB[:, kc, :], C("ones"), SC[:, kc:kc + 1], None, OP.mult, None, ["CST", "SC"], ["SCB"])
            sc3 = SC[:].rearrange("p (a k) -> p a k", a=2)
            pm, pmk = nacc()
            for j in range(6):
                wm = WM[j % 3]
                wmk = "WM%d" % (j % 3)
                for kh in range(2):
                    dma(wm[:, kh * 4:(kh + 1) * 4, :],
                        wmod_d[l, kh * 512:(kh + 1) * 512, j * 512:(j + 1) * 512].rearrange("(kc p) n -> p kc n", p=128),
                        w=[wmk], key="wm%d_%d" % (j % 3, kh))
                if j < 4:
                    for c4 in range(4):
                        ch = j * 4 + c4
                        for kc in range(8):
                            mm(pm[:, ch * 2:ch * 2 + 2], wm[:, kc, c4 * 128:(c4 + 1) * 128], sc3[:, :, kc], kc == 0, kc == 7,
                               [wmk, "SC"], [pmk])
                else:
                    for which in range(2):
                        pg, pgk = nb()
                        for kc in range(8):
                            mm(pg[:, :], SCB[:, which * 8 + kc, :], wm[:, kc, :], kc == 0, kc == 7, [wmk, "SCB"], [pgk])
                        tt("dve", GATEB[:, which, (j - 4) * 512:(j - 3) * 512], pg[:, :], GATEB[:, which, (j - 4) * 512:(j - 3) * 512],
                           OP.add, [pgk, "GATEB"], ["GATEB"])
            cp("dve", MODP[:], pm[:, 0:32], [pmk], ["MODP"])
            mp3 = MODP[:].rearrange("p (c a) -> p c a", a=2)
            for a in range(2):
                tt("dve", SH[:, a * 8:(a + 1) * 8], mp3[:, 0:8, a], BMODT[:, l * 24:l * 24 + 8], OP.add, ["MODP", "BMODT"], ["SH"])
                tt("dve", HS[:, a * 8:(a + 1) * 8], mp3[:, 8:16, a], BMODT[:, l * 24 + 8:l * 24 + 16], OP.add, ["MODP", "BMODT"], ["HS"])
                stt(HS[:, a * 8:(a + 1) * 8], HS[:, a * 8:(a + 1) * 8], 1.0, NORMG[:, l * 8:(l + 1) * 8], OP.add, OP.mult,
                    ["HS", "NORMG"], ["HS"])
        P.barrier()

        if KSTOP < 2:
            break
        with contextlib.ExitStack() as ns:
            XN = [sb("XN%d" % i, [128, D], BF16, ns) for i in range(2)]
            JUNK = sb("JUNK", [128, D], BF16, ns)
            SSA = sb("SSA", [128, 2 * NT], F32, ns)
            for t in range(NT):
                act(JUNK[:], X[:, t, :], AF.Square, ["X%d" % t], ["JUNK", "SSA"], accum=SSA[:, t:t + 1])
            act(SSA[:, NT:2 * NT], SSA[:, 0:NT], AF.Ln, ["SSA", "CC"], ["SSA1"], bias=EPS_AP, scale=1.0 / D)
            act(SSA[:, NT:2 * NT], SSA[:, NT:2 * NT], AF.Exp, ["SSA1"], ["SSA1"], scale=-0.5)
            for t in range(NT):
                a = 1 if t < 2 else 0
                xn = XN[t % 2]
                xk = "XN%d" % (t % 2)
                ts("dve", xn[:], X[:, t, :], SSA[:, NT + t:NT + t + 1], None, OP.mult, None, ["X%d" % t, "SSA1"], [xk])
                for kc in range(8):
                    P.op("pe", lambda e, kc=kc, xn=xn: e.transpose(out=PBT[:, kc * 128:(kc + 1) * 128], in_=xn[:, kc * 128:(kc + 1) * 128], identity=IDB[:]),
                         r=[xk, "IDB"], w=["PBT"])
                for kc in range(8):
                    eng = "dve" if kc % 2 == 0 else "pool"
                    if eng == "dve":
                        ts("dve", HT[:, kc, t * 128:(t + 1) * 128], PBT[:, kc * 128:(kc + 1) * 128], HS[:, a * 8 + kc:a * 8 + kc + 1],
                           SH[:, a * 8 + kc:a * 8 + kc + 1], OP.mult, OP.add, ["PBT", "HS", "SH"], ["HT"])
                    else:
                        act(HT[:, kc, t * 128:(t + 1) * 128], PBT[:, kc * 128:(kc + 1) * 128], AF.Identity, ["PBT", "HS", "SH"], ["HT"],
                            bias=SH[:, a * 8 + kc:a * 8 + kc + 1], scale=HS[:, a * 8 + kc:a * 8 + kc + 1])
        P.barrier()

        def epilogue(mx, extra_sig=None):
            if dbg and l == 0 and KSTOP < 10:
                dma(dbg_mix[4 + mx], MIX[:].rearrange("p a t -> p (a t)"), r=["MIX"], key="dbg")
            with contextlib.ExitStack() as es:
                WG = sb("WG", [128, 8, 256], BF16, es)
                SG = [sb("SGt%d" % i, [128, 512], BF16, es) for i in range(2)]
                WOS = sb("WOS", [128, 2, D], F32, es)
                WO = sb("WO", [128, 2, 2, D], BF16, es)
                load_w(l, "G%d" % (2 * mx), WG[:, :, 0:128], "WG")
                load_w(l, "G%d" % (2 * mx + 1), WG[:, :, 128:256], "WG")
                if extra_sig is not None:
                    WS = sb("WSg", [128, 8, 256], BF16, es)
                    load_w(l, "Dog0", WS[:, :, 0:128], "WSg")
                    load_w(l, "Dog1", WS[:, :, 128:256], "WSg")
                dma(WOS[:], wout_d[l, mx * 256:(mx + 1) * 256, :].rearrange("(pc p) n -> p pc n", p=128), w=["WOS"])
                for pc in range(2):
                    for a in range(2 if need_ctx else 1):
                        tt("dve", WO[:, pc, a, :], WOS[:, pc, :], GATEB[:, a, :], OP.mult, ["WOS", "GATEB"], ["WO"])
                for (tok0, ntok, isctx) in blocks:
                    for pc in range(2):
                        ps, pk = nb()
                        proj_fm(ps, pk, WG[:, :, pc * 128:(pc + 1) * 128], "WG", tok0, ntok)
                        sg = SG[pc]
                        gi = SEGS["G%d" % (2 * mx + pc)][0] // 128
                        act(sg[:, 0:ntok], ps[:, 0:ntok], AF.Silu, [pk, "B2T"], ["SG%d" % pc], bias=B2T[:, l * 64 + gi:l * 64 + gi + 1])
                        tt("dve", MIX[:, pc, tok0:tok0 + ntok], MIX[:, pc, tok0:tok0 + ntok], sg[:, 0:ntok], OP.mult, ["MIX", "SG%d" % pc], ["MIX"])
                        if extra_sig is not None:
                            ps2, pk2 = nb()
                            proj_fm(ps2, pk2, WS[:, :, pc * 128:(pc + 1) * 128], "WSg", tok0, ntok)
                            oi = SEGS["Dog%d" % pc][0] // 128
                            act(sg[:, 0:ntok], ps2[:, 0:ntok], AF.Sigmoid, [pk2, "B2T"], ["SG%d" % pc], bias=B2T[:, l * 64 + oi:l * 64 + oi + 1])
                            tt("dve", MIX[:, pc, tok0:tok0 + ntok], MIX[:, pc, tok0:tok0 + ntok], sg[:, 0:ntok], OP.mult, ["MIX", "SG%d" % pc], ["MIX"])
                    for tl in range(tok0 // 128, (tok0 + ntok) // 128):
                        a = 1 if isctx else 0
                        for half in range(2):
                            po, pok = nb()
                            for pc in range(2):
                                mm(po[:, :], MIX[:, pc, tl * 128:(tl + 1) * 128], WO[:, pc, a, half * 512:(half + 1) * 512], pc == 0, pc == 1,
                                   ["MIX", "WO"], [pok])
                            tt("dve", X[:, tl, half * 512:(half + 1) * 512], po[:, :], X[:, tl, half * 512:(half + 1) * 512], OP.add,
                               [pok, "X%d" % tl], ["X%d" % tl])
                if dbg:
                    dma(dbg_mix[l * 4 + mx], MIX[:].rearrange("p a t -> p (a t)"), r=["MIX"], key="dbg")
            P.barrier()

        def bias_col(seg):
            gi = SEGS[seg][0] // 128
            return B2T[:, l * 64 + gi:l * 64 + gi + 1]

        def group_norm_store(oa_ap, oakey, gcol_ap, ntok, dst_ap, scr, tagr, bank=None, split=None):
            SQ, RS = scr
            act(SQ[:, 0:ntok], oa_ap, AF.Square, [oakey], ["SQ" + tagr])
            pn, pnk = nb() if bank is None else bk(bank)
            mm(pn[:, 0:ntok], C("bd64"), SQ[:, 0:ntok], True, True, ["CST", "SQ" + tagr], [pnk])
            act(RS[:, 0:ntok], pn[:, 0:ntok], AF.Ln, [pnk], ["RS" + tagr], bias=EPS_AP, scale=1.0)
            act(RS[:, 0:ntok], RS[:, 0:ntok], AF.Exp, ["RS" + tagr], ["RS" + tagr], scale=-0.5)
            if split is None:
                stt(dst_ap, oa_ap, gcol_ap, RS[:, 0:ntok], OP.mult, OP.mult, [oakey, "RS" + tagr, "GCOL", "GCOLA"], ["MIX"])
            else:
                stt(dst_ap, oa_ap.rearrange("p (a b) -> p a b", a=split), gcol_ap, RS[:, 0:ntok].rearrange("p (a b) -> p a b", a=split),
                    OP.mult, OP.mult, [oakey, "RS" + tagr, "GCOL", "GCOLA"], ["MIX"])

        if KSTOP < 3:
            break
        for pr in range(2):
            with contextlib.ExitStack() as ws:
                WA = sb("WA", [128, 8, 512], BF16, ws)
                WV = sb("WV", [128, 8, 128], BF16, ws)
                QZ = [sb("QZ%d" % m, [128, T], BF16, ws) for m in range(2)]
                KT = sb("KTa", [128, 2, T], BF16, ws)
                VA = sb("VAa", [128, NT, 2, 192], BF16, ws)
                PT = [sb("PTa%d" % i, [128, 512], BF16, ws) for i in range(4)]
                OM = [sb("OM%d" % i, [128, 512], F32, ws) for i in range(2)]
                TB = OM
                OA = sb("OA", [128, 512], F32, ws)
                REC = sb("REC", [128, 512], F32, ws)
                SQ = sb("SQa", [128, 512], F32, ws)
                RS = sb("RSa", [128, 512], F32, ws)
                RT = [SQ, RS, OA]
                for i, sg in enumerate(["Aq%d" % pr, "Aqs%d" % pr, "Ak%d" % pr, "Aks%d" % pr]):
                    load_w(l, sg, WA[:, :, i * 128:(i + 1) * 128], "WA")
                load_w(l, "Av%d" % pr, WV[:, :, :], "WV")
                memset("pool", VA[:].rearrange("p a b c -> p (a b c)"), 1.0, ["VA"])
                for t in range(NT):
                    pv, pvk = nb()
                    proj_tm(pv[:, 0:128], pvk, WV, "WV", 0, 128, t, TMOFF["Av%d" % pr])
                    cp("dve", VA[:, t, :, 64:128], pv[:, 0:128].rearrange("p (h d) -> p h d", h=2), [pvk], ["VA"])
                for (tok0, ntok, isctx) in BLOCKS:
                    if not isctx:
                        dma(TB[0][:], tab_d["Ac"][:, tok0 - 256:tok0 - 256 + 512], w=["TB0"], key="tb0")
                        dma(TB[1][:], tab_d["As"][:, tok0 - 256:tok0 - 256 + 512], w=["TB1"], key="tb1")
                    for qk in range(2):
                        if qk == 0 and isctx and not need_ctx:
                            continue
                        ps, pk = nb()
                        proj_fm(ps, pk, WA[:, :, qk * 256:qk * 256 + 128], "WA", tok0, ntok)
                        bc = bias_col(("Aq%d" if qk == 0 else "Ak%d") % pr)
                        if isctx:
                            act(RT[2][:, 0:ntok], ps[:, 0:ntok], AF.Identity, [pk, "B2T"], ["RT2"], bias=bc)
                        else:
                            ps2, pk2 = nb()
                            proj_fm(ps2, pk2, WA[:, :, qk * 256 + 128:qk * 256 + 256], "WA", tok0, ntok)
                            bcs = bias_col(("Aqs%d" if qk == 0 else "Aks%d") % pr)
                            stt(RT[0][:, :], ps[:, :], bc, TB[0][:, :], OP.add, OP.mult, [pk, "TB0", "B2T"], ["RT0"])
                            stt(RT[1][:, :], ps2[:, :], bcs, TB[1][:, :], OP.add, OP.mult, [pk2, "TB1", "B2T"], ["RT1"])
                            tt("pool", RT[2][:, :], RT[0][:, :], RT[1][:, :], OP.add, ["RT0", "RT1"], ["RT2"])
                        if qk == 0:
                            for m in range(2):
                                if m == 0:
                                    act(QZ[m][:, tok0:tok0 + ntok], RT[2][:, 0:ntok], AF.Identity, ["RT2", "SMALL"], ["QZ%d" % m], scale=ROWM[:, m:m + 1])
                                else:
                                    ts("pool", QZ[m][:, tok0:tok0 + ntok], RT[2][:, 0:ntok], ROWM[:, m:m + 1], None, OP.mult, None,
                                       ["RT2", "SMALL"], ["QZ%d" % m])
                        else:
                            for hh_ in range(2):
                                if hh_ == 0:
                                    act(KT[:, hh_, tok0:tok0 + ntok], RT[2][:, 0:ntok], AF.Identity, ["RT2", "SMALL"], ["KTa"], scale=HEADM[:, hh_:hh_ + 1])
                                else:
                                    ts("pool", KT[:, hh_, tok0:tok0 + ntok], RT[2][:, 0:ntok], HEADM[:, hh_:hh_ + 1], None, OP.mult, None,
                                       ["RT2", "SMALL"], ["KTa"])
                P.barrier()
                items = []
                for (tok0, ntok, isctx) in blocks:
                    kts = [0, 1] if isctx else list(range(NT))
                    for hh in range(2):
                        for m in range(2):
                            pacc, pak = nacc()
                            for ki, kt in enumerate(kts):
                                items.append(dict(tok0=tok0, ntok=ntok, hh=hh, m=m, kt=kt, first=(ki == 0), last=(ki == len(kts) - 1),
                                                  pacc=pacc, pak=pak))

                rotA = [0]

                def S_a(it):
                    hb = it["hh"] * 64
                    ps, pk = bk(rotA[0] % 4)
                    rotA[0] += 1
                    it["ps"], it["pk"] = ps, pk
                    mm(ps[:, 0:it["ntok"]], KT[:, it["hh"], it["kt"] * 128:(it["kt"] + 1) * 128],
                       QZ[it["m"]][:, it["tok0"]:it["tok0"] + it["ntok"]], True, True, ["KTa", "QZ%d" % it["m"]], [pk])

                pti_a = [0]

                def EV_a(it):
                    hh, m, ntok, tok0 = it["hh"], it["m"], it["ntok"], it["tok0"]
                    hb = hh * 64
                    num = slice(hb, hb + 64)
                    den = slice(64 - hb, 128 - hb)
                    pacc, pak = it["pacc"], it["pak"]
                    pt = PT[pti_a[0] % 4]
                    ptk = "PT%d" % (pti_a[0] % 4)
                    pti_a[0] += 1
                    act(pt[:, 0:ntok], it["ps"][:, 0:ntok], AF.Exp, [it["pk"]], [ptk], scale=32 ** -0.5)
                    va = VA[:, it["kt"], hh, 64:192] if hh == 0 else VA[:, it["kt"], hh, 0:128]
                    mm(pacc[:, 0:ntok], va, pt[:, 0:ntok], it["first"], it["last"], ["VA", ptk], [pak])
                    if it["last"]:
                        recip(REC[den, 0:ntok], pacc[den, 0:ntok], [pak], ["REC"])
                        tt("dve", OM[m][num, 0:ntok], pacc[num, 0:ntok], REC[den, 0:ntok], OP.mult, [pak, "REC"], ["OM%d" % m])
                        if m == 1:
                            stt(OA[num, 0:ntok], OM[1][num, 0:ntok], NLAM[num, l:l + 1], OM[0][num, 0:ntok], OP.mult, OP.add,
                                ["OM0", "OM1", "NLAM"], ["OA"])
                            if hh == 1:
                                group_norm_store(OA[:, 0:ntok], "OA", GCOLA[:, l:l + 1], ntok, MIX[:, pr, tok0:tok0 + ntok], (SQ, RS), "a", bank=4)

                LA = 3
                for i_ in range(len(items) + LA):
                    if i_ < len(items):
                        S_a(items[i_])
                    if i_ >= LA:
                        EV_a(items[i_ - LA])
            P.barrier()
        epilogue(0)

        if KSTOP < 4:
            break
        with contextlib.ExitStack() as ws:
            WB = sb("WB", [128, 8, 1024], BF16, ws)
            M4F = sb("M4F", [128, 4, 128], BF16, ws)
            M4B = sb("M4B", [128, 4, 128], BF16, ws)
            for h in range(4):
                cp("dve", M4F[:, h, :], C("tri32f"), ["CST"], ["M4F"])
                cp("pool", M4B[:, h, :], C("tri32b"), ["CST"], ["M4B"])
            U = sb("Ub", [128, 256], F32, ws)
            FF = sb("Fb", [128, 256], F32, ws)
            LOGF = sb("LOGFb", [128, 256], F32, ws)
            KK = sb("KKb", [128, 256], F32, ws)
            VB = sb("VBb", [128, 256], BF16, ws)
            E = sb("Eb", [128, 256], F32, ws)
            EI = sb("EIb", [128, 256], F32, ws)
            ER = sb("ERb", [128, 256], F32, ws)
            G = sb("Gb", [128, 8], F32, ws)
            QE = sb("QEb", [128, 256], BF16, ws)
            KE = sb("KEb", [128, 256], BF16, ws)
            KEND = sb("KENDb", [128, 256], BF16, ws)
            KM = sb("KMb", [128, 4, 256], BF16, ws)
            QKT = sb("QKTb", [128, 4, 128], BF16, ws)
            AM = sb("AMb", [128, 4, 128], BF16, ws)
            S32S = sb("S32Sb", [128, 5, 2, 64], F32, ws)
            SBFS = sb("SBFSb", [128, 4, 2, 64], BF16, ws)
            OF = sb("OFb", [128, 2, T], BF16, ws)
            OT = sb("OTb", [128, 256], F32, ws)
            SQ = sb("SQb", [128, 512], F32, ws)
            RS = sb("RSb", [128, 512], F32, ws)
            for i, sg in enumerate(["Bff", "Bq", "Bfb", "Bi"]):
                load_w(l, sg, WB[:, :, i * 256:(i + 1) * 256], "WB")
            s0q = TMOFF["Bff"]
            try:
              chk(-1)
              for dr in range(2):
                if KSUB < 99 and dr == 1:
                    break
                tri = C("tri32f") if dr == 0 else C("tri32b")
                rem = C("rem32f") if dr == 0 else C("rem32b")
                m4 = M4F if dr == 0 else M4B
                order = list(range(NT)) if dr == 0 else [1, 0] + list(range(NT - 1, 1, -1))
                segs = [0, 1, 2, 3] if dr == 0 else [3, 2, 1, 0]
                order = order[:KTILES]
                if dr >= KDIRS:
                    break
                chk(0.1)
                memset("pool", S32S[:, 0, :, :].rearrange("p a b -> p (a b)"), 0.0, ["S32_0"])
                for t in order:
                    chk(0.2)
                    pq, pqk = bk(0)
                    proj_tm(pq[:, 0:512], pqk, WB, "WB", dr * 256, 512, t, s0q)
                    qsl = slice(256, 512) if dr == 0 else slice(0, 256)
                    zsl = slice(0, 256) if dr == 0 else slice(256, 512)
                    chk(0.3)
                    pv_, pvk_ = bk(2)
                    proj_tm(pv_[:, 0:256], pvk_, WB, "WB", 768, 256, t, s0q)
                    chk(0.4)
                    act(U[:], pq[:, zsl], AF.Exp, [pqk], ["U"], scale=-1.0)
                    chk(0.5)
                    act(U[:], U[:], AF.Ln, ["U", "CC"], ["U"], bias=ONE_AP, scale=1.0)
                    act(FF[:], U[:], AF.Exp, ["U"], ["FF"], scale=-1.0)
                    chk(0.6)
                    if l > 0:
                        tt("dve", FF[:], FF[:], OML[:], OP.mult, ["FF", "OML"], ["FF"])
                        tt("dve", FF[:], FF[:], LB[:], OP.add, ["FF", "LB"], ["FF"])
                        act(LOGF[:], FF[:], AF.Ln, ["FF"], ["LOGF"])
                    else:
                        ts("dve", LOGF[:], U[:], -1.0, None, OP.mult, None, ["U"], ["LOGF"])
                    ts("dve", KK[:], FF[:], -1.0, 1.0, OP.mult, OP.add, ["FF"], ["KK"])
                    cp("dve", VB[:], pv_[:, 0:256], [pvk_], ["VB"])
                    chk(1)
                    pc_, pck = bk(3)
                    mm(pc_[:, 0:256], tri, LOGF[:], True, True, ["CST", "LOGF"], [pck])
                    pr_, prk_ = bk(4)
                    mm(pr_[:, 0:256], rem, LOGF[:], True, True, ["CST", "LOGF"], [prk_])
                    pg, pgk = bk(1)
                    for hf in range(2):
                        mm(pg[:, hf * 4:(hf + 1) * 4], LOGF[:, hf * 128:(hf + 1) * 128], SEGIND, True, True, ["LOGF", "SMALL"], [pgk])
                    act(E[:], pc_[:, 0:256], AF.Exp, [pck], ["E"])
                    act(EI[:], pc_[:, 0:256], AF.Exp, [pck], ["EI"], scale=-1.0)
                    act(ER[:], pr_[:, 0:256], AF.Exp, [prk_], ["ER"])
                    act(G[:], pg[:, 0:8], AF.Exp, [pgk], ["G"])
                    chk(2)
                    stt(QE[:], pq[:, qsl], 0.125, E[:], OP.mult, OP.mult, [pqk, "E"], ["QE"])
                    tt("dve", KE[:], KK[:], EI[:], OP.mult, ["KK", "EI"], ["KE"])
                    tt("dve", KEND[:], KK[:], ER[:], OP.mult, ["KK", "ER"], ["KEND"])
                    for c4 in range(4):
                        ts("dve", KM[:, c4, :], KEND[:], SEGIND[:, c4:c4 + 1], None, OP.mult, None, ["KEND", "SMALL"], ["KM"])
                    for i4, (src, sk) in enumerate([(QE, "QE"), (QE, "QE"), (KE, "KE"), (KE, "KE")]):
                        hf = i4 % 2
                        P.op("pe", lambda e, i4=i4, src=src, hf=hf: e.transpose(out=PBT[:, i4 * 128:(i4 + 1) * 128], in_=src[:, hf * 128:(hf + 1) * 128], identity=IDB[:]),
                             r=[sk, "IDB"], w=["PBT"])
                    pd, pdk = bk(4)
                    for si, c4 in enumerate(segs):
                        for h in range(4):
                            hb = (h % 2) * 64
                            pp = h // 2
                            mm(pd[hb:hb + 64, si * 128 + pp * 64:si * 128 + (pp + 1) * 64], KM[:, c4, h * 64:(h + 1) * 64], VB[:, h * 64:(h + 1) * 64],
                               True, True, ["KM", "VB"], [pdk], tp=(0, hb))
                    for si, c4 in enumerate(segs):
                        for pp in range(2):
                            stt(S32S[:, si + 1, pp, :], S32S[:, si, pp, :], G[:, pp * 4 + c4:pp * 4 + c4 + 1], pd[:, si * 128 + pp * 64:si * 128 + (pp + 1) * 64],
                                OP.mult, OP.add, ["S32_%d" % si, "G", pdk], ["S32_%d" % (si + 1)])
                    act(SBFS[:].rearrange("p a b c -> p (a b c)"), S32S[:, 0:4, :, :].rearrange("p a b c -> p (a b c)"), AF.Copy,
                        ["S32_0", "S32_1", "S32_2", "S32_3"], ["SBFS"])
                    cp("pool", S32S[:, 0, :, :].rearrange("p a b -> p (a b)"), S32S[:, 4, :, :].rearrange("p a b -> p (a b)"), ["S32_4"], ["S32_0"])
                    chk(3)
                    cp("dve", QKT[:].rearrange("p a b -> p (a b)"), PBT[:, 0:512], ["PBT"], ["QKT"])
                    chk(4)
                    pa0, pak0 = bk(2)
                    pa1, pak1 = bk(3)
                    pas = [pa0, pa1]
                    paks = [pak0, pak1]
                    for h in range(4):
                        hb = (h % 2) * 64
                        pp = h // 2
                        mm(pas[h % 2][:, pp * 128:(pp + 1) * 128], QKT[hb:hb + 64, 2 + pp, :], QKT[hb:hb + 64, pp, :], True, True, ["QKT"], [paks[h % 2]])
                    for par in range(2):
                        tt("dve", AM[:, par::2, :] if False else AM[:].rearrange("p (a two) b -> p two a b", two=2)[:, par, :, :],
                           pas[par][:, 0:256].rearrange("p (a b) -> p a b", a=2), m4[:, 0:2, :], OP.mult, [paks[par], "M4F", "M4B"], ["AM"])
                    chk(5)
                    po0, pok0 = bk(5)
                    po1, pok1 = bk(6)
                    pos = [po0, po1]
                    poks = [pok0, pok1]
                    for h in range(4):
                        hb = (h % 2) * 64
                        pp = h // 2
                        po = pos[pp]
                        pok = poks[pp]
                        mm(po[hb:hb + 64, 0:128], VB[:, h * 64:(h + 1) * 64], AM[:, h, :], True, False, ["VB", "AM"],
                           [pok], tp=(0, hb))
                    chk(6)
                    for si, c4 in enumerate(segs):
                        for h in range(4):
                            hb = (h % 2) * 64
                            pp = h // 2
                            mm(pos[pp][hb:hb + 64, c4 * 32:c4 * 32 + 32], SBFS[hb:hb + 64, si, pp, :], QKT[hb:hb + 64, pp, c4 * 32:(c4 + 1) * 32],
                               False, si == 3, ["SBFS", "QKT"], [poks[pp]], tp=(hb, hb))
                    chk(7)
                    if dr == 0:
                        for pp in range(2):
                            cp("dve", OF[:, pp, t * 128:(t + 1) * 128], pos[pp][:, 0:128], [poks[pp]], ["OF"])
                        chk(8)
                    else:
                        for pp in range(2):
                            tt("dve", OT[:, pp * 128:(pp + 1) * 128], pos[pp][:, 0:128], OF[:, pp, t * 128:(t + 1) * 128], OP.add, [poks[pp], "OF"], ["OT"])
                        if need_ctx or t >= 2:
                            if "2" in KF:
                                group_norm_store(OT[:, 0:256], "OT", GCOL[:, l * 3 + 1:l * 3 + 2], 256,
                                                 MIX[:, :, t * 128:(t + 1) * 128], (SQ, RS), "b", bank=3, split=2)
                            else:
                                for pp in range(2):
                                    group_norm_store(OT[:, pp * 128:(pp + 1) * 128], "OT", GCOL[:, l * 3 + 1:l * 3 + 2], 128,
                                                     MIX[:, pp, t * 128:(t + 1) * 128], (SQ, RS), "b", bank=3)
            except StopB:
                pass
        P.barrier()
        epilogue(1)

        if KSTOP < 5:
            break
        with contextlib.ExitStack() as ws:
            QT = sb("QTc", [128, 4, T], BF16, ws)
            KT = sb("KTc", [128, 2, T], BF16, ws)
            VA = sb("VAc", [128, NT, 2, 192], BF16, ws)
            ws2 = contextlib.ExitStack()
            WC = sb("WC", [128, 8, 512], BF16, ws2)
            WV = sb("WVc", [128, 8, 128], BF16, ws2)
            TB = [sb("TBc%d" % i, [128, 512], F32, ws2) for i in range(2)]
            RT = [sb("RTc%d" % i, [128, 512], F32, ws2) for i in range(2)]
            names = ["Cq0", "Cqs0", "Cq1", "Cqs1", "Ck0", "Cks0", "Ck1", "Cks1"]
            load_w(l, "Cv", WV[:, :, :], "WVc")
            memset("pool", VA[:].rearrange("p a b c -> p (a b c)"), 1.0, ["VA"])
            for t in range(NT):
                pv, pvk = nb()
                proj_tm(pv[:, 0:128], pvk, WV, "WVc", 0, 128, t, TMOFF["Cv"])
                cp("dve", VA[:, t, :, 64:128], pv[:, 0:128].rearrange("p (h d) -> p h d", h=2), [pvk], ["VA"])
            for grp, (tok0, ntok, isctx) in [(g_, b_) for g_ in range(2) for b_ in BLOCKS]:
                if (tok0, ntok, isctx) == BLOCKS[0]:
                    for i in range(4):
                        load_w(l, names[grp * 4 + i], WC[:, :, i * 128:(i + 1) * 128], "WC")
                if not isctx:
                    dma(TB[0][:], tab_d["Cc"][:, tok0 - 256:tok0 - 256 + 512], w=["TB0"], key="tb0")
                    dma(TB[1][:], tab_d["Cs"][:, tok0 - 256:tok0 - 256 + 512], w=["TB1"], key="tb1")
                for ci in range(2 * grp, 2 * grp + 2):
                    if ci < 2 and isctx and not need_ctx:
                        continue
                    dst = RT[0][:, 0:ntok] if ci < 2 else KT[:, ci - 2, tok0:tok0 + ntok]
                    dk = "RT0" if ci < 2 else "KTc"
                    ps, pk = nb()
                    proj_fm(ps, pk, WC[:, :, (ci % 2) * 256:(ci % 2) * 256 + 128], "WC", tok0, ntok)
                    bc = bias_col(names[ci * 2])
                    if isctx:
                        act(dst, ps[:, 0:ntok], AF.Identity, [pk, "B2T"], [dk], bias=bc)
                    else:
                        ps2, pk2 = nb()
                        proj_fm(ps2, pk2, WC[:, :, (ci % 2) * 256 + 128:(ci % 2) * 256 + 256], "WC", tok0, ntok)
                        bcs = bias_col(names[ci * 2 + 1])
                        stt(RT[0][:, :], ps[:, :], bc, TB[0][:, :], OP.add, OP.mult, [pk, "TB0", "B2T"], ["RT0"])
                        stt(RT[1][:, :], ps2[:, :], bcs, TB[1][:, :], OP.add, OP.mult, [pk2, "TB1", "B2T"], ["RT1"])
                        tt("pool", dst, RT[0][:, :], RT[1][:, :], OP.add, ["RT0", "RT1"], [dk])
                    if ci < 2:
                        for half in range(2):
                            if half == 0:
                                act(QT[:, 2 * ci + half, tok0:tok0 + ntok], RT[0][:, 0:ntok], AF.Identity, ["RT0", "SMALL"], ["QTc"], scale=HEADM[:, half:half + 1])
                            else:
                                ts("pool", QT[:, 2 * ci + half, tok0:tok0 + ntok], RT[0][:, 0:ntok], HEADM[:, half:half + 1], None, OP.mult, None,
                                   ["RT0", "SMALL"], ["QTc"])
            P.barrier()
            ws2.close()
            PT = [sb("PTc%d" % i, [128, 512], BF16, ws) for i in range(4)]
            REC = sb("RECc", [128, 512], F32, ws)
            WMASK = sb("WMASK", [128, 6, 512], BF16, ws)
            for r6 in range(6):
                stg = STG[r6 % 2]
                sv = stg[:, 0:4, :].rearrange("p a b -> p (a b)")
                dma(sv, wmask_d[:, r6 * 512:(r6 + 1) * 512], w=["STG%d" % (r6 % 2)], key="stg%d" % (r6 % 2))
                cp("pool", WMASK[:, r6, :], sv, ["STG%d" % (r6 % 2)], ["WMASK"])
            items = []
            for (tok0, ntok, isctx) in blocks:
                if isctx:
                    kts = [(0, None), (1, None)]
                else:
                    J = (tok0 - 256) // 512
                    kts = [(0, None), (1, None)] + [(2 + lt, lt - 4 * J + 1) for lt in range(4 * J - 1, 4 * J + 5) if 0 <= lt < 16]
                for h in range(4):
                    pacc, pak = nacc()
                    for ki, (kt, mr) in enumerate(kts):
                        items.append(dict(tok0=tok0, ntok=ntok, h=h, kt=kt, mr=mr, first=(ki == 0), last=(ki == len(kts) - 1), pacc=pacc, pak=pak))

            def S_c(it):
                h = it["h"]
                hb = (h % 2) * 64
                ps, pk = nb()
                it["ps"], it["pk"] = ps, pk
                mm(ps[:, 0:it["ntok"]], KT[:, h // 2, it["kt"] * 128:(it["kt"] + 1) * 128],
                   QT[:, h, it["tok0"]:it["tok0"] + it["ntok"]], True, True, ["KTc", "QTc"], [pk])

            pti_c = [0]

            def EV_c(it):
                h, ntok, tok0 = it["h"], it["ntok"], it["tok0"]
                hb = (h % 2) * 64
                pp = h // 2
                kv = h // 2
                num = slice(hb, hb + 64)
                den = slice(64 - hb, 128 - hb)
                pacc, pak = it["pacc"], it["pak"]
                pt = PT[pti_c[0] % 4]
                ptk = "PT%d" % (pti_c[0] % 4)
                pti_c[0] += 1
                act(pt[:, 0:ntok], it["ps"][:, 0:ntok], AF.Exp, [it["pk"]], [ptk], scale=0.125)
                if it["mr"] is not None:
                    tt("dve", pt[:, 0:ntok], pt[:, 0:ntok], WMASK[:, it["mr"], 0:ntok], OP.mult, [ptk, "WMASK"], [ptk])
                va = VA[:, it["kt"], kv, 64:192] if hb == 0 else VA[:, it["kt"], kv, 0:128]
                mm(pacc[:, 0:ntok], va, pt[:, 0:ntok], it["first"], it["last"], ["VA", ptk], [pak])
                if it["last"]:
                    ts("dve", REC[den, 0:ntok], pacc[den, 0:ntok], ESINK[den, l * 4 + h:l * 4 + h + 1], None, OP.add, None, [pak, "ESINK"], ["REC"])
                    recip(REC[den, 0:ntok], REC[den, 0:ntok], ["REC"], ["REC"])
                    tt("dve", MIX[num, pp, tok0:tok0 + ntok], pacc[num, 0:ntok], REC[den, 0:ntok], OP.mult, [pak, "REC"], ["MIX"])

            LA = 4
            for i_ in range(len(items) + LA):
                if i_ < len(items):
                    S_c(items[i_])
                if i_ >= LA:
                    EV_c(items[i_ - LA])
        P.barrier()
        epilogue(2)

        if KSTOP < 6:
            break
        with contextlib.ExitStack() as ws:
            QKT = sb("QKTd", [128, 4, T], BF16, ws)
            ws2 = contextlib.ExitStack()
            WD = sb("WD", [128, 8, 512], BF16, ws2)
            for i, sg in enumerate(["Dq0", "Dq1", "Dk0", "Dk1"]):
                load_w(l, sg, WD[:, :, i * 128:(i + 1) * 128], "WD")
            for (tok0, ntok, isctx) in BLOCKS:
                for ci, sg in enumerate(["Dq0", "Dq1", "Dk0", "Dk1"]):
                    ps, pk = nb()
                    proj_fm(ps, pk, WD[:, :, ci * 128:(ci + 1) * 128], "WD", tok0, ntok)
                    if ci % 2 == 0:
                        act(QKT[:, ci, tok0:tok0 + ntok], ps[:, 0:ntok], AF.Identity, [pk, "B2T"], ["QKTd"], bias=bias_col(sg))
                    else:
                        ts("dve", QKT[:, ci, tok0:tok0 + ntok], ps[:, 0:ntok], bias_col(sg), None, OP.add, None, [pk, "B2T"], ["QKTd"])
            P.barrier()
            ws2.close()
            WT = sb("WTd", [128, 8, 528], BF16, ws)
            VA2 = [sb("VAd%d" % i, [128, 4, 192], BF16, ws) for i in range(2)]
            NEGB = sb("NEGBd", [128, 2, 128], BF16, ws)
            cp("dve", NEGB[:, 0, :], C("negf"), ["CST"], ["NEGB"])
            cp("dve", NEGB[:, 1, :], C("negb"), ["CST"], ["NEGB"])
            LBH = sb("LBHd", [128, 4, 128], F32, ws)
            DT = sb("DTd", [128, 4, 128], F32, ws)
            EROW = sb("EROWd", [128, 2, 128], F32, ws)
            QS2 = [sb("QSd%d" % i, [128, 2, 128], BF16, ws) for i in range(2)]
            SM2 = [sb("SMd%d" % i, [128, 4, 128], BF16, ws) for i in range(2)]
            KW2 = [sb("KWd%d" % i, [128, 4, 64], BF16, ws) for i in range(2)]
            CN32 = sb("CN32d", [128, 4, 128], F32, ws)
            CNB = sb("CNBd", [128, 4, 128], BF16, ws)
            DN = sb("DNd", [128, 4, 128], F32, ws)
            HF = sb("HFd", [128, 2, T], BF16, ws)
            HT2 = [sb("HTd%d" % i, [128, 256], F32, ws) for i in range(2)]
            SQ = sb("SQd", [128, 256], F32, ws)
            RS = sb("RSd", [128, 256], F32, ws)
            load_w(l, "Dkt", WT[:, :, 0:256], "WT")
            load_w(l, "Dvt", WT[:, :, 256:512], "WT")
            load_w(l, "Dg", WT[:, :, 512:528], "WT")
            s0k = TMOFF["Dkt"]
            for i_ in range(2):
                memset("pool", VA2[i_][:].rearrange("p a b -> p (a b)"), 1.0, ["VAd%d" % i_])
            LOGFA = sb("LOGFAd", [128, NT, 8], F32, ws)
            IGA = sb("IGAd", [128, NT, 8], F32, ws)
            BIASA = sb("BIASAd", [128, 2, NT * 4], F32, ws)
            WWA = sb("WWAd", [128, 2, NT * 4], F32, ws)
            GLA = sb("GLAd", [128, 2, NT * 4], F32, ws)
            pga, pgak = bk(1)
            for t in range(NT):
                proj_tm(pga[:, t * 16:(t + 1) * 16], pgak, WT, "WT", 512, 16, t, s0k)
            pga3 = pga[:, 0:NT * 16].rearrange("p (t g) -> p t g", g=16)
            act(LOGFA[:], pga3[:, :, 8:16], AF.Exp, [pgak], ["LOGFA"], scale=-1.0)
            act(LOGFA[:].rearrange("p t g -> p (t g)"), LOGFA[:].rearrange("p t g -> p (t g)"), AF.Ln, ["LOGFA", "CC"], ["LOGFA"], bias=ONE_AP, scale=1.0)
            ts("dve", LOGFA[:].rearrange("p t g -> p (t g)"), LOGFA[:].rearrange("p t g -> p (t g)"), -1.0, None, OP.mult, None, ["LOGFA"], ["LOGFA"])
            cp("dve", IGA[:], pga3[:, :, 0:8], [pgak], ["IGA"])
            for dr_ in range(2):
                tri_ = C("tri128f") if dr_ == 0 else C("tri128b")
                rem_ = C("rem128f") if dr_ == 0 else C("rem128b")
                pca, pcak = bk(2 + dr_)
                rhs_ = LOGFA[:, :, dr_ * 4:(dr_ + 1) * 4]
                mm(pca[:, 0:NT * 4].rearrange("p (t h) -> p t h", h=4), tri_, rhs_, True, True, ["CST", "LOGFA"], [pcak])
                mm(pca[:, NT * 4:2 * NT * 4].rearrange("p (t h) -> p t h", h=4), rem_, rhs_, True, True, ["CST", "LOGFA"], [pcak])
                mm(pca[:, 2 * NT * 4:3 * NT * 4].rearrange("p (t h) -> p t h", h=4), C("ones"), rhs_, True, True, ["CST", "LOGFA"], [pcak])
                tt("dve", BIASA[:, dr_, :].rearrange("p (t h) -> p t h", h=4), IGA[:, :, dr_ * 4:(dr_ + 1) * 4],
                   pca[:, 0:NT * 4].rearrange("p (t h) -> p t h", h=4), OP.subtract, ["IGA", pcak], ["BIASA"])
                tt("dve", WWA[:, dr_, :].rearrange("p (t h) -> p t h", h=4), IGA[:, :, dr_ * 4:(dr_ + 1) * 4],
                   pca[:, NT * 4:2 * NT * 4].rearrange("p (t h) -> p t h", h=4), OP.add, ["IGA", pcak], ["WWA"])
                act(WWA[:, dr_, :], WWA[:, dr_, :], AF.Exp, ["WWA"], ["WWA"])
                ts("dve", WWA[:, dr_, :], WWA[:, dr_, :], 0.125, None, OP.mult, None, ["WWA"], ["WWA"])
                act(GLA[:, dr_, :], pca[:, 2 * NT * 4:3 * NT * 4], AF.Exp, [pcak], ["GLA"])
            for dr in range(2):
                tri = C("tri128f") if dr == 0 else C("tri128b")
                rem = C("rem128f") if dr == 0 else C("rem128b")
                neg = C("negf") if dr == 0 else C("negb")
                order = list(range(NT)) if dr == 0 else [1, 0] + list(range(NT - 1, 1, -1))
                memset("pool", CN32[:].rearrange("p a b -> p (a b)"), 0.0, ["CN32"])
                memset("pool", CNB[:].rearrange("p a b -> p (a b)"), 0.0, ["CNB"])
                def producer(t, bp):
                    tsl = slice(t * 128, (t + 1) * 128)
                    VAp, KWp, QSp, SMp = VA2[bp], KW2[bp], QS2[bp], SM2[bp]
                    pk_, pkk = bk(0)
                    proj_tm(pk_[:, 0:512], pkk, WT, "WT", 0, 512, t, s0k)
                    cp("dve", VAp[:, :, 64:128], pk_[:, 256:512].rearrange("p (h d) -> p h d", h=4), [pkk], ["VAd%d" % bp])
                    tt("dve", KWp[:], pk_[:, 0:256].rearrange("p (h d) -> p h d", h=4),
                       WWA[:, dr, t * 4:(t + 1) * 4].rearrange("p (h o) -> p h o", o=1).to_broadcast([128, 4, 64]), OP.mult,
                       [pkk, "WWA"], ["KW%d_%d" % (h_, bp) for h_ in range(4)])
                    pf, pfk = bk(3)
                    pe2, pe2k = bk(4)
                    cp("dve", LBH[:], LOGFA[:, t, dr * 4:dr * 4 + 4].rearrange("p (h o) -> p h o", o=1).to_broadcast([128, 4, 128]),
                       ["LOGFA"], ["LBH%d" % h_ for h_ in range(4)])
                    for h in range(4):
                        hb = (h % 2) * 64
                        pp = h // 2
                        mm(pf[:, h * 128:(h + 1) * 128], LBH[:, h, :], tri, True, False, ["LBH%d" % h, "CST"], [pfk])
                        mm(pf[:, h * 128:(h + 1) * 128], IDB[:], NEGB[:, dr, :], False, True, ["IDB", "NEGB"], [pfk])
                        mm(pe2[hb:hb + 64, pp * 128:(pp + 1) * 128], LBH[:, h, 0:64], tri, True, True, ["LBH%d" % h, "CST"], [pe2k], tp=(0, hb))
                    for h in range(4):
                        act(DT[:, h, :], pf[:, h * 128:(h + 1) * 128], AF.Exp, [pfk, "BIASA"], ["DT"], bias=BIASA[:, dr, t * 4 + h:t * 4 + h + 1], scale=1.0)
                    act(EROW[:].rearrange("p a b -> p (a b)"), pe2[:, 0:256], AF.Exp, [pe2k], ["EROW"])
                    tt("dve", QSp[:], QKT[:, 0:2, tsl], EROW[:], OP.mult, ["QKTd", "EROW"], ["QS%d" % bp])
                    pkq0, pkqk0 = bk(3)
                    pkq1, pkqk1 = bk(4)
                    pkqs = [pkq0, pkq1]
                    pkqks = [pkqk0, pkqk1]
                    for h in range(4):
                        hb = (h % 2) * 64
                        pp = h // 2
                        mm(pkqs[h % 2][:, pp * 128:(pp + 1) * 128], QKT[hb:hb + 64, 2 + pp, tsl], QKT[hb:hb + 64, pp, tsl], True, True, ["QKTd"], [pkqks[h % 2]])
                    for par in range(2):
                        stt(SMp[:].rearrange("p (a two) b -> p two a b", two=2)[:, par, :, :], pkqs[par][:, 0:256].rearrange("p (a b) -> p a b", a=2), 0.125,
                            DT[:].rearrange("p (a two) b -> p two a b", two=2)[:, par, :, :], OP.mult, OP.mult, [pkqks[par], "DT"], ["SM%d" % bp])

                def gn_d(t, bp):
                    tsl = slice(t * 128, (t + 1) * 128)
                    if dr == 1 and (need_ctx or t >= 2):
                        group_norm_store(HT2[bp][:, 0:256], "HTd_%d" % bp, GCOL[:, l * 3 + 2:l * 3 + 3], 256,
                                         MIX[:, :, tsl], (SQ, RS), "d", bank=2, split=2)

                def consumer(t, bp):
                    HT_ = HT2[bp]
                    tsl = slice(t * 128, (t + 1) * 128)
                    VAp, KWp, QSp, SMp = VA2[bp], KW2[bp], QS2[bp], SM2[bp]
                    pn, pnk = bk(5 if bp == 0 else 1)
                    for h in range(4):
                        hb = (h % 2) * 64
                        pp = h // 2
                        va = VAp[:, h, 64:192] if hb == 0 else VAp[:, h, 0:128]
                        mm(pn[:, h * 128:(h + 1) * 128], va, SMp[:, h, :], True, False, ["VAd%d" % bp, "SM%d" % bp], [pnk])
                        mm(pn[:, h * 128:(h + 1) * 128], CNB[hb:hb + 64, h, :], QSp[hb:hb + 64, pp, :], False, True, ["CNB", "QS%d" % bp], [pnk])
                    pst, pstk = bk(6)
                    for h in range(4):
                        hb = (h % 2) * 64
                        va = VAp[:, h, 64:192] if hb == 0 else VAp[:, h, 0:128]
                        mm(pst[hb:hb + 64, h * 128:(h + 1) * 128], KWp[:, h, :], va, True, True, ["KW%d_%d" % (h, bp), "VAd%d" % bp], [pstk], tp=(0, hb))
                    for h in range(4):
                        hb = (h % 2) * 64
                        sl = slice(hb, hb + 64)
                        stt(CN32[sl, h, :], CN32[sl, h, :], GLA[sl, dr, t * 4 + h:t * 4 + h + 1], pst[sl, h * 128:(h + 1) * 128], OP.mult, OP.add, ["CN32", "GLA", pstk], ["CN32"])
                    act(CNB[:].rearrange("p a b -> p (a b)"), CN32[:].rearrange("p a b -> p (a b)"), AF.Copy, ["CN32"], ["CNB"])

                def fin(t, bp):
                    HT_ = HT2[bp]
                    tsl = slice(t * 128, (t + 1) * 128)
                    pn, pnk = bk(5 if bp == 0 else 1)
                    pn3 = pn[:, :].rearrange("p (a two t) -> p two a t", two=2, t=128)
                    DNv = DN[:].rearrange("p (two a) t -> p two a t", two=2)
                    for par in range(2):
                        hb = par * 64
                        num = slice(hb, hb + 64)
                        den = slice(64 - hb, 128 - hb)
                        act(DNv[den, par, :, :], pn3[den, par, :, :], AF.Abs, [pnk], ["DN%d" % par])
                        ts("dve", DNv[den, par, :, :], DNv[den, par, :, :], 1.0, None, OP.max, None, ["DN%d" % par], ["DN%d" % par])
                        recip(DNv[den, par, :, :], DNv[den, par, :, :], ["DN%d" % par], ["DN%d" % par])
                        dst = HF[num, :, tsl] if dr == 0 else HT_[num, :].rearrange("p (a t) -> p a t", a=2)
                        tt("dve", dst, pn3[num, par, :, :], DNv[den, par, :, :], OP.mult, [pnk, "DN%d" % par], ["HF"] if dr == 0 else ["HTd_%d" % bp])
                    if dr == 1:
                        for pp in range(2):
                            tt("dve", HT_[:, pp * 128:(pp + 1) * 128], HT_[:, pp * 128:(pp + 1) * 128], HF[:, pp, tsl], OP.add,
                               ["HTd_%d" % bp, "HF"], ["HTd_%d" % bp])

                producer(order[0], 0)
                producer(order[1], 1)
                for i_ in range(len(order)):
                    consumer(order[i_], i_ % 2)
                    if i_ + 2 < len(order):
                        producer(order[i_ + 2], i_ % 2)
                    fin(order[i_], i_ % 2)
                    if i_ >= 1:
                        gn_d(order[i_ - 1], (i_ - 1) % 2)
                gn_d(order[-1], (len(order) - 1) % 2)
        P.barrier()
        epilogue(3, extra_sig=True)
        if dbg:
            for t in range(NT):
                dma(dbg_x[l, t * 128:(t + 1) * 128, :], X[:, t, :], r=["X%d" % t], key="dbg")

    with contextlib.ExitStack() as fs:
        FG = sb("FG", [128, D], F32, fs)
        YO = [sb("YO%d" % i, [128, D], F32, fs) for i in range(2)]
        JUNK = sb("JUNKf", [128, D], BF16, fs)
        dma(FG[:], fing_d, w=["FG"])
        for t in range(2, NT):
            yo = YO[t % 2]
            yk = "YO%d" % (t % 2)
            act(JUNK[:], X[:, t, :], AF.Square, ["X%d" % t], ["JUNKf", "SS"], accum=SS[:, 0:1])
            act(SS[:, 1:2], SS[:, 0:1], AF.Ln, ["SS", "CC"], ["SS1"], bias=EPS_AP, scale=1.0 / D)
            act(SS[:, 2:3], SS[:, 1:2], AF.Exp, ["SS1"], ["SS2"], scale=-0.5)
            stt(yo[:], X[:, t, :], SS[:, 2:3], FG[:], OP.mult, OP.mult, ["X%d" % t, "SS2", "FG"], [yk])
            dma(out_d[(t - 2) * 128:(t - 1) * 128, :], yo[:], r=[yk], key="out%d" % (t % 2))
    P.emit(nc)
    st.close()
    return nc, P


_CACHE = {}


def _prep_shared(inputs):
    f = lambda a: np.ascontiguousarray(np.asarray(a, dtype=np.float32))
    w_in = f(inputs["w_in"]); b_in = f(inputs["b_in"])
    sh = {}
    sh["w_mod"] = f(inputs["w_mod"])
    b_mod = f(inputs["b_mod"])
    sh["bmodT"] = np.ascontiguousarray(b_mod.reshape(NL, 24, 128).transpose(2, 0, 1).reshape(128, NL * 24))
    sh["bgate"] = np.ascontiguousarray(np.broadcast_to(b_mod[:, None, 2048:3072], (NL, 128, D)))
    sh["normgT"] = np.ascontiguousarray(f(inputs["norm_g"]).reshape(NL, 8, 128).transpose(2, 0, 1).reshape(128, NL * 8))
    w2p = w_in[:, :, PERM].reshape(NL, 8, 128, NW // 128, 128)
    sh["w2"] = np.ascontiguousarray(w2p.transpose(0, 3, 2, 1, 4)).reshape(NL, NW // 128, 128, 8 * 128)
    b2 = b_in[:, PERM]
    nch = NW // 128
    b2T = np.zeros((128, NL, 64), np.float32)
    b2T[:, :, :nch] = b2.reshape(NL, nch, 128).transpose(2, 0, 1)
    sh["b2T"] = b2T.reshape(128, NL * 64)
    b2row = np.zeros((NL, 1, NTM), np.float32)
    for n_ in TMSEGS:
        s0, n = SEGS[n_]
        b2row[:, 0, TMOFF[n_]:TMOFF[n_] + n] = b2[:, s0:s0 + n]
    sh["b2row"] = b2row
    sh["w_out"] = f(inputs["w_out"])
    sh["fing"] = np.ascontiguousarray(np.broadcast_to(f(inputs["final_g"])[None, :], (128, D)))
    sh["lam"] = np.ascontiguousarray(np.broadcast_to(f(inputs["diff_lam"]).reshape(1, NL * 128), (128, NL * 128)))
    gcol = np.zeros((128, NL, 3), np.float32)
    for i, k in enumerate(["diff_g", "hg_g", "ml_g"]):
        g = f(inputs[k])
        gcol[:, :, i] = g[:, np.arange(128) % 64].T
    sh["gcol"] = gcol.reshape(128, NL * 3)
    sh["hglb"] = np.ascontiguousarray(np.broadcast_to(f(inputs["hg_lb"]).reshape(1, 512), (128, 512)))
    sh["sink"] = np.ascontiguousarray(np.broadcast_to(f(inputs["sw_sink"]).reshape(1, NL * 4), (128, NL * 4)))
    tabs = rope_tables()
    for k in ("Ac", "As", "Cc", "Cs"):
        sh["tab" + k] = tabs[k]
    ca = const_arrays()
    sh["cst"] = np.ascontiguousarray(np.stack([ca[n] for n in CONST_F32], 1).reshape(128, -1))
    hm = np.stack([(np.arange(128) // 64 == 0), (np.arange(128) // 64 == 1)], 1).astype(np.float32)
    sh["small"] = np.ascontiguousarray(np.concatenate([ca["segind"], ca["rowmask"], hm], 1))
    sh["wmask"] = ca["wmask"]
    return sh


def kernel(x, c, ctx, c_ctx, w_mod, b_mod, norm_g, w_in, b_in, diff_lam, diff_g, hg_lb, hg_g, sw_sink, ml_g,
           w_out, final_g, _dbg=False):
    inputs = dict(x=x, c=c, ctx=ctx, c_ctx=c_ctx, w_mod=w_mod, b_mod=b_mod, norm_g=norm_g, w_in=w_in, b_in=b_in,
                  diff_lam=diff_lam, diff_g=diff_g, hg_lb=hg_lb, hg_g=hg_g, sw_sink=sw_sink, ml_g=ml_g,
                  w_out=w_out, final_g=final_g)
    sh = _prep_shared(inputs)
    x = np.asarray(x, np.float32); ctx = np.asarray(ctx, np.float32)
    c = np.asarray(c, np.float32); c_ctx = np.asarray(c_ctx, np.float32)
    key = "dbg" if _dbg else "main"
    if key not in _CACHE:
        _CACHE[key] = build_program(dbg=_dbg)[0]
    nc = _CACHE[key]
    in_maps = []
    for b in range(8):
        m = dict(sh)
        m["xin"] = np.ascontiguousarray(np.concatenate([ctx[b], x[b]], 0))
        cT = np.concatenate([c[b].reshape(8, 128).T, c_ctx.reshape(8, 128).T], 1)
        m["cT"] = np.ascontiguousarray(cT)
        in_maps.append(m)
    res = run_bass_kernel_spmd(nc, in_maps, core_ids=list(range(8)))
    out = np.stack([np.asarray(r["out"], np.float32) for r in res.results], 0)
    if _dbg:
        return out, res.results
    return out
```
